# Optimizing a Trainium2 kernel written in Bass

```python
import math
import jax, jax.numpy as jnp
from jax import lax
import numpy as np

D_MODEL = 1024
BATCH = 4
SEQ = 8192
DEPTH = 2

D_MIX = D_MODEL
GROUP_W = D_MIX // 4
CONV_K = 31
NSA_HEADS = 4
NSA_HD = GROUP_W // NSA_HEADS
CMP_LEN = 32
CMP_STRIDE = 16
CMP_HIDDEN = 2 * NSA_HD
SLC_LEN = 64
N_SELECT = 16
WINDOW = 512
MLA_HEADS = 4
MLA_NOPE = 64
MLA_ROPE = 32
MLA_V = GROUP_W // MLA_HEADS
Q_LORA = 256
KV_LORA = 128
ROPE_THETA = 10000.0
POOL_WINDOWS = (2, 4, 8, 16)
POOL_GW = GROUP_W // len(POOL_WINDOWS)
D_FF = 4 * D_MODEL
N_BUCKETS = 32
MAX_DIST = 128
Q_BLOCK = 128
EPS = 1e-6
NEG_INF = -1e30
FORCE_SCORE = 1e9
IN_SPLITS = (2 * GROUP_W, NSA_HEADS * NSA_HD, 6 * NSA_HD, 3 * NSA_HEADS, Q_LORA, KV_LORA, MLA_ROPE, GROUP_W)
D_IN = sum(IN_SPLITS)
SPLIT_AT = tuple(sum(IN_SPLITS[:i + 1]) for i in range(len(IN_SPLITS) - 1))

kernel_name = 'hybrid_parallel_mixers'


def rmsnorm(x, g):
    xf = x.astype(jnp.float32)
    y = xf * lax.rsqrt(jnp.mean(xf * xf, axis=-1, keepdims=True) + EPS)
    return (y * g.astype(jnp.float32)).astype(x.dtype)


def layernorm(x, g, b):
    xf = x.astype(jnp.float32)
    mu = jnp.mean(xf, axis=-1, keepdims=True)
    var = jnp.mean(jnp.square(xf - mu), axis=-1, keepdims=True)
    y = (xf - mu) * lax.rsqrt(var + EPS) * g.astype(jnp.float32) + b.astype(jnp.float32)
    return y.astype(x.dtype)


def masked_softmax(logits, mask):
    lf = jnp.where(mask, logits.astype(jnp.float32), NEG_INF)
    p = jax.nn.softmax(lf, axis=-1)
    return jnp.where(mask, p, 0.0)


def rel_bucket(dist):
    n = jnp.maximum(dist, 0)
    max_exact = N_BUCKETS // 2
    nf = jnp.maximum(n, 1).astype(jnp.float32)
    large = max_exact + (jnp.log(nf / max_exact) / math.log(MAX_DIST / max_exact)
                         * (N_BUCKETS - max_exact)).astype(jnp.int32)
    large = jnp.minimum(large, N_BUCKETS - 1)
    return jnp.where(n < max_exact, n, large)


def rel_bias(dist, table):
    return table[rel_bucket(dist)]


def apply_rope(x, cos, sin):
    x1, x2 = jnp.split(x, 2, axis=-1)
    return jnp.concatenate([x1 * cos - x2 * sin, x1 * sin + x2 * cos], axis=-1).astype(x.dtype)


def conv_module(u, dw, dw_b, ln_g, ln_b, pw):
    a, g = jnp.split(u, 2, axis=-1)
    h = a * jax.nn.sigmoid(g)
    h = lax.conv_general_dilated(h, dw[:, None, :], (1,), [(CONV_K - 1, 0)],
                                 dimension_numbers=('NWC', 'WIO', 'NWC'),
                                 feature_group_count=GROUP_W) + dw_b
    h = jax.nn.silu(layernorm(h, ln_g, ln_b))
    return h @ pw


def pool_mixer(u, w_grp, scale):
    B_, S_, _ = u.shape
    uf = u.astype(jnp.float32)
    c = jnp.pad(jnp.cumsum(uf, axis=1), ((0, 0), (1, 0), (0, 0)))
    t = jnp.arange(S_)
    outs = []
    for gi, w in enumerate(POOL_WINDOWS):
        cg = c[..., gi * POOL_GW:(gi + 1) * POOL_GW]
        lo = jnp.maximum(t + 1 - w, 0)
        cnt = (t + 1 - lo).astype(jnp.float32)
        mean = (cg[:, 1:] - jnp.take(cg, lo, axis=1)) / cnt[None, :, None]
        outs.append(mean - uf[..., gi * POOL_GW:(gi + 1) * POOL_GW])
    d = jnp.stack(outs, axis=2).astype(u.dtype)
    y = jnp.einsum('bsgc,gcd->bsgd', d, w_grp).reshape(B_, S_, GROUP_W)
    return y * scale


def nsa_mixer(q, kv, gates, cmp_pos, cmp_w1, cmp_w2, rel_table):
    B_, S_ = q.shape[0], q.shape[1]
    k_c, v_c, k_s, v_s, k_w, v_w = jnp.split(kv, 6, axis=-1)
    n_cmp = (S_ - CMP_LEN) // CMP_STRIDE + 1
    blk_idx = jnp.arange(n_cmp)[:, None] * CMP_STRIDE + jnp.arange(CMP_LEN)[None, :]

    def compress(k, i):
        kb = k[:, blk_idx] + cmp_pos[i]
        hdn = jax.nn.silu(kb.reshape(B_, n_cmp, CMP_LEN * NSA_HD) @ cmp_w1[i])
        return hdn @ cmp_w2[i]

    Kc = compress(k_c, 0)
    Vc = compress(v_c, 1)
    cmp_end = jnp.arange(n_cmp) * CMP_STRIDE + CMP_LEN - 1
    n_slc = S_ // SLC_LEN
    c_lo = jnp.arange(n_cmp)[:, None] * CMP_STRIDE
    s_lo = jnp.arange(n_slc)[None, :] * SLC_LEN
    overlap = jnp.clip(jnp.minimum(c_lo + CMP_LEN, s_lo + SLC_LEN) - jnp.maximum(c_lo, s_lo),
                       0, None).astype(jnp.float32) / CMP_LEN
    n_top = min(N_SELECT, n_slc)
    k_w_pad = jnp.pad(k_w, ((0, 0), (WINDOW, 0), (0, 0)))
    v_w_pad = jnp.pad(v_w, ((0, 0), (WINDOW, 0), (0, 0)))
    n_qb = S_ // Q_BLOCK
    q_blocks = (q * NSA_HD ** -0.5).reshape(B_, n_qb, Q_BLOCK, NSA_HEADS, NSA_HD).transpose(1, 0, 3, 2, 4)
    g_blocks = gates.reshape(B_, n_qb, Q_BLOCK, NSA_HEADS, 3).transpose(1, 0, 3, 2, 4)
    j_slc = jnp.arange(n_slc)
    off = jnp.arange(SLC_LEN)
    band = jnp.arange(WINDOW + Q_BLOCK)
    gather_rows = jax.vmap(lambda a, i: a[i])

    def block(args):
        qb, qblk, gblk = args
        t = qb * Q_BLOCK + jnp.arange(Q_BLOCK)
        lc = jnp.einsum('bhqd,bcd->bhqc', qblk, Kc) + jnp.moveaxis(
            rel_bias(t[:, None] - cmp_end[None, :], rel_table), -1, 0)
        pc = masked_softmax(lc, cmp_end[None, :] <= t[:, None])
        o_c = jnp.einsum('bhqc,bcd->bhqd', pc.astype(Vc.dtype), Vc)
        imp = jnp.einsum('bhqc,cj->bqj', pc, overlap)
        jc = (t // SLC_LEN)[:, None]
        forced = (j_slc[None, :] == 0) | (j_slc[None, :] == jc) | (j_slc[None, :] == jc - 1)
        imp = jnp.where(forced, FORCE_SCORE, imp)
        imp = jnp.where(j_slc[None, :] <= jc, imp, NEG_INF)
        top_v, top_i = lax.top_k(imp, n_top)
        blk_ok = top_v > 0.5 * NEG_INF
        pos = (top_i[..., None] * SLC_LEN + off).reshape(B_, Q_BLOCK, n_top * SLC_LEN)
        pos_flat = pos.reshape(B_, Q_BLOCK * n_top * SLC_LEN)
        Ks = gather_rows(k_s, pos_flat).reshape(B_, Q_BLOCK, n_top * SLC_LEN, NSA_HD)
        Vs = gather_rows(v_s, pos_flat).reshape(B_, Q_BLOCK, n_top * SLC_LEN, NSA_HD)
        ms = (pos <= t[None, :, None]) & jnp.repeat(blk_ok, SLC_LEN, axis=-1)
        ls = jnp.einsum('bhqd,bqkd->bhqk', qblk, Ks) + jnp.moveaxis(
            rel_bias(t[None, :, None] - pos, rel_table), -1, 1)
        ps = masked_softmax(ls, ms[:, None])
        o_s = jnp.einsum('bhqk,bqkd->bhqd', ps.astype(Vs.dtype), Vs)
        start = qb * Q_BLOCK
        Kw = lax.dynamic_slice_in_dim(k_w_pad, start, WINDOW + Q_BLOCK, axis=1)
        Vw = lax.dynamic_slice_in_dim(v_w_pad, start, WINDOW + Q_BLOCK, axis=1)
        s_pos = start - WINDOW + band
        dist = t[:, None] - s_pos[None, :]
        mw = (dist >= 0) & (dist < WINDOW) & (s_pos[None, :] >= 0)
        lw = jnp.einsum('bhqd,bkd->bhqk', qblk, Kw) + jnp.moveaxis(rel_bias(dist, rel_table), -1, 0)
        pw = masked_softmax(lw, mw)
        o_w = jnp.einsum('bhqk,bkd->bhqd', pw.astype(Vw.dtype), Vw)
        return gblk[..., 0:1] * o_c + gblk[..., 1:2] * o_s + gblk[..., 2:3] * o_w

    out = lax.map(block, (jnp.arange(n_qb), q_blocks, g_blocks))
    return out.transpose(1, 0, 3, 2, 4).reshape(B_, S_, NSA_HEADS * NSA_HD)


def mla_mixer(c_q, c_kv, k_r, q_norm_g, w_uq, kv_norm_g, w_ukv, cos, sin):
    B_, S_ = c_q.shape[0], c_q.shape[1]
    q = (rmsnorm(c_q, q_norm_g) @ w_uq).reshape(B_, S_, MLA_HEADS, MLA_NOPE + MLA_ROPE)
    q_nope, q_rope = jnp.split(q, [MLA_NOPE], axis=-1)
    q_rope = apply_rope(q_rope, cos[:, None, :], sin[:, None, :])
    kv = (rmsnorm(c_kv, kv_norm_g) @ w_ukv).reshape(B_, S_, MLA_HEADS, MLA_NOPE + MLA_V)
    k_nope, v = jnp.split(kv, [MLA_NOPE], axis=-1)
    k_rope = apply_rope(k_r, cos, sin)
    scale = (MLA_NOPE + MLA_ROPE) ** -0.5
    n_qb = S_ // Q_BLOCK
    qn = (q_nope * scale).reshape(B_, n_qb, Q_BLOCK, MLA_HEADS, MLA_NOPE).transpose(1, 0, 2, 3, 4)
    qr = (q_rope * scale).reshape(B_, n_qb, Q_BLOCK, MLA_HEADS, MLA_ROPE).transpose(1, 0, 2, 3, 4)
    kpos = jnp.arange(S_)

    def block(args):
        qb, qn_b, qr_b = args
        t = qb * Q_BLOCK + jnp.arange(Q_BLOCK)
        lg = (jnp.einsum('bqhd,bkhd->bhqk', qn_b, k_nope)
              + jnp.einsum('bqhr,bkr->bhqk', qr_b, k_rope))
        p = masked_softmax(lg, kpos[None, :] <= t[:, None])
        return jnp.einsum('bhqk,bkhd->bqhd', p.astype(v.dtype), v)

    out = lax.map(block, (jnp.arange(n_qb), qn, qr))
    return out.transpose(1, 0, 2, 3, 4).reshape(B_, S_, MLA_HEADS * MLA_V)


def setup_inputs(seed: int = 0) -> dict:
    key = jax.random.key(seed)
    ks = jax.random.split(key, 23)
    f32 = jnp.float32

    def nrm(k, shape, scale):
        return jax.random.normal(k, shape, f32) * scale

    def gain(k, shape):
        return 1.0 + 0.02 * jax.random.normal(k, shape, f32)

    return {
        'x': nrm(ks[0], (BATCH, SEQ, D_MODEL), 1.0),
        'w_in': nrm(ks[1], (DEPTH, D_MODEL, D_IN), D_MODEL ** -0.5),
        'w_out': nrm(ks[2], (DEPTH, D_MIX, D_MODEL), D_MIX ** -0.5),
        'ln_mix_g': gain(ks[3], (DEPTH, D_MODEL)),
        'ln_mlp_g': gain(ks[4], (DEPTH, D_MODEL)),
        'conv_dw': nrm(ks[5], (DEPTH, CONV_K, GROUP_W), CONV_K ** -0.5),
        'conv_dw_b': nrm(ks[6], (DEPTH, GROUP_W), 0.02),
        'conv_ln_g': gain(ks[7], (DEPTH, GROUP_W)),
        'conv_ln_b': nrm(ks[8], (DEPTH, GROUP_W), 0.02),
        'conv_pw': nrm(ks[9], (DEPTH, GROUP_W, GROUP_W), GROUP_W ** -0.5),
        'nsa_cmp_pos': nrm(ks[10], (DEPTH, 2, CMP_LEN, NSA_HD), 0.1),
        'nsa_cmp_w1': nrm(ks[11], (DEPTH, 2, CMP_LEN * NSA_HD, CMP_HIDDEN), (CMP_LEN * NSA_HD) ** -0.5),
        'nsa_cmp_w2': nrm(ks[12], (DEPTH, 2, CMP_HIDDEN, NSA_HD), CMP_HIDDEN ** -0.5),
        'mla_q_norm_g': gain(ks[13], (DEPTH, Q_LORA)),
        'mla_w_uq': nrm(ks[14], (DEPTH, Q_LORA, MLA_HEADS * (MLA_NOPE + MLA_ROPE)), Q_LORA ** -0.5),
        'mla_kv_norm_g': gain(ks[15], (DEPTH, KV_LORA)),
        'mla_w_ukv': nrm(ks[16], (DEPTH, KV_LORA, MLA_HEADS * (MLA_NOPE + MLA_V)), KV_LORA ** -0.5),
        'pool_w': nrm(ks[17], (DEPTH, len(POOL_WINDOWS), POOL_GW, POOL_GW), POOL_GW ** -0.5),
        'pool_scale': gain(ks[18], (DEPTH, GROUP_W)),
        'mlp_w1': nrm(ks[19], (DEPTH, D_MODEL, D_FF), D_MODEL ** -0.5),
        'mlp_w2': nrm(ks[20], (DEPTH, D_FF, D_MODEL), D_FF ** -0.5),
        'rel_bias_table': nrm(ks[21], (N_BUCKETS, NSA_HEADS), 0.2),
        'final_norm_g': gain(ks[22], (D_MODEL,)),
    }


def reference(x, w_in, w_out, ln_mix_g, ln_mlp_g, conv_dw, conv_dw_b, conv_ln_g, conv_ln_b,
              conv_pw, nsa_cmp_pos, nsa_cmp_w1, nsa_cmp_w2, mla_q_norm_g, mla_w_uq,
              mla_kv_norm_g, mla_w_ukv, pool_w, pool_scale, mlp_w1, mlp_w2,
              rel_bias_table, final_norm_g):
    B_, S_ = x.shape[0], x.shape[1]
    pos = jnp.arange(S_, dtype=jnp.float32)
    inv_freq = ROPE_THETA ** (-jnp.arange(0, MLA_ROPE, 2, dtype=jnp.float32) / MLA_ROPE)
    ang = pos[:, None] * inv_freq[None, :]
    cos, sin = jnp.cos(ang), jnp.sin(ang)
    h = x
    for l in range(DEPTH):
        u = rmsnorm(h, ln_mix_g[l]) @ w_in[l]
        u_conv, u_q, u_kv, u_g, u_cq, u_ckv, u_kr, u_pool = jnp.split(u, SPLIT_AT, axis=-1)
        y_conv = conv_module(u_conv, conv_dw[l], conv_dw_b[l], conv_ln_g[l], conv_ln_b[l], conv_pw[l])
        y_nsa = nsa_mixer(u_q.reshape(B_, S_, NSA_HEADS, NSA_HD), u_kv,
                          jax.nn.sigmoid(u_g).reshape(B_, S_, NSA_HEADS, 3),
                          nsa_cmp_pos[l], nsa_cmp_w1[l], nsa_cmp_w2[l], rel_bias_table)
        y_mla = mla_mixer(u_cq, u_ckv, u_kr, mla_q_norm_g[l], mla_w_uq[l],
                          mla_kv_norm_g[l], mla_w_ukv[l], cos, sin)
        y_pool = pool_mixer(u_pool, pool_w[l], pool_scale[l])
        h = h + jnp.concatenate([y_conv, y_nsa, y_mla, y_pool], axis=-1) @ w_out[l]
        z = rmsnorm(h, ln_mlp_g[l]) @ mlp_w1[l]
        h = h + jnp.square(jax.nn.relu(z)) @ mlp_w2[l]
    return rmsnorm(h, final_norm_g)
```

```python
import math
import contextlib
import numpy as np
import concourse.bass as bass
import concourse.mybir as mybir
from concourse.bass_utils import run_bass_kernel_spmd

F32 = mybir.dt.float32
BF16 = mybir.dt.bfloat16
ALU = mybir.AluOpType
AF = mybir.ActivationFunctionType

ENGS = ['pe', 'act', 'dve', 'pool', 'sp']
EPOCH = 30000
_MARKS = []

D_MODEL = 1024
DEPTH = 2
GROUP_W = 256
CONV_K = 31
NSA_HD = 64
CMP_LEN = 32
CMP_STRIDE = 16
SLC_LEN = 64
N_SELECT = 16
WINDOW = 512
Q_LORA = 256
KV_LORA = 128
D_FF = 4096
EPS = 1e-6
MLA_SCALE = 96 ** -0.5
NSA_SCALE = 0.125
OFF = 2064
EVL = 4640

C_CONV = 0
C_Q = 512
C_KVC = 768
C_KS = 896
C_KW = 1024
C_GATE = 1152
C_CQ = 1164
C_CKV = 1420
C_KR = 1548
C_POOL = 1612
C_V = 1868
NCOL = 1996


class Prog:
    def __init__(self, nc):
        self.nc = nc
        self.ops = {e: [] for e in ENGS}
        self.cnt = {e: 0 for e in ENGS}
        self.esem = {}
        self.epoch_idx = {e: 0 for e in ENGS}
        self.sems = {}
        self.seen = {e: {} for e in ENGS}
        self.last_w = {}
        self.readers = {}
        self.dma_cnt = {}
        self.last_tok = {}
        self._cms = []
        for e in ENGS:
            self._new_epoch(e)

    def _alloc_sem(self, key):
        cm = self.nc.semaphore("s" + "_".join(str(k) for k in key))
        h = cm.__enter__()
        self._cms.append(cm)
        self.sems[key] = h
        return h

    def _new_epoch(self, e):
        key = ('E', e, self.epoch_idx[e])
        self.epoch_idx[e] += 1
        self._alloc_sem(key)
        self.esem[e] = key
        self.cnt[e] = 0

    def _need(self, eng, tok):
        if tok is None:
            return
        key, val = tok
        if eng == 'pe' and key[0] == 'E' and key[1] == 'pe':
            return
        if self.seen[eng].get(key, 0) >= val:
            return
        self.seen[eng][key] = val
        self.ops[eng].append(('wait', key, val))

    def _deps(self, eng, reads, writes):
        toks = []
        for r in reads:
            toks.append(self.last_w.get(r))
            if r.startswith('ps'):
                for t in self.readers.get(r, ()):
                    if t[0][0] == 'E' and t[0][1] != eng:
                        toks.append(t)
        for w in writes:
            toks.append(self.last_w.get(w))
            toks.extend(self.readers.get(w, ()))
        for t in toks:
            self._need(eng, t)

    def _commit(self, tok, reads, writes):
        for w in writes:
            self.last_w[w] = tok
            self.readers[w] = []
        for r in reads:
            if r in writes:
                continue
            self.readers.setdefault(r, []).append(tok)

    def op(self, eng, fn, reads=(), writes=()):
        self._deps(eng, reads, writes)
        if self.cnt[eng] >= EPOCH:
            self._new_epoch(eng)
        self.cnt[eng] += 1
        key = self.esem[eng]
        tok = (key, self.cnt[eng])
        self.ops[eng].append(('op', fn, key, 1))
        self.last_tok[eng] = tok
        self._commit(tok, reads, writes)
        return tok

    def dma(self, eng, fn, sem, reads=(), writes=()):
        key = ('D', sem)
        if key not in self.sems:
            self._alloc_sem(key)
            self.dma_cnt[key] = 0
        self._deps(eng, reads, writes)
        if self.dma_cnt[key] > 0:
            self._need(eng, (key, self.dma_cnt[key]))
        self.dma_cnt[key] += 16
        tok = (key, self.dma_cnt[key])
        self.ops[eng].append(('op', fn, key, 16))
        self._commit(tok, reads, writes)
        return tok

    def barrier(self):
        for e in ENGS:
            for f in ENGS:
                if f != e and f in self.last_tok:
                    self._need(e, self.last_tok[f])
            for key, v in self.dma_cnt.items():
                if v > 0:
                    self._need(e, (key, v))
        self.last_w = {}
        self.readers = {}

    def finish(self):
        nc = self.nc
        emap = {'pe': 'tensor', 'act': 'scalar', 'dve': 'vector', 'pool': 'gpsimd', 'sp': 'sync'}
        with nc.Block() as block:
            for e in ENGS:
                lst = self.ops[e]
                if not lst:
                    continue

                def body(engobj, lst=lst):
                    for it in lst:
                        if it[0] == 'wait':
                            engobj.wait_ge(self.sems[it[1]], it[2])
                        else:
                            ins = it[1](engobj)
                            ins.then_inc(self.sems[it[2]], it[3])
                getattr(block, emap[e])(body)
        for cm in reversed(self._cms):
            cm.__exit__(None, None, None)
        self._cms = []


def _rel_bucket_np(n):
    n = np.maximum(n, 0)
    nf = np.maximum(n, 1).astype(np.float32)
    val = (np.log(nf / np.float32(16)) / np.float32(math.log(128 / 16)) * np.float32(16)).astype(np.float32)
    large = 16 + val.astype(np.int32)
    large = np.minimum(large, 31)
    return np.where(n < 16, n, large)


def host_constants(S):
    c = {}
    c['ident'] = np.eye(128, dtype=np.float32)
    c['antiI'] = np.ascontiguousarray(np.eye(128, dtype=np.float32)[::-1])
    dist = np.arange(EVL) - OFF
    b = _rel_bucket_np(dist)
    oh_c = np.zeros((33, EVL), np.float32)
    oh_w = np.zeros((33, EVL), np.float32)
    for i in range(EVL):
        d = dist[i]
        if d < 0:
            oh_c[32, i] = 1
            oh_w[32, i] = 1
        else:
            oh_c[b[i], i] = 1
            if d >= WINDOW:
                oh_w[32, i] = 1
            else:
                oh_w[b[i], i] = 1
    c['oh_c'] = oh_c
    c['oh_w'] = oh_w
    pos = np.arange(S, dtype=np.float32)
    inv_freq = (np.float32(10000.0) ** (-np.arange(0, 32, 2, dtype=np.float32) / np.float32(32))).astype(np.float32)
    ang = (pos[:, None] * inv_freq[None, :]).astype(np.float32)
    cos = np.cos(ang).astype(np.float32).T
    sin = np.sin(ang).astype(np.float32).T
    cos2 = np.concatenate([cos, cos], 0)
    sin2 = np.concatenate([-sin, sin], 0)
    c['rope'] = np.stack([cos2 * np.float32(MLA_SCALE), sin2 * np.float32(MLA_SCALE), cos2, sin2], 0).astype(np.float32)
    n_slc = S // SLC_LEN
    ovl = np.zeros((512, 128), np.float32)
    n_cmp = S // CMP_STRIDE - 1
    for cc in range(n_cmp):
        for j in range(min(n_slc, 128)):
            lo = max(cc * 16, j * 64)
            hi = min(cc * 16 + 32, j * 64 + 64)
            ovl[cc, j] = max(hi - lo, 0) / 32.0
    ovl[:, 0] = 1.0
    c['ovl'] = ovl.reshape(4, 128, 128).transpose(1, 0, 2).copy()
    fb = np.zeros((128, 256), np.float32)
    for q in range(128):
        hv = q // 64
        for x in range(256):
            rel = x - 127
            if rel > hv:
                fb[q, x] = -1e30
            elif rel == hv or rel == hv - 1:
                fb[q, x] = 1e9
    c['fbase'] = fb
    kp = np.zeros((64, S), np.float32)
    yy = np.arange(S)
    kp[2 * ((yy // 128) % 32) + (yy % 128) // 64, yy] = 1
    c['kpat'] = kp
    pc = np.zeros((128, 2, 2), np.float32)
    pc[0:64, 0, 0] = 0.5
    pc[64:128, 0, 1] = 0.25
    pc[0:64, 1, 0] = 0.125
    pc[64:128, 1, 1] = 1.0 / 16
    c['poolc'] = pc
    corr = np.ones((128, 2, 16), np.float32)
    for ch in range(2):
        for half in range(2):
            w = (2, 4, 8, 16)[ch * 2 + half]
            for t in range(16):
                corr[half * 64:(half + 1) * 64, ch, t] = w / min(w, t + 1)
    c['poolcorr'] = corr
    return c


def layout_weights(inp):
    w = {}
    L = DEPTH
    cols = []
    cols += list(range(0, 512))
    cols += list(range(512, 768))
    kv0 = 768
    kc = list(range(kv0, kv0 + 64)); vc = list(range(kv0 + 64, kv0 + 128))
    ks = list(range(kv0 + 128, kv0 + 192)); vs = list(range(kv0 + 192, kv0 + 256))
    kw = list(range(kv0 + 256, kv0 + 320)); vw = list(range(kv0 + 320, kv0 + 384))
    cols += kc + vc + ks + ks + kw + kw
    g0 = 1152
    cols += list(range(g0, g0 + 12))
    cq0 = 1164
    cols += list(range(cq0, cq0 + 256))
    ckv0 = 1420
    cols += list(range(ckv0, ckv0 + 128))
    kr0 = 1548
    kr = list(range(kr0, kr0 + 32))
    cols += kr + kr[16:] + kr[:16]
    p0 = 1580
    cols += list(range(p0, p0 + 256))
    cols += vs + vw
    assert len(cols) == NCOL
    w['w_in'] = np.ascontiguousarray(inp['w_in'][:, :, cols])
    w['w_out'] = np.ascontiguousarray(inp['w_out'])

    def pc(v):
        return np.ascontiguousarray(v.reshape(L, -1, 128).transpose(0, 2, 1))
    w['g_mix'] = pc(inp['ln_mix_g'])
    w['g_mlp'] = pc(inp['ln_mlp_g'])
    w['g_fin'] = np.ascontiguousarray(inp['final_norm_g'].reshape(8, 128).T)
    w['conv_dw'] = np.ascontiguousarray(inp['conv_dw'].transpose(0, 2, 1).reshape(L, 2, 128, CONV_K).transpose(0, 2, 1, 3))
    w['conv_vec'] = np.ascontiguousarray(np.stack([pc(inp['conv_dw_b']), pc(inp['conv_ln_g']), pc(inp['conv_ln_b'])], 3))
    w['conv_pw'] = np.ascontiguousarray(inp['conv_pw'])
    pos = inp['nsa_cmp_pos']
    w['cmp_posT'] = np.ascontiguousarray(pos.transpose(0, 1, 3, 2).reshape(L, 128, 32))
    w1 = inp['nsa_cmp_w1'].reshape(L, 2, 32, 64, 128)
    w['cmp_w1'] = np.ascontiguousarray(w1.transpose(0, 1, 3, 2, 4).reshape(L, 128, 32, 128))
    w2 = inp['nsa_cmp_w2']
    w['cmp_w2'] = np.ascontiguousarray(np.concatenate([w2[:, 0], w2[:, 0], w2[:, 1]], axis=2))
    w['mla_g'] = np.ascontiguousarray(np.concatenate([pc(inp['mla_q_norm_g']), pc(inp['mla_kv_norm_g'])], 2))
    uq = inp['mla_w_uq'].reshape(L, 256, 4, 96)
    uq_l = np.concatenate([uq, uq[..., 80:96], uq[..., 64:80]], axis=3)
    w['w_uq'] = np.ascontiguousarray(uq_l.reshape(L, 256, 512))
    ukv = inp['mla_w_ukv'].reshape(L, 128, 4, 128)
    w['w_ukv'] = np.ascontiguousarray(np.concatenate([ukv[..., 0:64].reshape(L, 128, 256), ukv[..., 64:128].reshape(L, 128, 256)], axis=2))
    pw = inp['pool_w']
    bd = np.zeros((L, 2, 128, 128), np.float32)
    for ch in range(2):
        bd[:, ch, 0:64, 0:64] = pw[:, 2 * ch]
        bd[:, ch, 64:128, 64:128] = pw[:, 2 * ch + 1]
    w['pool_w'] = np.ascontiguousarray(bd.transpose(0, 2, 1, 3))
    w['pool_scale'] = pc(inp['pool_scale'])
    w['mlp_w1'] = np.ascontiguousarray(inp['mlp_w1'])
    w['mlp_w2'] = np.ascontiguousarray(inp['mlp_w2'])
    w['rel_table'] = np.ascontiguousarray(inp['rel_bias_table'])
    return w


def build_program(S, depth=DEPTH, debug=False, stop_after=None):
    nc = bass.Bass("TRN2", target_bir_lowering=False)
    NT = S // 512
    NKT = S // 128
    NCMP = S // 16 - 1
    NCC = (NCMP + 127) // 128
    P = Prog(nc)
    consts = host_constants(S)

    def din(name, shape, dt=F32):
        return nc.dram_tensor(name, list(shape), dt, kind="ExternalInput").ap()

    dbg_kind = "ExternalOutput" if debug else "Internal"

    def dscr(name, shape, dt):
        return nc.dram_tensor(name, list(shape), dt, kind=dbg_kind).ap()

    x_d = din("x", [S, 1024])
    w_in_d = din("w_in", [depth, 1024, NCOL])
    w_out_d = din("w_out", [depth, 1024, 1024])
    g_mix_d = din("g_mix", [depth, 128, 8])
    g_mlp_d = din("g_mlp", [depth, 128, 8])
    g_fin_d = din("g_fin", [128, 8])
    conv_dw_d = din("conv_dw", [depth, 128, 2, CONV_K])
    conv_vec_d = din("conv_vec", [depth, 128, 2, 3])
    conv_pw_d = din("conv_pw", [depth, 256, 256])
    cmp_posT_d = din("cmp_posT", [depth, 128, 32])
    cmp_w1_d = din("cmp_w1", [depth, 128, 32, 128])
    cmp_w2_d = din("cmp_w2", [depth, 128, 192])
    mla_g_d = din("mla_g", [depth, 128, 3])
    w_uq_d = din("w_uq", [depth, 256, 512])
    w_ukv_d = din("w_ukv", [depth, 128, 512])
    pool_w_d = din("pool_w", [depth, 128, 2, 128])
    pool_scale_d = din("pool_scale", [depth, 128, 2])
    mlp_w1_d = din("mlp_w1", [depth, 1024, 4096])
    mlp_w2_d = din("mlp_w2", [depth, 4096, 1024])
    rel_table_d = din("rel_table", [32, 4])
    cd = {k: din("c_" + k, v.shape) for k, v in consts.items()}
    out_d = nc.dram_tensor("out", [S, 1024], F32, kind="ExternalOutput").ap()

    hT_d = dscr("hT", [8, 128, S], F32)
    yT_d = dscr("yT", [8, 128, S], BF16)
    xn2_d = dscr("xn2", [8, 128, S], BF16)
    qn_d = dscr("qn", [2, 128, S], BF16)
    ks_d = dscr("ksT", [128, S], BF16)
    kw_d = dscr("kwT", [128, S], BF16)
    vtok_d = dscr("vtok", [NKT, 128, 130], BF16)
    gates_d = dscr("gatesT", [12, S], F32)
    qm_d = dscr("qm", [4, 96, S], BF16)
    km_d = dscr("km", [4, 96, S], BF16)
    vm_d = dscr("vm", [NKT, 128, 260], BF16)
    kvc_d = dscr("kvcT", [128, S], BF16)
    cq_d = dscr("cqT", [3, 128, S], F32)
    krr_d = dscr("krr", [64, S], F32)
    kc_d = dscr("kcT", [128, 512], BF16)
    vc_d = dscr("vcA", [128, 4, 65], BF16)
    evc_d = dscr("evc", [5, EVL], F32)
    evw_d = dscr("evw", [4, EVL], F32)
    gw_d = dscr("gw", [4, 128, 1408], BF16)
    gs_d = dscr("gs", [5, 128, 1024], BF16)
    gc_d = dscr("gc", [4, 5, 128, 512], BF16)

    top = contextlib.ExitStack()

    uid = [0]

    def mk(es, name, shape, dt):
        uid[0] += 1
        return es.enter_context(nc.sbuf_tensor(f"{name}_u{uid[0]}", list(shape), dt))

    psall = top.enter_context(nc.psum_tensor("psall", [128, 4096], F32))
    PS = [psall[:, i * 512:(i + 1) * 512] for i in range(8)]
    psn = [f"ps{i}" for i in range(8)]

    ident = mk(top, "ident", [128, 128], F32)
    antiI = mk(top, "antiI", [128, 128], F32)
    ones_bf = mk(top, "ones_bf", [128, 128], BF16)
    ones_f = mk(top, "ones_f", [128, 128], F32)
    b31 = mk(top, "b31", [128, 4], F32)
    P.dma('sp', lambda e: e.dma_start(out=ident[:], in_=cd['ident']), 'c0', writes=['ident'])
    P.dma('sp', lambda e: e.dma_start(out=antiI[:], in_=cd['antiI']), 'c1', writes=['antiI'])
    P.op('dve', lambda e: e.memset(ones_bf[:], 1.0), writes=['ones_bf'])
    P.op('dve', lambda e: e.memset(ones_f[:], 1.0), writes=['ones_f'])
    P.dma('sp', lambda e: e.dma_start(out=b31[:], in_=bass.AP(rel_table_d.tensor, 31 * 4, [[0, 128], [1, 4]])), 'c0', writes=['b31'])

    def mm(out, lhsT, rhs, start, stop, reads, writes):
        return P.op('pe', lambda e: e.matmul(out, lhsT=lhsT, rhs=rhs, start=start, stop=stop), reads, writes)

    def setup_tables():
        es = contextlib.ExitStack()
        tab = mk(es, "tab", [33, 5], F32)
        ohc = mk(es, "ohc", [33, EVL], F32)
        ohw = mk(es, "ohw", [33, EVL], F32)
        ev = mk(es, "ev", [5, 2, EVL], F32)
        hk = mk(es, "hk", [128, 2, 512], F32)
        gst = mk(es, "gst", [128, 2, 512], BF16)
        P.op('dve', lambda e: e.memset(tab[:], 0.0), writes=['tab'])
        P.dma('sp', lambda e: e.dma_start(out=tab[0:32, 0:4], in_=rel_table_d), 'c0', reads=[], writes=['tab'])
        P.op('dve', lambda e: e.memset(tab[32:33, :], -30000.0), writes=['tab'])
        P.dma('sp', lambda e: e.dma_start(out=ohc[:], in_=cd['oh_c']), 'c1', writes=['ohc'])
        P.dma('sp', lambda e: e.dma_start(out=ohw[:], in_=cd['oh_w']), 'c0', writes=['ohw'])
        nch = (EVL + 511) // 512
        for which, oh, ohn, nh in ((0, ohc, 'ohc', 5), (1, ohw, 'ohw', 4)):
            for ci in range(nch):
                lo = ci * 512
                hi = min(EVL, lo + 512)
                pi = ci % 2
                mm(PS[pi][0:nh, 0:hi - lo], tab[:, 0:nh], oh[:, lo:hi], True, True, ['tab', ohn], [psn[pi]])
                P.op('act', lambda e, pi=pi, lo=lo, hi=hi, nh=nh, which=which: e.activation(
                    out=ev[0:nh, which, lo:hi], in_=PS[pi][0:nh, 0:hi - lo], func=AF.Copy), [psn[pi]], ['ev'])
        P.dma('sp', lambda e: e.dma_start(out=evc_d, in_=ev[0:5, 0, :]), 'c0', reads=['ev'], writes=['evc_d'])
        P.dma('sp', lambda e: e.dma_start(out=evw_d, in_=ev[0:4, 1, :]), 'c1', reads=['ev'], writes=['evw_d'])
        jobs = []
        for h in range(4):
            for lo in range(0, 1408, 512):
                n = min(512, 1408 - lo)
                jobs.append((evw_d, 'evw_d', h * EVL + OFF - 511 + lo, 1, n, gw_d[h, :, lo:lo + n]))
        for h in range(5):
            for lo in range(0, 1024, 512):
                jobs.append((evc_d, 'evc_d', h * EVL + OFF - 511 + lo, 1, 512, gs_d[h, :, lo:lo + 512]))
        for h in range(4):
            for m in range(5):
                jobs.append((evc_d, 'evc_d', h * EVL + OFF + 512 * m - 2063, 16, 512, gc_d[h, m, :, :]))
        for ji, (src, srcn, off, pstep, n, dst) in enumerate(jobs):
            s = ji % 2
            P.dma('sp', lambda e, s=s, src=src, off=off, pstep=pstep, n=n: e.dma_start(
                out=hk[:, s, 0:n], in_=bass.AP(src.tensor, off, [[pstep, 128], [1, n]])), f'hk{s}',
                reads=[srcn], writes=[f'hk{s}'])
            pi = 2 + s
            mm(PS[pi][:, 0:n], antiI[:], hk[:, s, 0:n], True, True, ['antiI', f'hk{s}'], [psn[pi]])
            P.op('dve', lambda e, s=s, pi=pi, n=n: e.tensor_copy(out=gst[:, s, 0:n], in_=PS[pi][:, 0:n]), [psn[pi]], [f'gst{s}'])
            P.dma('pool', lambda e, s=s, n=n, dst=dst: e.dma_start(out=dst, in_=gst[:, s, 0:n]), f'gsto{s}',
                  reads=[f'gst{s}'], writes=['gtabs'])
        P.barrier()
        es.close()

    def phase0():
        es = contextlib.ExitStack()
        xin = mk(es, "xin", [128, 2, 4, 1024], F32)
        hst = mk(es, "hst", [128, 2, 8, 512], F32)
        for t in range(NT):
            s = t % 2
            P.dma('sp', lambda e, t=t, s=s: e.dma_start(
                out=xin[:, s], in_=x_d[t * 512:(t + 1) * 512, :].rearrange("(k p) f -> p k f", p=128)), f'xin{s}',
                writes=[f'xin{s}'])
            for c in range(8):
                pi = c % 4
                for sub in range(4):
                    P.op('pe', lambda e, pi=pi, sub=sub, s=s, c=c: e.transpose(
                        out=PS[pi][:, sub * 128:(sub + 1) * 128], in_=xin[:, s, sub, c * 128:(c + 1) * 128], identity=ident[:]),
                        [f'xin{s}', 'ident'], [psn[pi]])
                eng = 'act' if c % 2 == 0 else 'dve'
                if eng == 'act':
                    P.op('act', lambda e, pi=pi, s=s, c=c: e.activation(out=hst[:, s, c, :], in_=PS[pi][:], func=AF.Copy),
                         [psn[pi]], [f'hst{s}'])
                else:
                    P.op('dve', lambda e, pi=pi, s=s, c=c: e.tensor_copy(out=hst[:, s, c, :], in_=PS[pi][:]),
                         [psn[pi]], [f'hst{s}'])
            P.dma('pool', lambda e, t=t, s=s: e.dma_start(
                out=hT_d[:, :, t * 512:(t + 1) * 512].rearrange("c p s -> p c s"), in_=hst[:, s]), f'hsto{s}',
                reads=[f'hst{s}'], writes=['hT_d'])
        P.barrier()
        es.close()

    def rms_stats(src_ap_flat, nfeat_chunks, sq_tile, sqn, src_names, ps_i, rstd_tile, rstdn, inv_n):
        P.op('act', lambda e: e.activation(out=sq_tile, in_=src_ap_flat, func=AF.Square), src_names, [sqn])

    def phaseA(l):
        es = contextlib.ExitStack()
        win = mk(es, "win", [128, 8, NCOL], BF16)
        gmix = mk(es, "gmix", [128, 8], F32)
        dwt = mk(es, "dwt", [128, 2, CONV_K], F32)
        cvec = mk(es, "cvec", [128, 2, 3], F32)
        diag = mk(es, "diag", [128, 2, CONV_K, 128], BF16)
        identb = mk(es, "identb", [128, 128], BF16)
        pw = mk(es, "pw", [128, 2, 256], BF16)
        poolw = mk(es, "poolw", [128, 2, 128], BF16)
        pscale = mk(es, "pscale", [128, 2], F32)
        poolc = mk(es, "poolc", [128, 2, 2], F32)
        poolcorr = mk(es, "poolcorr", [128, 2, 16], F32)
        ones256 = mk(es, "ones256", [128, 128], BF16)
        ones1024 = mk(es, "ones1024", [128, 128], BF16)
        hin = mk(es, "hin", [128, 2, 8, 512], F32)
        sqb = mk(es, "sqb", [128, 8, 512], BF16)
        rstd = mk(es, "rstd", [128, 2, 512], F32)
        xn = mk(es, "xn", [128, 2, 8, 512], BF16)
        sig = mk(es, "sig", [128, 512], F32)
        convbuf = mk(es, "convbuf", [128, 2, 30 + 512], BF16)
        cx = mk(es, "cx", [128, 2, 512], F32)
        cxb = mk(es, "cxb", [128, 2, 2, 512], BF16)
        mean_sb = mk(es, "mean_sb", [128, 512], F32)
        tmpf = mk(es, "tmpf", [128, 2, 512], F32)
        sl = mk(es, "sl", [128, 2, 512], BF16)
        yst = mk(es, "yst", [128, 1, 4, 512], BF16)
        qst = mk(es, "qst", [128, 1, 2, 512], BF16)
        kst = mk(es, "kst", [128, 1, 3, 512], BF16)
        vst = mk(es, "vst", [128, 1, 4, 2, 65], BF16)
        gst = mk(es, "gstA", [12, 1, 512], F32)
        krst = mk(es, "krst", [64, 512], F32)
        cq = mk(es, "cq", [128, 3, 512], F32)
        ub = mk(es, "ub", [128, 2, 16 + 512], F32)
        s2 = mk(es, "s2", [128, 2, 16 + 512], F32)
        s4 = mk(es, "s4", [128, 2, 16 + 512], F32)
        s8 = mk(es, "s8", [128, 16 + 512], F32)
        s16 = mk(es, "s16", [128, 16 + 512], F32)
        dacc = mk(es, "dacc", [128, 2, 512], F32)
        db = mk(es, "db", [128, 2, 512], BF16)

        for jb in range(4):
            c0, c1 = jb * 512, min(NCOL, (jb + 1) * 512)
            P.dma('pool', lambda e, c0=c0, c1=c1: e.dma_start(out=win[:, :, c0:c1],
                                                               in_=w_in_d[l, :, c0:c1].rearrange("(c p) n -> p c n", p=128)), f'wl{jb % 2}',
                  writes=[f'win{jb}'])
        P.dma('sp', lambda e: e.dma_start(out=gmix[:], in_=g_mix_d[l]), 'c0', writes=['gmix'])
        P.dma('sp', lambda e: e.dma_start(out=dwt[:], in_=conv_dw_d[l]), 'c1', writes=['dwt'])
        P.dma('sp', lambda e: e.dma_start(out=cvec[:], in_=conv_vec_d[l]), 'c0', writes=['cvec'])
        P.dma('pool', lambda e: e.dma_start(out=pw[:], in_=conv_pw_d[l].rearrange("(c p) n -> p c n", p=128)), 'wl0', writes=['pw'])
        P.dma('pool', lambda e: e.dma_start(out=poolw[:], in_=pool_w_d[l]), 'wl0', writes=['poolw'])
        P.dma('sp', lambda e: e.dma_start(out=pscale[:], in_=pool_scale_d[l]), 'c0', writes=['pscale'])
        P.dma('sp', lambda e: e.dma_start(out=poolc[:], in_=cd['poolc']), 'c1', writes=['poolc'])
        P.dma('sp', lambda e: e.dma_start(out=poolcorr[:], in_=cd['poolcorr']), 'c0', writes=['poolcorr'])
        P.op('dve', lambda e: e.memset(ones256[:], 1.0 / 256), writes=['ones256'])
        P.op('dve', lambda e: e.memset(ones1024[:], 1.0 / 1024), writes=['ones1024'])
        P.op('dve', lambda e: e.tensor_copy(out=identb[:], in_=ident[:]), ['ident'], ['identb'])
        P.op('pool', lambda e: e.memset(convbuf[:], 0.0), writes=['convbuf'])
        P.op('pool', lambda e: e.memset(ub[:], 0.0), writes=['ub'])
        P.op('pool', lambda e: e.memset(s2[:], 0.0), writes=['s2'])
        P.op('pool', lambda e: e.memset(s4[:], 0.0), writes=['s4'])
        P.op('pool', lambda e: e.memset(s8[:], 0.0), writes=['s8'])
        P.op('pool', lambda e: e.memset(s16[:], 0.0), writes=['s16'])
        P.op('pool', lambda e: e.memset(vst[:], 1.0), writes=['vst0'])
        for ch in range(2):
            for k in range(CONV_K):
                eng = 'dve' if (k % 2 == 0) else 'pool'
                P.op(eng, lambda e, ch=ch, k=k: e.tensor_scalar(
                    out=diag[:, ch, k, :], in0=identb[:], scalar1=dwt[:, ch, k:k + 1], scalar2=None, op0=ALU.mult),
                    ['identb', 'dwt'], ['diag'])

        def load_tile(t):
            s = t % 2
            P.dma('sp', lambda e: e.dma_start(out=hin[:, s], in_=hT_d[:, :, t * 512:(t + 1) * 512].rearrange("c p s -> p c s")),
                  f'hin{s}', reads=['hT_d'], writes=[f'hin{s}'])

        cur = [0]

        def proj(col, M, pi, extra_writes=()):
            sx = cur[0]
            wn = [f'win{b_}' for b_ in range(col // 512, (col + M - 1) // 512 + 1)]
            for c in range(8):
                mm(PS[pi][0:M, :], win[:, c, col:col + M], xn[:, sx, c, :], c == 0, c == 7, wn + [f'xn{sx}'], [psn[pi]])

        def normA1(t):
            s_ = t % 2
            P.op('act', lambda e: e.activation(out=sqb[:].rearrange("p c s -> p (c s)"), in_=hin[:, s_].rearrange("p c s -> p (c s)"),
                                               func=AF.Square), [f'hin{s_}'], ['sqb'])

        def normA2(t):
            s_ = t % 2
            hn_ = f'hin{s_}'
            for c in range(8):
                mm(PS[0][:], ones1024[:], sqb[:, c, :], c == 0, c == 7, ['ones1024', 'sqb'], [psn[0]])
            P.op('act', lambda e: e.activation(out=rstd[:, s_, :], in_=PS[0][:], func=AF.Sqrt, bias=EPS, scale=1.0), [psn[0]], [f'rstd{s_}'])
            P.op('dve', lambda e: e.reciprocal(out=rstd[:, s_, :], in_=rstd[:, s_, :]), [f'rstd{s_}'], [f'rstd{s_}'])
            for c in range(8):
                P.op('dve', lambda e, c=c: e.scalar_tensor_tensor(out=xn[:, s_, c, :], in0=hin[:, s_, c, :], scalar=gmix[:, c:c + 1],
                                                                   in1=rstd[:, s_, :], op0=ALU.mult, op1=ALU.mult),
                     [hn_, 'gmix', f'rstd{s_}'], [f'xn{s_}'])

        def tileA(t):
            s = t % 2
            tsl = slice(t * 512, (t + 1) * 512)
            if t + 1 < NT:
                load_tile(t + 1)
            hn = f'hin{s}'
            so = 0
            cur[0] = s
            if t + 1 < NT:
                normA1(t + 1)
            for ch in range(2):
                proj(C_CONV + 256 + ch * 128, 128, 1)
                P.op('act', lambda e: e.activation(out=sig[:], in_=PS[1][:], func=AF.Sigmoid), [psn[1]], ['sig'])
                proj(C_CONV + ch * 128, 128, 2)
                P.op('dve', lambda e, ch=ch: e.tensor_tensor(out=convbuf[:, ch, 30:542], in0=PS[2][:], in1=sig[:], op=ALU.mult),
                     [psn[2], 'sig'], ['convbuf'])
            for m in range(2):
                pi = 5 + m
                proj(C_Q + m * 128, 128, pi)
                P.op('act', lambda e, m=m, pi=pi: e.activation(out=qst[:, so, m, :], in_=PS[pi][:], func=AF.Copy, scale=NSA_SCALE),
                     [psn[pi]], [f'qst{so}'])
            P.dma('pool', lambda e: e.dma_start(out=qn_d[:, :, tsl].rearrange("c p s -> p c s"), in_=qst[:, so]), f'qsto{s}',
                  reads=[f'qst{so}'], writes=['qn_d'])
            for ch in range(2):
                pi = 3 + ch
                for k in range(CONV_K):
                    mm(PS[pi][:], diag[:, ch, k, :], convbuf[:, ch, k:k + 512], k == 0, k == CONV_K - 1, ['diag', 'convbuf'], [psn[pi]])
                P.op('act', lambda e, ch=ch, pi=pi: e.activation(out=cx[:, ch, :], in_=PS[pi][:], func=AF.Identity,
                                                                bias=cvec[:, ch, 0:1], scale=1.0), [psn[pi], 'cvec'], ['cx'])
                P.op('act', lambda e, ch=ch, pi=pi: e.activation(out=cxb[:, ch, 1, :], in_=PS[pi][:], func=AF.Square,
                                                                bias=cvec[:, ch, 0:1], scale=1.0), [psn[pi], 'cvec'], ['cxb'])
                P.op('pool', lambda e, ch=ch: e.tensor_copy(out=cxb[:, ch, 0, :], in_=cx[:, ch, :]), ['cx'], ['cxb'])
            if t + 1 < NT:
                normA2(t + 1)
            P.op('pool', lambda e: e.tensor_copy(out=convbuf[:, :, 0:30], in_=convbuf[:, :, 512:542]), ['convbuf'], ['convbuf'])
            proj(C_KVC, 128, 7)
            P.op('dve', lambda e: e.tensor_copy(out=kst[:, so, 2, :], in_=PS[7][:]), [psn[7]], [f'kst{so}'])
            proj(C_KS, 128, 1)
            P.op('act', lambda e: e.activation(out=kst[:, so, 0, :], in_=PS[1][:], func=AF.Copy), [psn[1]], [f'kst{so}'])
            proj(C_KW, 128, 2)
            P.op('dve', lambda e: e.tensor_copy(out=kst[:, so, 1, :], in_=PS[2][:]), [psn[2]], [f'kst{so}'])
            P.dma('pool', lambda e: e.dma_start(out=ks_d[:, tsl], in_=kst[:, so, 0, :]), 'ksto0', reads=[f'kst{so}'], writes=['ks_d'])
            P.dma('pool', lambda e: e.dma_start(out=kw_d[:, tsl], in_=kst[:, so, 1, :]), 'ksto1', reads=[f'kst{so}'], writes=['kw_d'])
            P.dma('pool', lambda e: e.dma_start(out=kvc_d[:, tsl], in_=kst[:, so, 2, :]), 'ksto2', reads=[f'kst{so}'], writes=['kvc_d'])
            for ch in range(2):
                mm(PS[1][:], ones256[:], cxb[:, ch, 0, :], ch == 0, ch == 1, ['ones256', 'cxb'], [psn[1]])
            for ch in range(2):
                mm(PS[2][:], ones256[:], cxb[:, ch, 1, :], ch == 0, ch == 1, ['ones256', 'cxb'], [psn[2]])
            P.op('act', lambda e: e.activation(out=mean_sb[:], in_=PS[1][:], func=AF.Copy), [psn[1]], ['mean_sb'])
            P.op('dve', lambda e: e.tensor_tensor(out=tmpf[:, 0, :], in0=mean_sb[:], in1=mean_sb[:], op=ALU.mult), ['mean_sb'], ['tmpf0'])
            P.op('dve', lambda e: e.tensor_tensor(out=tmpf[:, 0, :], in0=PS[2][:], in1=tmpf[:, 0, :], op=ALU.subtract), [psn[2], 'tmpf0'], ['tmpf0'])
            P.op('act', lambda e: e.activation(out=tmpf[:, 0, :], in_=tmpf[:, 0, :], func=AF.Sqrt, bias=EPS, scale=1.0), ['tmpf0'], ['tmpf0'])
            P.op('dve', lambda e: e.reciprocal(out=tmpf[:, 0, :], in_=tmpf[:, 0, :]), ['tmpf0'], ['tmpf0'])
            for ch in range(2):
                P.op('dve', lambda e, ch=ch: e.tensor_tensor(out=cx[:, ch, :], in0=cx[:, ch, :], in1=mean_sb[:], op=ALU.subtract),
                     ['cx', 'mean_sb'], ['cx'])
                P.op('dve', lambda e, ch=ch: e.tensor_tensor(out=cx[:, ch, :], in0=cx[:, ch, :], in1=tmpf[:, 0, :], op=ALU.mult),
                     ['cx', 'tmpf0'], ['cx'])
                P.op('act', lambda e, ch=ch: e.activation(out=sl[:, ch, :], in_=cx[:, ch, :], func=AF.Silu,
                                                          bias=cvec[:, ch, 2:3], scale=cvec[:, ch, 1:2]), ['cx', 'cvec'], ['sl'])
            proj(C_GATE, 12, 3)
            P.op('act', lambda e: e.activation(out=gst[:, so, :], in_=PS[3][0:12, :], func=AF.Sigmoid), [psn[3]], [f'gstA{so}'])
            P.dma('pool', lambda e: e.dma_start(out=gates_d[:, tsl], in_=gst[:, so, :]), f'gsto{s}', reads=[f'gstA{so}'], writes=['gates_d'])
            for sub in range(4):
                for c in range(8):
                    mm(PS[4][:, sub * 128:(sub + 1) * 128], xn[:, s, c, sub * 128:(sub + 1) * 128], win[:, c, C_V:C_V + 128],
                       c == 0, c == 7, ['win3', f'xn{s}'], [psn[4]])
            P.op('dve', lambda e: e.tensor_copy(out=vst[:, so, :, :, 0:64],
                                                in_=PS[4][:].rearrange("p (a b c) -> p a b c", a=4, b=2)), [psn[4]], [f'vst{so}'])
            P.dma('pool', lambda e: e.dma_start(out=vtok_d[t * 4:(t + 1) * 4].rearrange("k p c -> p k c"),
                                                in_=vst[:, so].rearrange("p a b c -> p a (b c)")), f'vsto{s}',
                  reads=[f'vst{so}'], writes=['vtok_d'])
            for j, col in enumerate((C_CQ, C_CQ + 128, C_CKV)):
                pi = (5, 6, 7)[j]
                proj(col, 128, pi)
                if j % 2 == 0:
                    P.op('act', lambda e, j=j, pi=pi: e.activation(out=cq[:, j, :], in_=PS[pi][:], func=AF.Copy), [psn[pi]], ['cq'])
                else:
                    P.op('dve', lambda e, j=j, pi=pi: e.tensor_copy(out=cq[:, j, :], in_=PS[pi][:]), [psn[pi]], ['cq'])
            P.dma('pool', lambda e: e.dma_start(out=cq_d[:, :, tsl].rearrange("c p s -> p c s"), in_=cq[:]), 'cqo', reads=['cq'], writes=['cq_d'])
            proj(C_KR, 64, 0)
            P.op('act', lambda e: e.activation(out=krst[:], in_=PS[0][0:64, :], func=AF.Copy), [psn[0]], ['krst'])
            P.dma('pool', lambda e: e.dma_start(out=krr_d[:, tsl], in_=krst[:]), 'kro', reads=['krst'], writes=['krr_d'])
            for m in range(2):
                pi = 3 + m
                for ch in range(2):
                    mm(PS[pi][:], pw[:, ch, m * 128:(m + 1) * 128], sl[:, ch, :], ch == 0, ch == 1, ['pw', 'sl'], [psn[pi]])
                P.op('dve', lambda e, m=m, pi=pi: e.tensor_copy(out=yst[:, so, m, :], in_=PS[pi][:]), [psn[pi]], [f'yst{so}'])
            for ch in range(2):
                pi = 3 + ch
                proj(C_POOL + ch * 128, 128, pi)
                P.op('act', lambda e, ch=ch, pi=pi: e.activation(out=ub[:, ch, 16:528], in_=PS[pi][:], func=AF.Copy), [psn[pi]], ['ub'])
            P.op('pool', lambda e: e.tensor_tensor(out=s2[:, :, 2:528], in0=ub[:, :, 2:528], in1=ub[:, :, 1:527], op=ALU.add), ['ub'], ['s2'])
            P.op('pool', lambda e: e.tensor_tensor(out=s4[:, :, 4:528], in0=s2[:, :, 4:528], in1=s2[:, :, 2:526], op=ALU.add), ['s2'], ['s4'])
            P.op('pool', lambda e: e.tensor_tensor(out=s8[:, 8:528], in0=s4[:, 1, 8:528], in1=s4[:, 1, 4:524], op=ALU.add), ['s4'], ['s8'])
            P.op('pool', lambda e: e.tensor_tensor(out=s16[:, 16:528], in0=s8[:, 16:528], in1=s8[:, 8:520], op=ALU.add), ['s8'], ['s16'])
            srcs = ((s2[:, 0, 16:528], s4[:, 0, 16:528]), (s8[:, 16:528], s16[:, 16:528]))
            for ch in range(2):
                P.op('dve', lambda e, ch=ch: e.scalar_tensor_tensor(out=dacc[:, ch, :], in0=srcs[ch][0], scalar=poolc[:, ch, 0:1],
                                                                     in1=ub[:, ch, 16:528], op0=ALU.mult, op1=ALU.subtract),
                     ['s2', 's8', 'ub', 'poolc'], ['dacc'])
                P.op('dve', lambda e, ch=ch: e.scalar_tensor_tensor(out=dacc[:, ch, :], in0=srcs[ch][1], scalar=poolc[:, ch, 1:2],
                                                                     in1=dacc[:, ch, :], op0=ALU.mult, op1=ALU.add),
                     ['s4', 's16', 'dacc', 'poolc'], ['dacc'])
            if t == 0:
                P.op('dve', lambda e: e.tensor_tensor(out=dacc[:, :, 0:16], in0=dacc[:, :, 0:16], in1=ub[:, :, 16:32], op=ALU.add), ['dacc', 'ub'], ['dacc'])
                P.op('dve', lambda e: e.tensor_tensor(out=dacc[:, :, 0:16], in0=dacc[:, :, 0:16], in1=poolcorr[:], op=ALU.mult), ['dacc', 'poolcorr'], ['dacc'])
                P.op('dve', lambda e: e.tensor_tensor(out=dacc[:, :, 0:16], in0=dacc[:, :, 0:16], in1=ub[:, :, 16:32], op=ALU.subtract), ['dacc', 'ub'], ['dacc'])
            P.op('pool', lambda e: e.tensor_copy(out=db[:], in_=dacc[:]), ['dacc'], ['db'])
            P.op('pool', lambda e: e.tensor_copy(out=ub[:, :, 0:16], in_=ub[:, :, 512:528]), ['ub'], ['ub'])
            for ch in range(2):
                pi = 5 + ch
                mm(PS[pi][:], poolw[:, ch, :], db[:, ch, :], True, True, ['poolw', 'db'], [psn[pi]])
                P.op('act', lambda e, ch=ch, pi=pi: e.activation(out=yst[:, so, 2 + ch, :], in_=PS[pi][:], func=AF.Copy,
                                                                scale=pscale[:, ch:ch + 1]), [psn[pi], 'pscale'], [f'yst{so}'])
            P.dma('pool', lambda e: e.dma_start(out=yT_d[0:2, :, tsl].rearrange("c p s -> p c s"), in_=yst[:, so, 0:2, :]), f'ysto{s}',
                  reads=[f'yst{so}'], writes=['yT_d'])
            P.dma('pool', lambda e: e.dma_start(out=yT_d[6:8, :, tsl].rearrange("c p s -> p c s"), in_=yst[:, so, 2:4, :]), f'ysto{s}',
                  reads=[f'yst{so}'], writes=['yT_d'])
        load_tile(0)
        normA1(0)
        normA2(0)
        for t in range(NT):
            tileA(t)
        P.barrier()
        es.close()

    def attn_run(items, spairs, pbuf, pnames):
        groups = []
        k = 0
        while k < len(items):
            if k + 1 < len(items) and items[k]['cls'] == items[k + 1]['cls']:
                groups.append([items[k], items[k + 1]])
                k += 2
            else:
                groups.append([items[k]])
                k += 1
        ng = len(groups)

        def emit_s(g):
            bp = spairs[g % len(spairs)]
            for u, it in enumerate(groups[g]):
                bnk = bp[u]
                madds = it.get('madd') or []
                mm(PS[bnk][:, :], it['kT'], it['q'], True, len(madds) == 0, it['kn'] + it['qn'], [psn[bnk]])
                for mi, madd in enumerate(madds):
                    mm(PS[bnk][:, :], madd[0], madd[1], False, mi == len(madds) - 1, madd[2], [psn[bnk]])

        if ng == 0:
            return
        LA = len(spairs) - 1
        for g in range(min(LA, ng)):
            emit_s(g)
        for g in range(ng):
            if g + LA < ng:
                emit_s(g + LA)
            grp = groups[g]
            bp = spairs[g % len(spairs)]
            pi = g % len(pnames)
            pn = pnames[pi]
            w = 512 * len(grp)
            src = psall[:, bp[0] * 512:bp[0] * 512 + w]
            dst = pbuf[:, pi, 0:w]
            rd = [psn[bp[u]] for u in range(len(grp))]
            it0 = grp[0]
            if it0['bias'] is not None:
                P.op('act', lambda e, src=src, dst=dst, it0=it0: e.activation(out=dst, in_=src, func=AF.Exp, bias=it0['bias'], scale=1.0),
                     rd + ['b31'], [pn])
            else:
                P.op('act', lambda e, src=src, dst=dst: e.activation(out=dst, in_=src, func=AF.Exp), rd, [pn])
            for u, it in enumerate(grp):
                pt = pbuf[:, pi, u * 512:(u + 1) * 512]
                for (map_, mnames, meng) in it['mults']:
                    P.op(meng, lambda e, pt=pt, map_=map_: e.tensor_tensor(out=pt, in0=pt, in1=map_, op=ALU.mult), [pn] + mnames, [pn])
            for u, it in enumerate(grp):
                pt = pbuf[:, pi, u * 512:(u + 1) * 512]
                mm(it['o'], it['v'], pt, it['start'], it['stop'], it['vnames'] + [pn], [it['on']])
                if it.get('post') is not None:
                    it['post']()

    def phaseNSA(l):
        es = contextlib.ExitStack()
        ksX = mk(es, "ksX", [128, S], BF16)
        rq = mk(es, "rq", [128, 4, 2, 512], BF16)
        kwT = mk(es, "kwT", [128, 2, S], BF16)
        vtok = mk(es, "vtokS", [128, NKT, 130], BF16)
        kcT = mk(es, "kcTs", [128, 2, 512], BF16)
        vcA = mk(es, "vcAs", [128, 4, 65], BF16)
        gw = mk(es, "gwS", [128, 4, 1408], BF16)
        gs = mk(es, "gsS", [128, 4, 1024], BF16)
        gc = mk(es, "gcS", [128, 4, 5, 512], BF16)
        ovl = mk(es, "ovlS", [128, 4, 128], BF16)
        fbase = mk(es, "fbaseS", [128, 256], F32)
        identb = mk(es, "identbS", [128, 128], BF16)
        pbuf2 = mk(es, "pbuf2S", [128, 4, 1024], BF16)
        nsT2 = mk(es, "nsT2", [128, 512], BF16)
        qT = mk(es, "qTS", [128, 2, 2, 512], BF16)
        gat = mk(es, "gatS", [128, 2, 512], F32)
        pbuf = mk(es, "pbufS", [128, 4, 512], BF16)
        impsum = mk(es, "impsum", [128, 4, 128], F32)
        rimp = mk(es, "rimp", [128, 4], F32)
        adj = mk(es, "adj", [128, 2, 128], F32)
        m8 = mk(es, "m8", [128, 16], F32)
        thr = mk(es, "thr", [128, 1], F32)
        sel = mk(es, "sel", [128, 4, 128], F32)
        rinv = mk(es, "rinv", [128, 2, 512], F32)
        osb = mk(es, "osb", [64, 4, 512], F32)
        rs4 = mk(es, "rs4", [128, 512], F32)
        gat4 = mk(es, "gat4", [128, 512], F32)
        whl4 = mk(es, "whl4", [128, 2, 512], BF16)
        whl3 = mk(es, "whl3", [128, 2, 512], BF16)
        ynsa = mk(es, "ynsa", [64, 4, 512], F32)
        ytmp = mk(es, "ytmp", [64, 2, 512], F32)
        ynb = mk(es, "ynb", [64, 1, 4, 512], BF16)

        P.dma('sp', lambda e: e.dma_start(out=ksX[0:64, :], in_=ks_d[0:64, :]), 'n0', writes=['ksX'])
        P.op('pool', lambda e: e.memset(kwT[64:128, 0, :], 0.0), writes=['kwT'])
        P.op('pool', lambda e: e.memset(kwT[0:64, 1, :], 0.0), writes=['kwT'])
        P.dma('sp', lambda e: e.dma_start(out=kwT[0:64, 0, :], in_=kw_d[0:64, :]), 'n1', writes=['kwT'])
        P.dma('sp', lambda e: e.dma_start(out=kwT[64:128, 1, :], in_=kw_d[64:128, :]), 'n0', writes=['kwT'])
        for k0 in range(0, NKT, 8):
            P.dma('sp', lambda e, k0=k0: e.dma_start(out=vtok[:, k0:k0 + 8, :], in_=vtok_d[k0:k0 + 8].rearrange("k p c -> p k c")),
                  f'n{(k0 // 8) % 2}', writes=['vtok'])
        P.op('pool', lambda e: e.memset(kcT[64:128, 0, :], 0.0), writes=['kcT'])
        P.op('pool', lambda e: e.memset(kcT[0:64, 1, :], 0.0), writes=['kcT'])
        P.dma('sp', lambda e: e.dma_start(out=kcT[0:64, 0, :], in_=kc_d[0:64, :]), 'n1', writes=['kcT'])
        P.dma('sp', lambda e: e.dma_start(out=kcT[64:128, 1, :], in_=kc_d[64:128, :]), 'n0', writes=['kcT'])
        P.dma('sp', lambda e: e.dma_start(out=vcA[:], in_=vc_d), 'n0', writes=['vcA'])
        P.dma('sp', lambda e: e.dma_start(out=gw[:], in_=gw_d.rearrange("h p x -> p h x")), 'n1', writes=['gw'])
        P.dma('sp', lambda e: e.dma_start(out=gs[:], in_=gs_d[0:4].rearrange("h p x -> p h x")), 'n0', writes=['gs'])
        for h in range(4):
            P.dma('sp', lambda e, h=h: e.dma_start(out=gc[:, h], in_=gc_d[h].rearrange("m p x -> p m x")), f'n{h % 2}', writes=['gc'])
        P.dma('pool', lambda e: e.dma_start(out=ovl[:], in_=cd['ovl']), 'wl0', writes=['ovl'])
        P.op('dve', lambda e: e.memset(rs4[:], 1.0), writes=['rs4'])
        P.op('dve', lambda e: e.memset(gat4[:], 1.0), writes=['gat4'])
        P.op('dve', lambda e: e.tensor_copy(out=identb[:], in_=ident[:]), ['ident'], ['identb'])
        P.dma('sp', lambda e: e.dma_start(out=fbase[:], in_=cd['fbase']), 'n0', writes=['fbase'])
        for c0 in range(0, S, 2048):
            P.dma('pool', lambda e, c0=c0: e.dma_start(out=ksX[64:128, c0:c0 + 2048], in_=cd['kpat'][:, c0:c0 + 2048]), f'wl{(c0 // 2048) % 2}',
                  writes=['ksX'])

        def load_q(i):
            s = i % 2
            P.dma('sp', lambda e: e.dma_start(out=qT[:, s], in_=qn_d[:, :, i * 512:(i + 1) * 512].rearrange("c p s -> p c s")),
                  f'nq{s}', writes=[f'qT{s}'])

        def evac(h, ob):
            P.op('dve', lambda e: e.tensor_copy(out=osb[0:64, h, :], in_=PS[ob][0:64, :]), [psn[ob]], [f'osb{h}'])
            P.op('dve', lambda e: e.tensor_copy(out=rs4[32 * h:32 * h + 1, :], in_=PS[ob][64:65, :]), [psn[ob]], ['rs4'])

        def finish_stage(br, first, tsl):
            for h in range(4):
                P.dma('sp', lambda e, h=h: e.dma_start(out=gat4[32 * h:32 * h + 1, :], in_=gates_d[3 * h + br:3 * h + br + 1, tsl]), f'ngat{h % 2}',
                      reads=['gates_d'], writes=['gat4'])
            P.op('dve', lambda e: e.tensor_scalar(out=rs4[:], in0=rs4[:], scalar1=1e-30, scalar2=None, op0=ALU.max), ['rs4'], ['rs4'])
            P.op('dve', lambda e: e.reciprocal(out=rs4[:], in_=rs4[:]), ['rs4'], ['rs4'])
            P.op('dve', lambda e: e.tensor_tensor(out=rs4[:], in0=rs4[:], in1=gat4[:], op=ALU.mult), ['rs4', 'gat4'], ['rs4'])
            P.op('dve', lambda e: e.tensor_copy(out=whl4[:, 0, :], in_=rs4[:]), ['rs4'], ['whl4'])
            P.op('dve', lambda e: e.tensor_tensor(out=whl4[:, 1, :], in0=rs4[:], in1=whl4[:, 0, :], op=ALU.subtract), ['rs4', 'whl4'], ['whl4'])
            P.op('dve', lambda e: e.tensor_copy(out=whl3[64:65, :, :], in_=whl4[96:97, :, :]), ['whl4'], ['whl3'])
            P.op('dve', lambda e: e.memset(rs4[:], 1.0), ['rs4', 'whl4'], ['rs4'])
            for h in range(4):
                r = h % 2
                if h < 3:
                    src_hi, src_lo, lh, nm = whl4[32 * h:32 * h + 1, 0, :], whl4[32 * h:32 * h + 1, 1, :], ones_bf[32 * h:32 * h + 1, 0:64], 'whl4'
                else:
                    src_hi, src_lo, lh, nm = whl3[64:65, 0, :], whl3[64:65, 1, :], ones_bf[64:65, 0:64], 'whl3'
                mm(PS[r][0:64, :], lh, src_hi, True, False, ['ones_bf', nm], [psn[r]])
                mm(PS[r][0:64, :], lh, src_lo, False, True, ['ones_bf', nm], [psn[r]])
                if first:
                    P.op('dve', lambda e, h=h, r=r: e.tensor_tensor(out=ynsa[:, h, :], in0=PS[r][0:64, :], in1=osb[0:64, h, :], op=ALU.mult),
                         [psn[r], f'osb{h}'], [f'ynsa{h}'])
                else:
                    P.op('dve', lambda e, h=h, r=r: e.tensor_tensor(out=ytmp[:, r, :], in0=PS[r][0:64, :], in1=osb[0:64, h, :], op=ALU.mult),
                         [psn[r], f'osb{h}'], [f'ytmp{r}'])
                    P.op('pool', lambda e, h=h, r=r: e.tensor_tensor(out=ynsa[:, h, :], in0=ynsa[:, h, :], in1=ytmp[:, r, :], op=ALU.add),
                         [f'ynsa{h}', f'ytmp{r}'], [f'ynsa{h}'])

        def qtile(i):
            s = i % 2
            tsl = slice(i * 512, (i + 1) * 512)
            if i + 1 < NT:
                load_q(i + 1)
            qn_ = [f'qT{s}']

            def qap(h):
                return qT[64 * (h % 2):64 * (h % 2) + 64, s, h // 2, :]

            _MARKS.append(('q%d_start' % i, len([1 for it in P.ops['pe'] if it[0] == 'op'])))
            ncc = min(NCC, (32 * i + 30) // 128 + 1)
            for h in range(4):
                pr = slice(64 * (h % 2), 64 * (h % 2) + 64)
                ib = 2 + (h % 2)
                items = []
                for cc in range(ncc):
                    m = i - 4 * cc
                    near = m <= 4
                    items.append(dict(kT=kcT[:, h % 2, cc * 128:(cc + 1) * 128], q=qT[:, s, h // 2, :], kn=['kcT'], qn=qn_,
                                      bias=None if near else b31[:, h:h + 1],
                                      mults=[], madd=[(identb[:], gc[:, h, m, :], ['identb', 'gc'])] if near else [],
                                      v=vcA[:, cc, :], vnames=['vcA'], o=PS[6 + h % 2][0:65, :], on=psn[6 + h % 2],
                                      start=(cc == 0), stop=(cc == ncc - 1)))
                n = len(items)
                sb = [0, 1]

                def emit_s(k):
                    it = items[k]
                    b = sb[k % 2]
                    madds = it.get('madd') or []
                    mm(PS[b][:, :], it['kT'], it['q'], True, len(madds) == 0, it['kn'] + it['qn'], [psn[b]])
                    for mi, madd in enumerate(madds):
                        mm(PS[b][:, :], madd[0], madd[1], False, mi == len(madds) - 1, madd[2], [psn[b]])
                emit_s(0)
                for k in range(n):
                    it = items[k]
                    if k + 1 < n:
                        emit_s(k + 1)
                    b = sb[k % 2]
                    pi = k % 4
                    pt = pbuf[:, pi, :]
                    pn = f'pb{pi}'
                    if it['bias'] is not None:
                        P.op('act', lambda e, b=b, pt=pt, it=it: e.activation(out=pt, in_=PS[b][:, :], func=AF.Exp, bias=it['bias'], scale=1.0),
                             [psn[b], 'b31'], [pn])
                    else:
                        P.op('act', lambda e, b=b, pt=pt: e.activation(out=pt, in_=PS[b][:, :], func=AF.Exp), [psn[b]], [pn])
                    for (map_, mnames, meng) in it['mults']:
                        P.op(meng, lambda e, pt=pt, map_=map_: e.tensor_tensor(out=pt, in0=pt, in1=map_, op=ALU.mult), [pn] + mnames, [pn])
                    mm(it['o'], it['v'], pt, it['start'], it['stop'], it['vnames'] + [pn], [it['on']])
                    for qb in range(4):
                        mm(PS[ib][:, qb * 128:(qb + 1) * 128], pbuf[:, pi, qb * 128:(qb + 1) * 128], ovl[:, k, :],
                           (k == 0 and qb == 0), (k == n - 1 and qb == 3), [pn, 'ovl'], [psn[ib]])
                ib3 = PS[ib][:].rearrange("p (a b) -> p a b", a=4)
                P.op('dve', lambda e, ib3=ib3: e.tensor_scalar(out=rimp[:], in0=ib3[:, :, 0], scalar1=1e-30, scalar2=None, op0=ALU.max),
                     [psn[ib]], ['rimp'])
                P.op('dve', lambda e: e.reciprocal(out=rimp[:], in_=rimp[:]), ['rimp'], ['rimp'])
                for qb in range(4):
                    if h == 0:
                        P.op('dve', lambda e, qb=qb, ib3=ib3: e.tensor_scalar(out=impsum[:, qb, :], in0=ib3[:, qb, :], scalar1=rimp[:, qb:qb + 1],
                                                                               scalar2=None, op0=ALU.mult), [psn[ib], 'rimp'], ['impsum'])
                    else:
                        P.op('dve', lambda e, qb=qb, ib3=ib3: e.scalar_tensor_tensor(out=impsum[:, qb, :], in0=ib3[:, qb, :], scalar=rimp[:, qb:qb + 1],
                                                                                      in1=impsum[:, qb, :], op0=ALU.mult, op1=ALU.add),
                             [psn[ib], 'rimp', 'impsum'], ['impsum'])
                evac(h, 6 + h % 2)
            finish_stage(0, True, tsl)
            _MARKS.append(('q%d_cmpdone' % i, len([1 for it in P.ops['pe'] if it[0] == 'op'])))
            for qb in range(4):
                g = 4 * i + qb
                a = qb % 2
                P.op('dve', lambda e, qb=qb, g=g, a=a: e.tensor_tensor(out=adj[:, a, :], in0=impsum[:, qb, :], in1=fbase[:, 127 - 2 * g:255 - 2 * g],
                                                                      op=ALU.add), ['impsum', 'fbase'], [f'adj{a}'])
                P.op('dve', lambda e, a=a: e.memset(adj[:, a, 0:1], 1e9), [], [f'adj{a}'])
                P.op('dve', lambda e, a=a: e.max(out=m8[:, 0:8], in_=adj[:, a, :]), [f'adj{a}'], ['m8'])
                P.op('dve', lambda e, a=a, qb=qb: e.match_replace(out=sel[:, qb, :], in_to_replace=m8[:, 0:8], in_values=adj[:, a, :], imm_value=-3e38),
                     [f'adj{a}', 'm8'], [f'sel{qb}'])
                P.op('dve', lambda e, qb=qb: e.max(out=m8[:, 8:16], in_=sel[:, qb, :]), [f'sel{qb}'], ['m8'])
                P.op('dve', lambda e: e.tensor_scalar(out=thr[:], in0=m8[:, 15:16], scalar1=-1e29, scalar2=None, op0=ALU.max), ['m8'], ['thr'])
                P.op('dve', lambda e, a=a, qb=qb: e.tensor_scalar(out=sel[:, qb, :], in0=adj[:, a, :], scalar1=thr[:, 0:1], scalar2=None, op0=ALU.is_ge),
                     [f'adj{a}', 'thr'], [f'sel{qb}'])
                P.op('pe', lambda e, qb=qb: e.transpose(out=PS[2][:, qb * 128:(qb + 1) * 128], in_=sel[:, qb, :], identity=ident[:]),
                     [f'sel{qb}', 'ident'], [psn[2]])
            for g_ in range(2):
                P.op('dve', lambda e, g_=g_: e.tensor_scalar(
                    out=rq[64:128, :, g_, :], in0=PS[2][64 * g_:64 * g_ + 64, :].unsqueeze(1).to_broadcast([64, 4, 512]),
                    scalar1=30000.0, scalar2=-30000.0, op0=ALU.mult, op1=ALU.add), [psn[2]], ['rq'])
            for h in range(4):
                srcq = qT[64 * (h % 2):64 * (h % 2) + 64, s, h // 2, :].unsqueeze(1).to_broadcast([64, 2, 512])
                if h % 2 == 0:
                    P.op('act', lambda e, h=h, srcq=srcq: e.activation(out=rq[0:64, h, :, :], in_=srcq, func=AF.Copy), qn_, ['rq'])
                else:
                    P.op('pool', lambda e, h=h, srcq=srcq: e.tensor_copy(out=rq[0:64, h, :, :], in_=srcq), qn_, ['rq'])
            _MARKS.append(('q%d_seldone' % i, len([1 for it in P.ops['pe'] if it[0] == 'op'])))
            nkt = 4 * i + 4
            items = []
            for h in range(4):
                pr = slice(64 * (h % 2), 64 * (h % 2) + 64)
                for kt in range(nkt):
                    near = kt >= 4 * i - 1
                    j = kt - 4 * i + 4
                    mults = []
                    madds = []
                    if near:
                        madds.append((identb[:], gs[:, h, 128 * (7 - j):128 * (7 - j) + 512], ['identb', 'gs']))
                    items.append(dict(kT=ksX[:, kt * 128:(kt + 1) * 128], q=rq[:, h, (2 * kt) // 64, :], kn=['ksX'], qn=['rq'],
                                      bias=None if near else b31[:, h:h + 1], mults=mults,
                                      madd=madds,
                                      v=vtok[:, kt, 0:65], vnames=['vtok'], o=PS[6 + h % 2][0:65, :], on=psn[6 + h % 2],
                                      start=(kt == 0), stop=(kt == nkt - 1), cls=(h, near),
                                      post=(lambda h=h: evac(h, 6 + h % 2)) if kt == nkt - 1 else None))
            attn_run(items, [(0, 1), (2, 3), (4, 5)], pbuf2, [f'pq{k}' for k in range(4)])
            finish_stage(1, False, tsl)
            _MARKS.append(('q%d_sbdone' % i, len([1 for it in P.ops['pe'] if it[0] == 'op'])))
            items = []
            kts = [kt for kt in range(4 * i - 4, 4 * i + 4) if kt >= 0]
            for h in range(4):
                pr = slice(64 * (h % 2), 64 * (h % 2) + 64)
                for kt in kts:
                    j = kt - 4 * i + 4
                    items.append(dict(kT=kwT[:, h % 2, kt * 128:(kt + 1) * 128], q=qT[:, s, h // 2, :], kn=['kwT'], qn=qn_, bias=None,
                                      mults=[], madd=[(identb[:], gw[:, h, 128 * (7 - j):128 * (7 - j) + 512], ['identb', 'gw'])],
                                      v=vtok[:, kt, 65:130], vnames=['vtok'], o=PS[6 + h % 2][0:65, :], on=psn[6 + h % 2],
                                      start=(kt == kts[0]), stop=(kt == kts[-1]), cls=(h,),
                                      post=(lambda h=h: evac(h, 6 + h % 2)) if kt == kts[-1] else None))
            attn_run(items, [(0, 1), (2, 3), (4, 5)], pbuf2, [f'pq{k}' for k in range(4)])
            finish_stage(2, False, tsl)
            _MARKS.append(('q%d_windone' % i, len([1 for it in P.ops['pe'] if it[0] == 'op'])))
            for h in range(4):
                P.op('pool' if h % 2 else 'act',
                     (lambda e, h=h: e.tensor_copy(out=ynb[:, 0, h, :], in_=ynsa[:, h, :])) if h % 2 else
                     (lambda e, h=h: e.activation(out=ynb[:, 0, h, :], in_=ynsa[:, h, :], func=AF.Copy)),
                     [f'ynsa{h}'], ['ynb0'])
            P.dma('pool', lambda e: e.dma_start(out=yT_d[2:4, :, tsl].rearrange("c (hh p) s -> p (c hh) s", hh=2), in_=ynb[:, 0]), f'nyo{s}',
                  reads=['ynb0'], writes=['yT_d'])

        load_q(0)
        for i in range(NT):
            qtile(i)
        P.barrier()
        es.close()

    def phaseMLA(l):
        es = contextlib.ExitStack()
        kmS = mk(es, "kmS", [96, 4, S], BF16)
        vmS = mk(es, "vmS", [128, NKT, 260], BF16)
        g0 = mk(es, "g0S", [128, 1024], BF16)
        identb = mk(es, "identbM", [128, 128], BF16)
        qmS = mk(es, "qmS", [96, 2, 4, 512], BF16)
        pbuf = mk(es, "pbufM", [128, 4, 1024], BF16)
        rinv = mk(es, "rinvM", [128, 2, 512], F32)
        osb = mk(es, "osbM", [64, 4, 512], F32)
        rs4 = mk(es, "rs4M", [128, 512], F32)
        whl4 = mk(es, "whl4M", [128, 2, 512], BF16)
        whl3 = mk(es, "whl3M", [128, 2, 512], BF16)
        ymb = mk(es, "ymb", [64, 2, 4, 512], BF16)
        for h in range(4):
            P.dma('sp', lambda e, h=h: e.dma_start(out=kmS[:, h, :], in_=km_d[h]), f'n{h % 2}', writes=['kmS'])
        for k0 in range(0, NKT, 8):
            P.dma('sp', lambda e, k0=k0: e.dma_start(out=vmS[:, k0:k0 + 8, :], in_=vm_d[k0:k0 + 8].rearrange("k p c -> p k c")),
                  f'n{(k0 // 8) % 2}', writes=['vmS'])
        P.dma('sp', lambda e: e.dma_start(out=g0[:], in_=gs_d[4]), 'n1', writes=['g0'])
        P.op('dve', lambda e: e.memset(rs4[:], 1.0), writes=['rs4M'])
        P.op('dve', lambda e: e.tensor_copy(out=identb[:], in_=ident[:]), ['ident'], ['identbM'])

        def load_q(i):
            s = i % 2
            P.dma('sp', lambda e: e.dma_start(out=qmS[:, s], in_=qm_d[:, :, i * 512:(i + 1) * 512].rearrange("h p s -> p h s")),
                  f'nq{s}', writes=[f'qm{s}'])

        def qtile(i):
            s = i % 2
            tsl = slice(i * 512, (i + 1) * 512)
            if i + 1 < NT:
                load_q(i + 1)
            def mla_evac(h):
                ob = 6 + h % 2
                P.op('dve', lambda e: e.tensor_copy(out=osb[0:64, h, :], in_=PS[ob][0:64, :]), [psn[ob]], [f'osbM{h}'])
                P.op('dve', lambda e: e.tensor_copy(out=rs4[32 * h:32 * h + 1, :], in_=PS[ob][64:65, :]), [psn[ob]], ['rs4M'])

            nkt = 4 * i + 4
            items = []
            for h in range(4):
                for kt in range(nkt):
                    j = kt - 4 * i + 4
                    diag_ = kt >= 4 * i
                    items.append(dict(kT=kmS[:, h, kt * 128:(kt + 1) * 128], q=qmS[:, s, h, :], kn=['kmS'], qn=[f'qm{s}'], bias=None,
                                      mults=[], madd=[(identb[:], g0[:, 128 * (7 - j):128 * (7 - j) + 512], ['identbM', 'g0'])] if diag_ else [],
                                      v=vmS[:, kt, h * 65:(h + 1) * 65], vnames=['vmS'], o=PS[6 + h % 2][0:65, :], on=psn[6 + h % 2],
                                      start=(kt == 0), stop=(kt == nkt - 1), cls=(h,),
                                      post=(lambda h=h: mla_evac(h)) if kt == nkt - 1 else None))
            attn_run(items, [(0, 1), (2, 3), (4, 5)], pbuf, [f'pm{k}' for k in range(4)])
            P.op('dve', lambda e: e.reciprocal(out=rs4[:], in_=rs4[:]), ['rs4M'], ['rs4M'])
            P.op('dve', lambda e: e.tensor_copy(out=whl4[:, 0, :], in_=rs4[:]), ['rs4M'], ['whl4M'])
            P.op('dve', lambda e: e.tensor_tensor(out=whl4[:, 1, :], in0=rs4[:], in1=whl4[:, 0, :], op=ALU.subtract), ['rs4M', 'whl4M'], ['whl4M'])
            P.op('dve', lambda e: e.tensor_copy(out=whl3[64:65, :, :], in_=whl4[96:97, :, :]), ['whl4M'], ['whl3M'])
            P.op('dve', lambda e: e.memset(rs4[:], 1.0), ['rs4M', 'whl4M'], ['rs4M'])
            for h in range(4):
                r = h % 2
                if h < 3:
                    src_hi, src_lo, lh, nm = whl4[32 * h:32 * h + 1, 0, :], whl4[32 * h:32 * h + 1, 1, :], ones_bf[32 * h:32 * h + 1, 0:64], 'whl4M'
                else:
                    src_hi, src_lo, lh, nm = whl3[64:65, 0, :], whl3[64:65, 1, :], ones_bf[64:65, 0:64], 'whl3M'
                mm(PS[r][0:64, :], lh, src_hi, True, False, ['ones_bf', nm], [psn[r]])
                mm(PS[r][0:64, :], lh, src_lo, False, True, ['ones_bf', nm], [psn[r]])
                P.op('dve', lambda e, r=r, h=h: e.tensor_tensor(out=ymb[:, s, h, :], in0=PS[r][0:64, :], in1=osb[0:64, h, :], op=ALU.mult),
                     [psn[r], f'osbM{h}'], [f'ymb{s}'])
            P.dma('pool', lambda e: e.dma_start(out=yT_d[4:6, :, tsl].rearrange("c (hh p) s -> p (c hh) s", hh=2), in_=ymb[:, s]), f'nyo{s}',
                  reads=[f'ymb{s}'], writes=['yT_d'])

        load_q(0)
        for i in range(NT):
            qtile(i)
        P.barrier()
        es.close()

    def rmsnorm_tile(src3, srcn, gvec, gn, sqb, rstd, ones1024, dst3, dstn, psb, dst_f32=False):
        P.op('act', lambda e: e.activation(out=sqb[:].rearrange("p c s -> p (c s)"), in_=src3.rearrange("p c s -> p (c s)"), func=AF.Square),
             [srcn], ['sqbC'])
        for c in range(8):
            mm(PS[psb][:], ones1024[:], sqb[:, c, :], c == 0, c == 7, ['ones1024C', 'sqbC'], [psn[psb]])
        P.op('act', lambda e: e.activation(out=rstd[:], in_=PS[psb][:], func=AF.Sqrt, bias=EPS, scale=1.0), [psn[psb]], ['rstdC'])
        P.op('dve', lambda e: e.reciprocal(out=rstd[:], in_=rstd[:]), ['rstdC'], ['rstdC'])
        for c in range(8):
            P.op('dve', lambda e, c=c: e.scalar_tensor_tensor(out=dst3[:, c, :], in0=src3[:, c, :], scalar=gvec[:, c:c + 1],
                                                               in1=rstd[:], op0=ALU.mult, op1=ALU.mult), [srcn, gn, 'rstdC'], [dstn])

    def phaseC1(l):
        es = contextlib.ExitStack()
        wo = mk(es, "wo", [128, 8, 1024], BF16)
        gm = mk(es, "gmlp", [128, 8], F32)
        ones1024 = mk(es, "ones1024C", [128, 128], BF16)
        hin = mk(es, "hinC", [128, 3, 8, 512], F32)
        yin = mk(es, "yinC", [128, 2, 8, 512], BF16)
        sqb = mk(es, "sqbC", [128, 8, 512], BF16)
        rstd = mk(es, "rstdC", [128, 512], F32)
        xo = mk(es, "xoC", [128, 2, 8, 512], BF16)
        for jb in range(4):
            P.dma('pool', lambda e, jb=jb: e.dma_start(out=wo[:, :, jb * 256:(jb + 1) * 256],
                                                       in_=w_out_d[l, :, jb * 256:(jb + 1) * 256].rearrange("(c p) n -> p c n", p=128)),
                  f'wl{jb % 2}', writes=[f'wo{jb}'])
        P.dma('sp', lambda e: e.dma_start(out=gm[:], in_=g_mlp_d[l]), 'c0', writes=['gmlp'])
        P.op('dve', lambda e: e.memset(ones1024[:], 1.0 / 1024), writes=['ones1024C'])

        def load(t):
            s3 = t % 3
            s = t % 2
            tsl = slice(t * 512, (t + 1) * 512)
            P.dma('sp', lambda e: e.dma_start(out=hin[:, s3], in_=hT_d[:, :, tsl].rearrange("c p s -> p c s")), f'hin{s3}',
                  reads=['hT_d'], writes=[f'hinC{s3}'])
            P.dma('sp', lambda e: e.dma_start(out=yin[:, s], in_=yT_d[:, :, tsl].rearrange("c p s -> p c s")), f'yin{s}',
                  reads=['yT_d'], writes=[f'yinC{s}'])

        def mmC(t):
            s3 = t % 3
            s = t % 2
            for m in range(8):
                pi = m % 4
                for k in range(8):
                    mm(PS[pi][:], wo[:, k, m * 128:(m + 1) * 128], yin[:, s, k, :], k == 0, k == 7, [f'wo{m // 2}', f'yinC{s}'], [psn[pi]])
                P.op('dve', lambda e, m=m, pi=pi: e.tensor_tensor(out=hin[:, s3, m, :], in0=PS[pi][:], in1=hin[:, s3, m, :], op=ALU.add),
                     [psn[pi], f'hinC{s3}'], [f'hinC{s3}'])

        def normC(t):
            s3 = t % 3
            s = t % 2
            tsl = slice(t * 512, (t + 1) * 512)
            P.dma('pool', lambda e: e.dma_start(out=hT_d[:, :, tsl].rearrange("c p s -> p c s"), in_=hin[:, s3]), f'hsto{s}',
                  reads=[f'hinC{s3}'], writes=['hT_d2'])
            rmsnorm_tile(hin[:, s3], f'hinC{s3}', gm, 'gmlp', sqb, rstd, ones1024, xo[:, s], f'xoC{s}', 4)
            P.dma('pool', lambda e: e.dma_start(out=xn2_d[:, :, tsl].rearrange("c p s -> p c s"), in_=xo[:, s]), f'xsto{s}',
                  reads=[f'xoC{s}'], writes=['xn2_d'])

        load(0)
        if NT > 1:
            load(1)
        mmC(0)
        for t in range(NT):
            if t + 2 < NT:
                load(t + 2)
            if t + 1 < NT:
                mmC(t + 1)
            normC(t)
        P.barrier()
        es.close()

    def phaseC2(l, half):
        es = contextlib.ExitStack()
        w1s = mk(es, "w1s", [128, 8, 2048], BF16)
        w2s = mk(es, "w2s", [128, 16, 1024], BF16)
        hin = mk(es, "hinD", [128, 2, 8, 512], F32)
        xin = mk(es, "xinD", [128, 2, 8, 512], BF16)
        rl = mk(es, "rlD", [128, 2, 512], F32)
        act = mk(es, "actD", [128, 16, 512], BF16)
        f0 = half * 2048
        for jb in range(8):
            P.dma('pool', lambda e, jb=jb: e.dma_start(
                out=w1s[:, :, jb * 256:(jb + 1) * 256],
                in_=mlp_w1_d[l, :, f0 + jb * 256:f0 + (jb + 1) * 256].rearrange("(c p) n -> p c n", p=128)), f'wl{jb % 2}',
                writes=[f'w1s{jb}'])
        for c in range(16):
            P.dma('pool', lambda e, c=c: e.dma_start(out=w2s[:, c, :], in_=mlp_w2_d[l, f0 + c * 128:f0 + (c + 1) * 128, :]), f'wl{c % 2}',
                  writes=[f'w2s{c}'])

        def load(t):
            s = t % 2
            tsl = slice(t * 512, (t + 1) * 512)
            P.dma('sp', lambda e: e.dma_start(out=hin[:, s], in_=hT_d[:, :, tsl].rearrange("c p s -> p c s")), f'hin{s}',
                  reads=['hT_d'], writes=[f'hinD{s}'])
            P.dma('sp', lambda e: e.dma_start(out=xin[:, s], in_=xn2_d[:, :, tsl].rearrange("c p s -> p c s")), f'yin{s}',
                  reads=['xn2_d'], writes=[f'xinD{s}'])

        def tile(t):
            s = t % 2
            tsl = slice(t * 512, (t + 1) * 512)
            if t + 1 < NT:
                load(t + 1)
            for f in range(16):
                pi = f % 4
                r = f % 2
                for k in range(8):
                    mm(PS[pi][:], w1s[:, k, f * 128:(f + 1) * 128], xin[:, s, k, :], k == 0, k == 7, [f'w1s{f // 2}', f'xinD{s}'], [psn[pi]])
                P.op('act', lambda e, pi=pi, r=r: e.activation(out=rl[:, r, :], in_=PS[pi][:], func=AF.Relu), [psn[pi]], [f'rlD{r}'])
                P.op('dve', lambda e, pi=pi, r=r, f=f: e.scalar_tensor_tensor(out=act[:, f, :], in0=PS[pi][:], scalar=0.0, in1=rl[:, r, :],
                                                                              op0=ALU.max, op1=ALU.mult), [psn[pi], f'rlD{r}'], ['actD'])
            for m in range(8):
                pi = 4 + (m % 4)
                for f in range(16):
                    mm(PS[pi][:], w2s[:, f, m * 128:(m + 1) * 128], act[:, f, :], f == 0, f == 15, [f'w2s{f}', 'actD'], [psn[pi]])
                P.op('dve', lambda e, m=m, pi=pi: e.tensor_tensor(out=hin[:, s, m, :], in0=PS[pi][:], in1=hin[:, s, m, :], op=ALU.add),
                     [psn[pi], f'hinD{s}'], [f'hinD{s}'])
            P.dma('pool', lambda e: e.dma_start(out=hT_d[:, :, tsl].rearrange("c p s -> p c s"), in_=hin[:, s]), f'hsto{s}',
                  reads=[f'hinD{s}'], writes=['hT_d2'])

        load(0)
        for t in range(NT):
            tile(t)
        P.barrier()
        es.close()

    def phaseF():
        es = contextlib.ExitStack()
        gf = mk(es, "gfin", [128, 8], F32)
        ones1024 = mk(es, "ones1024F", [128, 128], BF16)
        hin = mk(es, "hinF", [128, 2, 8, 512], F32)
        sqb = mk(es, "sqbF", [128, 8, 512], BF16)
        rstd = mk(es, "rstdF", [128, 512], F32)
        xo = mk(es, "xoF", [128, 8, 512], F32)
        ot = mk(es, "otF", [128, 2, 4, 1024], F32)
        P.dma('sp', lambda e: e.dma_start(out=gf[:], in_=g_fin_d), 'c0', writes=['gfin'])
        P.op('dve', lambda e: e.memset(ones1024[:], 1.0 / 1024), writes=['ones1024C'])

        def load(t):
            s = t % 2
            P.dma('sp', lambda e: e.dma_start(out=hin[:, s], in_=hT_d[:, :, t * 512:(t + 1) * 512].rearrange("c p s -> p c s")), f'hin{s}',
                  reads=['hT_d'], writes=[f'hinF{s}'])

        def tile(t):
            s = t % 2
            if t + 1 < NT:
                load(t + 1)
            rmsnorm_tile(hin[:, s], f'hinF{s}', gf, 'gfin', sqb, rstd, ones1024, xo, 'xoF', 0)
            for sub in range(4):
                for c in range(8):
                    pi = 1 + (c // 4) + 2 * (sub % 2)
                    P.op('pe', lambda e, pi=pi, sub=sub, c=c: e.transpose(out=PS[pi][:, (c % 4) * 128:(c % 4 + 1) * 128],
                                                                          in_=xo[:, c, sub * 128:(sub + 1) * 128], identity=ident[:]),
                         ['xoF', 'ident'], [psn[pi]])
                    if c % 4 == 3:
                        if c == 3:
                            P.op('act', lambda e, pi=pi, sub=sub: e.activation(out=ot[:, s, sub, 0:512], in_=PS[pi][:], func=AF.Copy),
                                 [psn[pi]], [f'otF{s}'])
                        else:
                            P.op('dve', lambda e, pi=pi, sub=sub: e.tensor_copy(out=ot[:, s, sub, 512:1024], in_=PS[pi][:]),
                                 [psn[pi]], [f'otF{s}'])
            P.dma('pool', lambda e: e.dma_start(out=out_d[t * 512:(t + 1) * 512, :].rearrange("(k p) f -> p k f", p=128), in_=ot[:, s]), f'osto{s}',
                  reads=[f'otF{s}'], writes=['out_d'])

        load(0)
        for t in range(NT):
            tile(t)
        P.barrier()
        es.close()


    def phaseN0(l):
        es = contextlib.ExitStack()
        kvcT = mk(es, "kvcT", [128, S], BF16)
        posT = mk(es, "posT", [128, 32], BF16)
        w1 = mk(es, "w1", [128, 32, 128], BF16)
        w2 = mk(es, "w2", [128, 192], BF16)
        hid = mk(es, "hid", [128, 2, 512], BF16)
        cvs = mk(es, "cvs", [128, 2], F32)
        kcs = mk(es, "kcs", [128, 512], BF16)
        vcs = mk(es, "vcs", [128, 4, 65], BF16)
        P.dma('sp', lambda e: e.dma_start(out=kvcT[:], in_=kvc_d), 'n0', writes=['kvcT'])
        P.dma('pool', lambda e: e.dma_start(out=posT[:], in_=cmp_posT_d[l]), 'wl1', writes=['posT'])
        P.dma('pool', lambda e: e.dma_start(out=w1[:, 0:16, :], in_=cmp_w1_d[l, :, 0:16, :]), 'wl0', writes=['w1'])
        P.dma('pool', lambda e: e.dma_start(out=w1[:, 16:32, :], in_=cmp_w1_d[l, :, 16:32, :]), 'wl1', writes=['w1'])
        P.dma('pool', lambda e: e.dma_start(out=w2[:], in_=cmp_w2_d[l]), 'wl0', writes=['w2'])
        P.op('dve', lambda e: e.memset(kcs[:], 0.0), writes=['kcs'])
        P.op('dve', lambda e: e.memset(vcs[:], 0.0), writes=['vcs'])
        kv3 = kvcT[:].rearrange("p (c s) -> p c s", s=16)
        for i in range(2):
            pr = slice(64 * i, 64 * i + 64)
            for li in range(32):
                mm(PS[1 + i][:, 0:NCMP], w1[pr, li, :], kv3[pr, li // 16:li // 16 + NCMP, li % 16], li == 0, li == 31,
                   ['w1', 'kvcT'], [psn[1 + i]])
            for li in range(32):
                mm(PS[3 + i][:, 0:1], w1[pr, li, :], posT[pr, li:li + 1], li == 0, li == 31, ['w1', 'posT'], [psn[3 + i]])
            P.op('dve', lambda e, i=i: e.tensor_copy(out=cvs[:, i:i + 1], in_=PS[3 + i][:, 0:1]), [psn[3 + i]], ['cvs'])
            P.op('act', lambda e, i=i: e.activation(out=hid[:, i, 0:NCMP], in_=PS[1 + i][:, 0:NCMP], func=AF.Silu,
                                                    bias=cvs[:, i:i + 1], scale=1.0), [psn[1 + i], 'cvs'], ['hid'])
        mm(PS[5][:, 0:NCMP], w2[:, 0:128], hid[:, 0, 0:NCMP], True, True, ['w2', 'hid'], [psn[5]])
        P.op('dve', lambda e: e.tensor_copy(out=kcs[:, 0:NCMP], in_=PS[5][:, 0:NCMP]), [psn[5]], ['kcs'])
        for cc in range(NCC):
            n = min(128, NCMP - cc * 128)
            mm(PS[6][0:n, cc * 64:(cc + 1) * 64], hid[:, 1, cc * 128:cc * 128 + n], w2[:, 128:192], True, True, ['w2', 'hid'], [psn[6]])
            P.op('dve', lambda e, cc=cc, n=n: e.tensor_copy(out=vcs[0:n, cc, 0:64], in_=PS[6][0:n, cc * 64:(cc + 1) * 64]), [psn[6]], ['vcs'])
            P.op('dve', lambda e, cc=cc, n=n: e.memset(vcs[0:n, cc, 64:65], 1.0), [], ['vcs'])
        P.dma('pool', lambda e: e.dma_start(out=kc_d, in_=kcs[:]), 'kco', reads=['kcs'], writes=['kc_d'])
        P.dma('pool', lambda e: e.dma_start(out=vc_d, in_=vcs[:]), 'vco', reads=['vcs'], writes=['vc_d'])
        P.barrier()
        es.close()

    def phaseM0(l):
        es = contextlib.ExitStack()
        mlag = mk(es, "mlag", [128, 3], F32)
        wuq = mk(es, "wuq", [128, 2, 512], BF16)
        wukv = mk(es, "wukv", [128, 512], BF16)
        ones256 = mk(es, "ones256M", [128, 128], BF16)
        ones128 = mk(es, "ones128M", [128, 128], BF16)
        cqin = mk(es, "cqin", [128, 2, 3, 512], F32)
        krin = mk(es, "krin", [32, 2, 2, 512], F32)
        ropet = mk(es, "ropet", [32, 2, 4, 512], F32)
        cqsq = mk(es, "cqsq", [128, 3, 512], BF16)
        rs2 = mk(es, "rs2", [128, 2, 2, 512], F32)
        cqn = mk(es, "cqn", [128, 2, 3, 512], BF16)
        mq = mk(es, "mq", [96, 4, 512], BF16)
        mk_ = mk(es, "mkk", [96, 4, 512], BF16)
        t1 = mk(es, "t1", [32, 3, 2, 512], F32)
        krb = mk(es, "krb", [32, 512], BF16)
        vmst = mk(es, "vmst", [128, 4, 4, 65], BF16)
        P.dma('sp', lambda e: e.dma_start(out=mlag[:], in_=mla_g_d[l]), 'c1', writes=['mlag'])
        P.dma('pool', lambda e: e.dma_start(out=wuq[:], in_=w_uq_d[l].rearrange("(c p) n -> p c n", p=128)), 'wl0', writes=['wuq'])
        P.dma('pool', lambda e: e.dma_start(out=wukv[:], in_=w_ukv_d[l]), 'wl1', writes=['wukv'])
        P.op('dve', lambda e: e.memset(ones256[:], 1.0 / 256), writes=['ones256'])
        P.op('dve', lambda e: e.memset(ones128[:], 1.0 / 128), writes=['ones128'])
        P.op('pool', lambda e: e.memset(vmst[:], 1.0), writes=['vmst'])

        def load(t):
            s = t % 2
            tsl = slice(t * 512, (t + 1) * 512)
            P.dma('sp', lambda e: e.dma_start(out=cqin[:, s], in_=cq_d[:, :, tsl].rearrange("c p s -> p c s")), f'hin{s}',
                  reads=['cq_d'], writes=[f'cqin{s}'])
            P.dma('sp', lambda e: e.dma_start(out=krin[:, s], in_=krr_d[:, tsl].rearrange("(a p) s -> p a s", a=2)), f'yin{s}',
                  reads=['krr_d'], writes=[f'krin{s}'])
            P.dma('sp', lambda e: e.dma_start(out=ropet[:, s], in_=cd['rope'][:, :, tsl].rearrange("a p s -> p a s")),
                  f'rope{s}', writes=[f'rope{s}'])

        def normM(t):
            s = t % 2
            cn = f'cqin{s}'
            P.op('act', lambda e: e.activation(out=cqsq[:].rearrange("p c s -> p (c s)"), in_=cqin[:, s].rearrange("p c s -> p (c s)"),
                                               func=AF.Square), [cn], ['cqsq'])
            for j in range(2):
                mm(PS[7][:], ones256[:], cqsq[:, j, :], j == 0, j == 1, ['ones256', 'cqsq'], [psn[7]])
            mm(PS[0][:], ones128[:], cqsq[:, 2, :], True, True, ['ones128', 'cqsq'], [psn[0]])
            for j, pi in ((0, 7), (1, 0)):
                P.op('act', lambda e, j=j, pi=pi: e.activation(out=rs2[:, s, j, :], in_=PS[pi][:], func=AF.Sqrt, bias=EPS, scale=1.0),
                     [psn[pi]], [f'rs2{s}'])
            P.op('dve', lambda e: e.reciprocal(out=rs2[:, s], in_=rs2[:, s]), [f'rs2{s}'], [f'rs2{s}'])
            for j in range(3):
                P.op('dve', lambda e, j=j: e.scalar_tensor_tensor(out=cqn[:, s, j, :], in0=cqin[:, s, j, :], scalar=mlag[:, j:j + 1],
                                                                   in1=rs2[:, s, 0 if j < 2 else 1, :], op0=ALU.mult, op1=ALU.mult),
                     [cn, 'mlag', f'rs2{s}'], [f'cqn{s}'])

        def tile(t):
            s = t % 2
            tsl = slice(t * 512, (t + 1) * 512)
            if t + 1 < NT:
                load(t + 1)
            if t + 1 < NT:
                normM(t + 1)
            for h in range(4):
                pa = 2 + (h % 2) * 2
                pb = pa + 1
                for c in range(2):
                    mm(PS[pa][0:96, :], wuq[:, c, h * 128:h * 128 + 96], cqn[:, s, c, :], c == 0, c == 1, ['wuq', f'cqn{s}'], [psn[pa]])
                for c in range(2):
                    mm(PS[pb][0:32, :], wuq[:, c, h * 128 + 96:h * 128 + 128], cqn[:, s, c, :], c == 0, c == 1, ['wuq', f'cqn{s}'], [psn[pb]])
                P.op('act', lambda e, h=h, pa=pa: e.activation(out=mq[0:64, h, :], in_=PS[pa][0:64, :], func=AF.Copy, scale=MLA_SCALE),
                     [psn[pa]], ['mq'])
                hh = h % 2
                P.op('dve', lambda e, pa=pa, hh=hh: e.tensor_tensor(out=t1[:, hh, 0, :], in0=PS[pa][64:96, :], in1=ropet[:, s, 0, :], op=ALU.mult),
                     [psn[pa], f'rope{s}'], [f't1a{hh}'])
                P.op('dve', lambda e, pb=pb, hh=hh: e.tensor_tensor(out=t1[:, hh, 1, :], in0=PS[pb][0:32, :], in1=ropet[:, s, 1, :], op=ALU.mult),
                     [psn[pb], f'rope{s}'], [f't1b{hh}'])
                P.op('pool', lambda e, h=h, hh=hh: e.tensor_tensor(out=mq[64:96, h, :], in0=t1[:, hh, 0, :], in1=t1[:, hh, 1, :], op=ALU.add),
                     [f't1a{hh}', f't1b{hh}'], ['mq'])
            P.dma('pool', lambda e: e.dma_start(out=qm_d[:, :, tsl].rearrange("h p s -> p h s"), in_=mq[:]), 'mqo',
                  reads=['mq'], writes=['qm_d'])
            for h in range(4):
                pi = 5 + (h % 2)
                mm(PS[pi][0:64, :], wukv[:, h * 64:(h + 1) * 64], cqn[:, s, 2, :], True, True, ['wukv', f'cqn{s}'], [psn[pi]])
                if h % 2 == 0:
                    P.op('act', lambda e, h=h, pi=pi: e.activation(out=mk_[0:64, h, :], in_=PS[pi][0:64, :], func=AF.Copy),
                         [psn[pi]], ['mk'])
                else:
                    P.op('dve', lambda e, h=h, pi=pi: e.tensor_copy(out=mk_[0:64, h, :], in_=PS[pi][0:64, :]), [psn[pi]], ['mk'])
            P.op('dve', lambda e: e.tensor_tensor(out=t1[:, 2, 0, :], in0=krin[:, s, 0, :], in1=ropet[:, s, 2, :], op=ALU.mult),
                 [f'krin{s}', f'rope{s}'], ['t1a2'])
            P.op('dve', lambda e: e.tensor_tensor(out=t1[:, 2, 1, :], in0=krin[:, s, 1, :], in1=ropet[:, s, 3, :], op=ALU.mult),
                 [f'krin{s}', f'rope{s}'], ['t1b2'])
            P.op('pool', lambda e: e.tensor_tensor(out=krb[:], in0=t1[:, 2, 0, :], in1=t1[:, 2, 1, :], op=ALU.add), ['t1a2', 't1b2'], ['krb'])
            P.dma('pool', lambda e: e.dma_start(out=km_d[:, 0:64, tsl].rearrange("h p s -> p h s"), in_=mk_[0:64]), 'mko',
                  reads=['mk'], writes=['km_d'])
            for h in range(4):
                P.dma('pool', lambda e, h=h: e.dma_start(out=km_d[h, 64:96, tsl], in_=krb[:]), f'krbo{h % 2}', reads=['krb'], writes=['km_d'])
            for sub in range(4):
                pi = 1 + (sub // 2)
                mm(PS[pi][:, (sub % 2) * 256:(sub % 2) * 256 + 256], cqn[:, s, 2, sub * 128:(sub + 1) * 128], wukv[:, 256:512],
                   True, True, ['wukv', f'cqn{s}'], [psn[pi]])
            P.op('dve', lambda e: e.tensor_copy(out=vmst[:, 0:2, :, 0:64], in_=PS[1][:].rearrange("p (a b c) -> p a b c", a=2, b=4)),
                 [psn[1]], ['vmst'])
            P.op('act', lambda e: e.activation(out=vmst[:, 2:4, :, 0:64], in_=PS[2][:].rearrange("p (a b c) -> p a b c", a=2, b=4), func=AF.Copy),
                 [psn[2]], ['vmst'])
            P.dma('pool', lambda e: e.dma_start(out=vm_d[t * 4:(t + 1) * 4].rearrange("k p c -> p k c"),
                                                in_=vmst[:].rearrange("p a b c -> p a (b c)")), 'vmo',
                  reads=['vmst'], writes=['vm_d'])

        load(0)
        normM(0)
        for t in range(NT):
            tile(t)
        P.barrier()
        es.close()

    setup_tables()
    phase0()
    for l in range(depth):
        phaseA(l)
        if stop_after == 'A':
            break
        phaseN0(l)
        phaseM0(l)
        if stop_after == 'M0':
            break
        phaseNSA(l)
        if stop_after == 'NSA':
            break
        phaseMLA(l)
        if stop_after == 'MLA':
            break
        phaseC1(l)
        phaseC2(l, 0)
        phaseC2(l, 1)
        if stop_after == 'L0':
            break
    if stop_after is None:
        phaseF()

    P.barrier()
    P.finish()
    top.close()
    return nc, consts


_CACHE = {}


def _get_program(S):
    if S not in _CACHE:
        _CACHE[S] = build_program(S)
    return _CACHE[S]


def make_in_maps(inputs, S, consts, B):
    w = layout_weights(inputs)
    maps = []
    for b in range(B):
        m = {"x": np.ascontiguousarray(inputs['x'][b])}
        m.update(w)
        for k, v in consts.items():
            m["c_" + k] = np.ascontiguousarray(v)
        maps.append(m)
    return maps


def kernel(**inputs):
    inputs = {k: np.asarray(v, dtype=np.float32) for k, v in inputs.items()}
    B, S, _ = inputs['x'].shape
    nc, consts = _get_program(S)
    maps = make_in_maps(inputs, S, consts, B)
    res = run_bass_kernel_spmd(nc, maps, core_ids=list(range(B)))
    out = np.stack([np.asarray(r["out"]) for r in res.results], 0)
    return out.astype(np.float32)
```

```python
import math
import contextlib
import numpy as np
import concourse.bass as bass
import concourse.mybir as mybir
from concourse.bass_utils import run_bass_kernel_spmd

F32 = mybir.dt.float32
BF16 = mybir.dt.bfloat16
ALU = mybir.AluOpType
AF = mybir.ActivationFunctionType

ENGS = ['pe', 'act', 'dve', 'pool', 'sp']
EPOCH = 30000
_MARKS = []

D_MODEL = 1024
DEPTH = 2
GROUP_W = 256
CONV_K = 31
NSA_HD = 64
CMP_LEN = 32
CMP_STRIDE = 16
SLC_LEN = 64
N_SELECT = 16
WINDOW = 512
Q_LORA = 256
KV_LORA = 128
D_FF = 4096
EPS = 1e-6
MLA_SCALE = 96 ** -0.5
NSA_SCALE = 0.125
OFF = 2064
EVL = 4640

C_CONV = 0
C_Q = 512
C_KVC = 768
C_KS = 896
C_KW = 1024
C_GATE = 1152
C_CQ = 1164
C_CKV = 1420
C_KR = 1548
C_POOL = 1612
C_V = 1868
NCOL = 1996


class Prog:
    def __init__(self, nc):
        self.nc = nc
        self.ops = {e: [] for e in ENGS}
        self.cnt = {e: 0 for e in ENGS}
        self.esem = {}
        self.epoch_idx = {e: 0 for e in ENGS}
        self.sems = {}
        self.seen = {e: {} for e in ENGS}
        self.last_w = {}
        self.readers = {}
        self.dma_cnt = {}
        self.last_tok = {}
        self._cms = []
        for e in ENGS:
            self._new_epoch(e)

    def _alloc_sem(self, key):
        cm = self.nc.semaphore("s" + "_".join(str(k) for k in key))
        h = cm.__enter__()
        self._cms.append(cm)
        self.sems[key] = h
        return h

    def _new_epoch(self, e):
        key = ('E', e, self.epoch_idx[e])
        self.epoch_idx[e] += 1
        self._alloc_sem(key)
        self.esem[e] = key
        self.cnt[e] = 0

    def _need(self, eng, tok):
        if tok is None:
            return
        key, val = tok
        if eng == 'pe' and key[0] == 'E' and key[1] == 'pe':
            return
        if self.seen[eng].get(key, 0) >= val:
            return
        self.seen[eng][key] = val
        self.ops[eng].append(('wait', key, val))

    def _deps(self, eng, reads, writes):
        toks = []
        for r in reads:
            toks.append(self.last_w.get(r))
            if r.startswith('ps'):
                for t in self.readers.get(r, ()):
                    if t[0][0] == 'E' and t[0][1] != eng:
                        toks.append(t)
        for w in writes:
            toks.append(self.last_w.get(w))
            toks.extend(self.readers.get(w, ()))
        for t in toks:
            self._need(eng, t)

    def _commit(self, tok, reads, writes):
        for w in writes:
            self.last_w[w] = tok
            self.readers[w] = []
        for r in reads:
            if r in writes:
                continue
            self.readers.setdefault(r, []).append(tok)

    def op(self, eng, fn, reads=(), writes=()):
        self._deps(eng, reads, writes)
        if self.cnt[eng] >= EPOCH:
            self._new_epoch(eng)
        self.cnt[eng] += 1
        key = self.esem[eng]
        tok = (key, self.cnt[eng])
        self.ops[eng].append(('op', fn, key, 1))
        self.last_tok[eng] = tok
        self._commit(tok, reads, writes)
        return tok

    def dma(self, eng, fn, sem, reads=(), writes=()):
        key = ('D', sem)
        if key not in self.sems:
            self._alloc_sem(key)
            self.dma_cnt[key] = 0
        self._deps(eng, reads, writes)
        if self.dma_cnt[key] > 0:
            self._need(eng, (key, self.dma_cnt[key]))
        self.dma_cnt[key] += 16
        tok = (key, self.dma_cnt[key])
        self.ops[eng].append(('op', fn, key, 16))
        self._commit(tok, reads, writes)
        return tok

    def barrier(self):
        for e in ENGS:
            for f in ENGS:
                if f != e and f in self.last_tok:
                    self._need(e, self.last_tok[f])
            for key, v in self.dma_cnt.items():
                if v > 0:
                    self._need(e, (key, v))
        self.last_w = {}
        self.readers = {}

    def finish(self):
        nc = self.nc
        emap = {'pe': 'tensor', 'act': 'scalar', 'dve': 'vector', 'pool': 'gpsimd', 'sp': 'sync'}
        with nc.Block() as block:
            for e in ENGS:
                lst = self.ops[e]
                if not lst:
                    continue

                def body(engobj, lst=lst):
                    for it in lst:
                        if it[0] == 'wait':
                            engobj.wait_ge(self.sems[it[1]], it[2])
                        else:
                            ins = it[1](engobj)
                            ins.then_inc(self.sems[it[2]], it[3])
                getattr(block, emap[e])(body)
        for cm in reversed(self._cms):
            cm.__exit__(None, None, None)
        self._cms = []


def _rel_bucket_np(n):
    n = np.maximum(n, 0)
    nf = np.maximum(n, 1).astype(np.float32)
    val = (np.log(nf / np.float32(16)) / np.float32(math.log(128 / 16)) * np.float32(16)).astype(np.float32)
    large = 16 + val.astype(np.int32)
    large = np.minimum(large, 31)
    return np.where(n < 16, n, large)


def host_constants(S):
    c = {}
    c['ident'] = np.eye(128, dtype=np.float32)
    c['antiI'] = np.ascontiguousarray(np.eye(128, dtype=np.float32)[::-1])
    dist = np.arange(EVL) - OFF
    b = _rel_bucket_np(dist)
    oh_c = np.zeros((33, EVL), np.float32)
    oh_w = np.zeros((33, EVL), np.float32)
    for i in range(EVL):
        d = dist[i]
        if d < 0:
            oh_c[32, i] = 1
            oh_w[32, i] = 1
        else:
            oh_c[b[i], i] = 1
            if d >= WINDOW:
                oh_w[32, i] = 1
            else:
                oh_w[b[i], i] = 1
    c['oh_c'] = oh_c
    c['oh_w'] = oh_w
    pos = np.arange(S, dtype=np.float32)
    inv_freq = (np.float32(10000.0) ** (-np.arange(0, 32, 2, dtype=np.float32) / np.float32(32))).astype(np.float32)
    ang = (pos[:, None] * inv_freq[None, :]).astype(np.float32)
    cos = np.cos(ang).astype(np.float32).T
    sin = np.sin(ang).astype(np.float32).T
    cos2 = np.concatenate([cos, cos], 0)
    sin2 = np.concatenate([-sin, sin], 0)
    c['rope'] = np.stack([cos2 * np.float32(MLA_SCALE), sin2 * np.float32(MLA_SCALE), cos2, sin2], 0).astype(np.float32)
    n_slc = S // SLC_LEN
    ovl = np.zeros((512, 128), np.float32)
    n_cmp = S // CMP_STRIDE - 1
    for cc in range(n_cmp):
        for j in range(min(n_slc, 128)):
            lo = max(cc * 16, j * 64)
            hi = min(cc * 16 + 32, j * 64 + 64)
            ovl[cc, j] = max(hi - lo, 0) / 32.0
    ovl[:, 0] = 1.0
    c['ovl'] = ovl.reshape(4, 128, 128).transpose(1, 0, 2).copy()
    fb = np.zeros((128, 256), np.float32)
    for q in range(128):
        hv = q // 64
        for x in range(256):
            rel = x - 127
            if rel > hv:
                fb[q, x] = -1e30
            elif rel == hv or rel == hv - 1:
                fb[q, x] = 1e9
    c['fbase'] = fb
    kp = np.zeros((64, S), np.float32)
    yy = np.arange(S)
    kp[2 * ((yy // 128) % 32) + (yy % 128) // 64, yy] = 1
    c['kpat'] = kp
    pc = np.zeros((128, 2, 2), np.float32)
    pc[0:64, 0, 0] = 0.5
    pc[64:128, 0, 1] = 0.25
    pc[0:64, 1, 0] = 0.125
    pc[64:128, 1, 1] = 1.0 / 16
    c['poolc'] = pc
    corr = np.ones((128, 2, 16), np.float32)
    for ch in range(2):
        for half in range(2):
            w = (2, 4, 8, 16)[ch * 2 + half]
            for t in range(16):
                corr[half * 64:(half + 1) * 64, ch, t] = w / min(w, t + 1)
    c['poolcorr'] = corr
    return c


def layout_weights(inp):
    w = {}
    L = DEPTH
    cols = []
    cols += list(range(0, 512))
    cols += list(range(512, 768))
    kv0 = 768
    kc = list(range(kv0, kv0 + 64)); vc = list(range(kv0 + 64, kv0 + 128))
    ks = list(range(kv0 + 128, kv0 + 192)); vs = list(range(kv0 + 192, kv0 + 256))
    kw = list(range(kv0 + 256, kv0 + 320)); vw = list(range(kv0 + 320, kv0 + 384))
    cols += kc + vc + ks + ks + kw + kw
    g0 = 1152
    cols += list(range(g0, g0 + 12))
    cq0 = 1164
    cols += list(range(cq0, cq0 + 256))
    ckv0 = 1420
    cols += list(range(ckv0, ckv0 + 128))
    kr0 = 1548
    kr = list(range(kr0, kr0 + 32))
    cols += kr + kr[16:] + kr[:16]
    p0 = 1580
    cols += list(range(p0, p0 + 256))
    cols += vs + vw
    assert len(cols) == NCOL
    w['w_in'] = np.ascontiguousarray(inp['w_in'][:, :, cols])
    w['w_out'] = np.ascontiguousarray(inp['w_out'])

    def pc(v):
        return np.ascontiguousarray(v.reshape(L, -1, 128).transpose(0, 2, 1))
    w['g_mix'] = pc(inp['ln_mix_g'])
    w['g_mlp'] = pc(inp['ln_mlp_g'])
    w['g_fin'] = np.ascontiguousarray(inp['final_norm_g'].reshape(8, 128).T)
    w['conv_dw'] = np.ascontiguousarray(inp['conv_dw'].transpose(0, 2, 1).reshape(L, 2, 128, CONV_K).transpose(0, 2, 1, 3))
    w['conv_vec'] = np.ascontiguousarray(np.stack([pc(inp['conv_dw_b']), pc(inp['conv_ln_g']), pc(inp['conv_ln_b'])], 3))
    w['conv_pw'] = np.ascontiguousarray(inp['conv_pw'])
    pos = inp['nsa_cmp_pos']
    w['cmp_posT'] = np.ascontiguousarray(pos.transpose(0, 1, 3, 2).reshape(L, 128, 32))
    w1 = inp['nsa_cmp_w1'].reshape(L, 2, 32, 64, 128)
    w['cmp_w1'] = np.ascontiguousarray(w1.transpose(0, 1, 3, 2, 4).reshape(L, 128, 32, 128))
    w2 = inp['nsa_cmp_w2']
    w['cmp_w2'] = np.ascontiguousarray(np.concatenate([w2[:, 0], w2[:, 0], w2[:, 1]], axis=2))
    w['mla_g'] = np.ascontiguousarray(np.concatenate([pc(inp['mla_q_norm_g']), pc(inp['mla_kv_norm_g'])], 2))
    uq = inp['mla_w_uq'].reshape(L, 256, 4, 96)
    uq_l = np.concatenate([uq, uq[..., 80:96], uq[..., 64:80]], axis=3)
    w['w_uq'] = np.ascontiguousarray(uq_l.reshape(L, 256, 512))
    ukv = inp['mla_w_ukv'].reshape(L, 128, 4, 128)
    w['w_ukv'] = np.ascontiguousarray(np.concatenate([ukv[..., 0:64].reshape(L, 128, 256), ukv[..., 64:128].reshape(L, 128, 256)], axis=2))
    pw = inp['pool_w']
    bd = np.zeros((L, 2, 128, 128), np.float32)
    for ch in range(2):
        bd[:, ch, 0:64, 0:64] = pw[:, 2 * ch]
        bd[:, ch, 64:128, 64:128] = pw[:, 2 * ch + 1]
    w['pool_w'] = np.ascontiguousarray(bd.transpose(0, 2, 1, 3))
    w['pool_scale'] = pc(inp['pool_scale'])
    w['mlp_w1'] = np.ascontiguousarray(inp['mlp_w1'])
    w['mlp_w2'] = np.ascontiguousarray(inp['mlp_w2'])
    w['rel_table'] = np.ascontiguousarray(inp['rel_bias_table'])
    return w


def build_program(S, depth=DEPTH, debug=False, stop_after=None):
    nc = bass.Bass("TRN2", target_bir_lowering=False)
    NT = S // 512
    NKT = S // 128
    NCMP = S // 16 - 1
    NCC = (NCMP + 127) // 128
    P = Prog(nc)
    consts = host_constants(S)

    def din(name, shape, dt=F32):
        return nc.dram_tensor(name, list(shape), dt, kind="ExternalInput").ap()

    dbg_kind = "ExternalOutput" if debug else "Internal"

    def dscr(name, shape, dt):
        return nc.dram_tensor(name, list(shape), dt, kind=dbg_kind).ap()

    x_d = din("x", [8, 128, S])
    w_in_d = din("w_in", [depth, 1024, NCOL])
    w_out_d = din("w_out", [depth, 1024, 1024])
    g_mix_d = din("g_mix", [depth, 128, 8])
    g_mlp_d = din("g_mlp", [depth, 128, 8])
    g_fin_d = din("g_fin", [128, 8])
    conv_dw_d = din("conv_dw", [depth, 128, 2, CONV_K])
    conv_vec_d = din("conv_vec", [depth, 128, 2, 3])
    conv_pw_d = din("conv_pw", [depth, 256, 256])
    cmp_posT_d = din("cmp_posT", [depth, 128, 32])
    cmp_w1_d = din("cmp_w1", [depth, 128, 32, 128])
    cmp_w2_d = din("cmp_w2", [depth, 128, 192])
    mla_g_d = din("mla_g", [depth, 128, 3])
    w_uq_d = din("w_uq", [depth, 256, 512])
    w_ukv_d = din("w_ukv", [depth, 128, 512])
    pool_w_d = din("pool_w", [depth, 128, 2, 128])
    pool_scale_d = din("pool_scale", [depth, 128, 2])
    mlp_w1_d = din("mlp_w1", [depth, 1024, 4096])
    mlp_w2_d = din("mlp_w2", [depth, 4096, 1024])
    rel_table_d = din("rel_table", [32, 4])
    cd = {k: din("c_" + k, v.shape) for k, v in consts.items()}
    out_d = nc.dram_tensor("out", [8, 128, S], F32, kind="ExternalOutput").ap()

    hT_d = dscr("hT", [8, 128, S], F32)
    yT_d = dscr("yT", [8, 128, S], BF16)
    xn2_d = dscr("xn2", [8, 128, S], BF16)
    qn_d = dscr("qn", [2, 128, S], BF16)
    ks_d = dscr("ksT", [128, S], BF16)
    kw_d = dscr("kwT", [128, S], BF16)
    vtok_d = dscr("vtok", [NKT, 128, 130], BF16)
    gates_d = dscr("gatesT", [12, S], F32)
    qm_d = dscr("qm", [4, 96, S], BF16)
    km_d = dscr("km", [4, 96, S], BF16)
    vm_d = dscr("vm", [NKT, 128, 260], BF16)
    kvc_d = dscr("kvcT", [128, S], BF16)
    cq_d = dscr("cqT", [3, 128, S], F32)
    krr_d = dscr("krr", [64, S], F32)
    kc_d = dscr("kcT", [128, 512], BF16)
    vc_d = dscr("vcA", [128, 4, 65], BF16)
    evc_d = dscr("evc", [5, EVL], F32)
    evw_d = dscr("evw", [4, EVL], F32)
    gw_d = dscr("gw", [4, 128, 1408], BF16)
    gs_d = dscr("gs", [5, 128, 1024], BF16)
    gc_d = dscr("gc", [4, 5, 128, 512], BF16)

    top = contextlib.ExitStack()

    uid = [0]

    def mk(es, name, shape, dt):
        uid[0] += 1
        return es.enter_context(nc.sbuf_tensor(f"{name}_u{uid[0]}", list(shape), dt))

    psall = top.enter_context(nc.psum_tensor("psall", [128, 4096], F32))
    PS = [psall[:, i * 512:(i + 1) * 512] for i in range(8)]
    psn = [f"ps{i}" for i in range(8)]

    ident = mk(top, "ident", [128, 128], F32)
    antiI = mk(top, "antiI", [128, 128], F32)
    ones_bf = mk(top, "ones_bf", [128, 128], BF16)
    ones_f = mk(top, "ones_f", [128, 128], F32)
    b31 = mk(top, "b31", [128, 4], F32)
    P.dma('sp', lambda e: e.dma_start(out=ident[:], in_=cd['ident']), 'c0', writes=['ident'])
    P.dma('sp', lambda e: e.dma_start(out=antiI[:], in_=cd['antiI']), 'c1', writes=['antiI'])
    P.op('dve', lambda e: e.memset(ones_bf[:], 1.0), writes=['ones_bf'])
    P.op('dve', lambda e: e.memset(ones_f[:], 1.0), writes=['ones_f'])
    P.dma('sp', lambda e: e.dma_start(out=b31[:], in_=bass.AP(rel_table_d.tensor, 31 * 4, [[0, 128], [1, 4]])), 'c0', writes=['b31'])

    def mm(out, lhsT, rhs, start, stop, reads, writes):
        return P.op('pe', lambda e: e.matmul(out, lhsT=lhsT, rhs=rhs, start=start, stop=stop), reads, writes)

    def setup_tables():
        es = contextlib.ExitStack()
        tab = mk(es, "tab", [33, 5], F32)
        ohc = mk(es, "ohc", [33, EVL], F32)
        ohw = mk(es, "ohw", [33, EVL], F32)
        ev = mk(es, "ev", [5, 2, EVL], F32)
        hk = mk(es, "hk", [128, 2, 512], F32)
        gst = mk(es, "gst", [128, 2, 512], BF16)
        P.op('dve', lambda e: e.memset(tab[:], 0.0), writes=['tab'])
        P.dma('sp', lambda e: e.dma_start(out=tab[0:32, 0:4], in_=rel_table_d), 'c0', reads=[], writes=['tab'])
        P.op('dve', lambda e: e.memset(tab[32:33, :], -30000.0), writes=['tab'])
        P.dma('sp', lambda e: e.dma_start(out=ohc[:], in_=cd['oh_c']), 'c1', writes=['ohc'])
        P.dma('sp', lambda e: e.dma_start(out=ohw[:], in_=cd['oh_w']), 'c0', writes=['ohw'])
        nch = (EVL + 511) // 512
        for which, oh, ohn, nh in ((0, ohc, 'ohc', 5), (1, ohw, 'ohw', 4)):
            for ci in range(nch):
                lo = ci * 512
                hi = min(EVL, lo + 512)
                pi = ci % 2
                mm(PS[pi][0:nh, 0:hi - lo], tab[:, 0:nh], oh[:, lo:hi], True, True, ['tab', ohn], [psn[pi]])
                P.op('act', lambda e, pi=pi, lo=lo, hi=hi, nh=nh, which=which: e.activation(
                    out=ev[0:nh, which, lo:hi], in_=PS[pi][0:nh, 0:hi - lo], func=AF.Copy), [psn[pi]], ['ev'])
        P.dma('sp', lambda e: e.dma_start(out=evc_d, in_=ev[0:5, 0, :]), 'c0', reads=['ev'], writes=['evc_d'])
        P.dma('sp', lambda e: e.dma_start(out=evw_d, in_=ev[0:4, 1, :]), 'c1', reads=['ev'], writes=['evw_d'])
        jobs = []
        for h in range(4):
            for lo in range(0, 1408, 512):
                n = min(512, 1408 - lo)
                jobs.append((evw_d, 'evw_d', h * EVL + OFF - 511 + lo, 1, n, gw_d[h, :, lo:lo + n]))
        for h in range(5):
            for lo in range(0, 1024, 512):
                jobs.append((evc_d, 'evc_d', h * EVL + OFF - 511 + lo, 1, 512, gs_d[h, :, lo:lo + 512]))
        for h in range(4):
            for m in range(5):
                jobs.append((evc_d, 'evc_d', h * EVL + OFF + 512 * m - 2063, 16, 512, gc_d[h, m, :, :]))
        for ji, (src, srcn, off, pstep, n, dst) in enumerate(jobs):
            s = ji % 2
            P.dma('sp', lambda e, s=s, src=src, off=off, pstep=pstep, n=n: e.dma_start(
                out=hk[:, s, 0:n], in_=bass.AP(src.tensor, off, [[pstep, 128], [1, n]])), f'hk{s}',
                reads=[srcn], writes=[f'hk{s}'])
            pi = 2 + s
            mm(PS[pi][:, 0:n], antiI[:], hk[:, s, 0:n], True, True, ['antiI', f'hk{s}'], [psn[pi]])
            P.op('dve', lambda e, s=s, pi=pi, n=n: e.tensor_copy(out=gst[:, s, 0:n], in_=PS[pi][:, 0:n]), [psn[pi]], [f'gst{s}'])
            P.dma('pool', lambda e, s=s, n=n, dst=dst: e.dma_start(out=dst, in_=gst[:, s, 0:n]), f'gsto{s}',
                  reads=[f'gst{s}'], writes=['gtabs'])
        P.barrier()
        es.close()

    def phase0():
        es = contextlib.ExitStack()
        xin = mk(es, "xin", [128, 2, 4, 1024], F32)
        hst = mk(es, "hst", [128, 2, 8, 512], F32)
        for t in range(NT):
            s = t % 2
            P.dma('sp', lambda e, t=t, s=s: e.dma_start(
                out=xin[:, s], in_=x_d[t * 512:(t + 1) * 512, :].rearrange("(k p) f -> p k f", p=128)), f'xin{s}',
                writes=[f'xin{s}'])
            for c in range(8):
                pi = c % 4
                for sub in range(4):
                    P.op('pe', lambda e, pi=pi, sub=sub, s=s, c=c: e.transpose(
                        out=PS[pi][:, sub * 128:(sub + 1) * 128], in_=xin[:, s, sub, c * 128:(c + 1) * 128], identity=ident[:]),
                        [f'xin{s}', 'ident'], [psn[pi]])
                eng = 'act' if c % 2 == 0 else 'dve'
                if eng == 'act':
                    P.op('act', lambda e, pi=pi, s=s, c=c: e.activation(out=hst[:, s, c, :], in_=PS[pi][:], func=AF.Copy),
                         [psn[pi]], [f'hst{s}'])
                else:
                    P.op('dve', lambda e, pi=pi, s=s, c=c: e.tensor_copy(out=hst[:, s, c, :], in_=PS[pi][:]),
                         [psn[pi]], [f'hst{s}'])
            P.dma('pool', lambda e, t=t, s=s: e.dma_start(
                out=hT_d[:, :, t * 512:(t + 1) * 512].rearrange("c p s -> p c s"), in_=hst[:, s]), f'hsto{s}',
                reads=[f'hst{s}'], writes=['hT_d'])
        P.barrier()
        es.close()

    def rms_stats(src_ap_flat, nfeat_chunks, sq_tile, sqn, src_names, ps_i, rstd_tile, rstdn, inv_n):
        P.op('act', lambda e: e.activation(out=sq_tile, in_=src_ap_flat, func=AF.Square), src_names, [sqn])

    def phaseA(l):
        es = contextlib.ExitStack()
        win = mk(es, "win", [128, 8, NCOL], BF16)
        gmix = mk(es, "gmix", [128, 8], F32)
        dwt = mk(es, "dwt", [128, 2, CONV_K], F32)
        cvec = mk(es, "cvec", [128, 2, 3], F32)
        diag = mk(es, "diag", [128, 2, CONV_K, 128], BF16)
        identb = mk(es, "identb", [128, 128], BF16)
        pw = mk(es, "pw", [128, 2, 256], BF16)
        poolw = mk(es, "poolw", [128, 2, 128], BF16)
        pscale = mk(es, "pscale", [128, 2], F32)
        poolc = mk(es, "poolc", [128, 2, 2], F32)
        poolcorr = mk(es, "poolcorr", [128, 2, 16], F32)
        ones256 = mk(es, "ones256", [128, 128], BF16)
        ones1024 = mk(es, "ones1024", [128, 128], BF16)
        hin = mk(es, "hin", [128, 2, 8, 512], F32)
        sqb = mk(es, "sqb", [128, 8, 512], BF16)
        rstd = mk(es, "rstd", [128, 2, 512], F32)
        xn = mk(es, "xn", [128, 2, 8, 512], BF16)
        sig = mk(es, "sig", [128, 512], F32)
        convbuf = mk(es, "convbuf", [128, 2, 30 + 512], BF16)
        cx = mk(es, "cx", [128, 2, 512], F32)
        cxb = mk(es, "cxb", [128, 2, 2, 512], BF16)
        mean_sb = mk(es, "mean_sb", [128, 512], F32)
        tmpf = mk(es, "tmpf", [128, 2, 512], F32)
        sl = mk(es, "sl", [128, 2, 512], BF16)
        yst = mk(es, "yst", [128, 1, 4, 512], BF16)
        qst = mk(es, "qst", [128, 1, 2, 512], BF16)
        kst = mk(es, "kst", [128, 1, 3, 512], BF16)
        vst = mk(es, "vst", [128, 1, 4, 2, 65], BF16)
        gst = mk(es, "gstA", [12, 1, 512], F32)
        krst = mk(es, "krst", [64, 512], F32)
        cq = mk(es, "cq", [128, 3, 512], F32)
        ub = mk(es, "ub", [128, 2, 16 + 512], F32)
        s2 = mk(es, "s2", [128, 2, 16 + 512], F32)
        s4 = mk(es, "s4", [128, 2, 16 + 512], F32)
        s8 = mk(es, "s8", [128, 16 + 512], F32)
        s16 = mk(es, "s16", [128, 16 + 512], F32)
        dacc = mk(es, "dacc", [128, 2, 512], F32)
        db = mk(es, "db", [128, 2, 512], BF16)

        for jb in range(4):
            c0, c1 = jb * 512, min(NCOL, (jb + 1) * 512)
            P.dma('pool', lambda e, c0=c0, c1=c1: e.dma_start(out=win[:, :, c0:c1],
                                                               in_=w_in_d[l, :, c0:c1].rearrange("(c p) n -> p c n", p=128)), f'wl{jb % 2}',
                  writes=[f'win{jb}'])
        P.dma('sp', lambda e: e.dma_start(out=gmix[:], in_=g_mix_d[l]), 'c0', writes=['gmix'])
        P.dma('sp', lambda e: e.dma_start(out=dwt[:], in_=conv_dw_d[l]), 'c1', writes=['dwt'])
        P.dma('sp', lambda e: e.dma_start(out=cvec[:], in_=conv_vec_d[l]), 'c0', writes=['cvec'])
        P.dma('pool', lambda e: e.dma_start(out=pw[:], in_=conv_pw_d[l].rearrange("(c p) n -> p c n", p=128)), 'wl0', writes=['pw'])
        P.dma('pool', lambda e: e.dma_start(out=poolw[:], in_=pool_w_d[l]), 'wl0', writes=['poolw'])
        P.dma('sp', lambda e: e.dma_start(out=pscale[:], in_=pool_scale_d[l]), 'c0', writes=['pscale'])
        P.dma('sp', lambda e: e.dma_start(out=poolc[:], in_=cd['poolc']), 'c1', writes=['poolc'])
        P.dma('sp', lambda e: e.dma_start(out=poolcorr[:], in_=cd['poolcorr']), 'c0', writes=['poolcorr'])
        P.op('dve', lambda e: e.memset(ones256[:], 1.0 / 256), writes=['ones256'])
        P.op('dve', lambda e: e.memset(ones1024[:], 1.0 / 1024), writes=['ones1024'])
        P.op('dve', lambda e: e.tensor_copy(out=identb[:], in_=ident[:]), ['ident'], ['identb'])
        P.op('pool', lambda e: e.memset(convbuf[:], 0.0), writes=['convbuf'])
        P.op('pool', lambda e: e.memset(ub[:], 0.0), writes=['ub'])
        P.op('pool', lambda e: e.memset(s2[:], 0.0), writes=['s2'])
        P.op('pool', lambda e: e.memset(s4[:], 0.0), writes=['s4'])
        P.op('pool', lambda e: e.memset(s8[:], 0.0), writes=['s8'])
        P.op('pool', lambda e: e.memset(s16[:], 0.0), writes=['s16'])
        P.op('pool', lambda e: e.memset(vst[:], 1.0), writes=['vst0'])
        for ch in range(2):
            for k in range(CONV_K):
                eng = 'dve' if (k % 2 == 0) else 'pool'
                P.op(eng, lambda e, ch=ch, k=k: e.tensor_scalar(
                    out=diag[:, ch, k, :], in0=identb[:], scalar1=dwt[:, ch, k:k + 1], scalar2=None, op0=ALU.mult),
                    ['identb', 'dwt'], ['diag'])

        def load_tile(t):
            s = t % 2
            hsrc = x_d if l == 0 else hT_d
            P.dma('sp', lambda e: e.dma_start(out=hin[:, s], in_=hsrc[:, :, t * 512:(t + 1) * 512].rearrange("c p s -> p c s")),
                  f'hin{s}', reads=['hT_d'], writes=[f'hin{s}'])

        cur = [0]

        def proj(col, M, pi, extra_writes=()):
            sx = cur[0]
            wn = [f'win{b_}' for b_ in range(col // 512, (col + M - 1) // 512 + 1)]
            for c in range(8):
                mm(PS[pi][0:M, :], win[:, c, col:col + M], xn[:, sx, c, :], c == 0, c == 7, wn + [f'xn{sx}'], [psn[pi]])

        def normA1(t):
            s_ = t % 2
            P.op('act', lambda e: e.activation(out=sqb[:].rearrange("p c s -> p (c s)"), in_=hin[:, s_].rearrange("p c s -> p (c s)"),
                                               func=AF.Square), [f'hin{s_}'], ['sqb'])

        def normA2(t):
            s_ = t % 2
            hn_ = f'hin{s_}'
            for c in range(8):
                mm(PS[0][:], ones1024[:], sqb[:, c, :], c == 0, c == 7, ['ones1024', 'sqb'], [psn[0]])
            P.op('act', lambda e: e.activation(out=rstd[:, s_, :], in_=PS[0][:], func=AF.Sqrt, bias=EPS, scale=1.0), [psn[0]], [f'rstd{s_}'])
            P.op('dve', lambda e: e.reciprocal(out=rstd[:, s_, :], in_=rstd[:, s_, :]), [f'rstd{s_}'], [f'rstd{s_}'])
            for c in range(8):
                P.op('dve', lambda e, c=c: e.scalar_tensor_tensor(out=xn[:, s_, c, :], in0=hin[:, s_, c, :], scalar=gmix[:, c:c + 1],
                                                                   in1=rstd[:, s_, :], op0=ALU.mult, op1=ALU.mult),
                     [hn_, 'gmix', f'rstd{s_}'], [f'xn{s_}'])

        def tileA(t):
            s = t % 2
            tsl = slice(t * 512, (t + 1) * 512)
            if t + 1 < NT:
                load_tile(t + 1)
            hn = f'hin{s}'
            so = 0
            cur[0] = s
            if t + 1 < NT:
                normA1(t + 1)
            for ch in range(2):
                proj(C_CONV + 256 + ch * 128, 128, 1)
                P.op('act', lambda e: e.activation(out=sig[:], in_=PS[1][:], func=AF.Sigmoid), [psn[1]], ['sig'])
                proj(C_CONV + ch * 128, 128, 2)
                P.op('dve', lambda e, ch=ch: e.tensor_tensor(out=convbuf[:, ch, 30:542], in0=PS[2][:], in1=sig[:], op=ALU.mult),
                     [psn[2], 'sig'], ['convbuf'])
            for m in range(2):
                pi = 5 + m
                proj(C_Q + m * 128, 128, pi)
                P.op('act', lambda e, m=m, pi=pi: e.activation(out=qst[:, so, m, :], in_=PS[pi][:], func=AF.Copy, scale=NSA_SCALE),
                     [psn[pi]], [f'qst{so}'])
            P.dma('pool', lambda e: e.dma_start(out=qn_d[:, :, tsl].rearrange("c p s -> p c s"), in_=qst[:, so]), f'qsto{s}',
                  reads=[f'qst{so}'], writes=['qn_d'])
            for ch in range(2):
                pi = 3 + ch
                for k in range(CONV_K):
                    mm(PS[pi][:], diag[:, ch, k, :], convbuf[:, ch, k:k + 512], k == 0, k == CONV_K - 1, ['diag', 'convbuf'], [psn[pi]])
                P.op('act', lambda e, ch=ch, pi=pi: e.activation(out=cx[:, ch, :], in_=PS[pi][:], func=AF.Identity,
                                                                bias=cvec[:, ch, 0:1], scale=1.0), [psn[pi], 'cvec'], ['cx'])
                P.op('act', lambda e, ch=ch, pi=pi: e.activation(out=cxb[:, ch, 1, :], in_=PS[pi][:], func=AF.Square,
                                                                bias=cvec[:, ch, 0:1], scale=1.0), [psn[pi], 'cvec'], ['cxb'])
                P.op('pool', lambda e, ch=ch: e.tensor_copy(out=cxb[:, ch, 0, :], in_=cx[:, ch, :]), ['cx'], ['cxb'])
            if t + 1 < NT:
                normA2(t + 1)
            P.op('pool', lambda e: e.tensor_copy(out=convbuf[:, :, 0:30], in_=convbuf[:, :, 512:542]), ['convbuf'], ['convbuf'])
            proj(C_KVC, 128, 7)
            P.op('dve', lambda e: e.tensor_copy(out=kst[:, so, 2, :], in_=PS[7][:]), [psn[7]], [f'kst{so}'])
            proj(C_KS, 128, 1)
            P.op('act', lambda e: e.activation(out=kst[:, so, 0, :], in_=PS[1][:], func=AF.Copy), [psn[1]], [f'kst{so}'])
            proj(C_KW, 128, 2)
            P.op('dve', lambda e: e.tensor_copy(out=kst[:, so, 1, :], in_=PS[2][:]), [psn[2]], [f'kst{so}'])
            P.dma('pool', lambda e: e.dma_start(out=ks_d[:, tsl], in_=kst[:, so, 0, :]), 'ksto0', reads=[f'kst{so}'], writes=['ks_d'])
            P.dma('pool', lambda e: e.dma_start(out=kw_d[:, tsl], in_=kst[:, so, 1, :]), 'ksto1', reads=[f'kst{so}'], writes=['kw_d'])
            P.dma('pool', lambda e: e.dma_start(out=kvc_d[:, tsl], in_=kst[:, so, 2, :]), 'ksto2', reads=[f'kst{so}'], writes=['kvc_d'])
            for ch in range(2):
                mm(PS[1][:], ones256[:], cxb[:, ch, 0, :], ch == 0, ch == 1, ['ones256', 'cxb'], [psn[1]])
            for ch in range(2):
                mm(PS[2][:], ones256[:], cxb[:, ch, 1, :], ch == 0, ch == 1, ['ones256', 'cxb'], [psn[2]])
            P.op('act', lambda e: e.activation(out=mean_sb[:], in_=PS[1][:], func=AF.Copy), [psn[1]], ['mean_sb'])
            P.op('dve', lambda e: e.tensor_tensor(out=tmpf[:, 0, :], in0=mean_sb[:], in1=mean_sb[:], op=ALU.mult), ['mean_sb'], ['tmpf0'])
            P.op('dve', lambda e: e.tensor_tensor(out=tmpf[:, 0, :], in0=PS[2][:], in1=tmpf[:, 0, :], op=ALU.subtract), [psn[2], 'tmpf0'], ['tmpf0'])
            P.op('act', lambda e: e.activation(out=tmpf[:, 0, :], in_=tmpf[:, 0, :], func=AF.Sqrt, bias=EPS, scale=1.0), ['tmpf0'], ['tmpf0'])
            P.op('dve', lambda e: e.reciprocal(out=tmpf[:, 0, :], in_=tmpf[:, 0, :]), ['tmpf0'], ['tmpf0'])
            for ch in range(2):
                P.op('dve', lambda e, ch=ch: e.tensor_tensor(out=cx[:, ch, :], in0=cx[:, ch, :], in1=mean_sb[:], op=ALU.subtract),
                     ['cx', 'mean_sb'], ['cx'])
                P.op('dve', lambda e, ch=ch: e.tensor_tensor(out=cx[:, ch, :], in0=cx[:, ch, :], in1=tmpf[:, 0, :], op=ALU.mult),
                     ['cx', 'tmpf0'], ['cx'])
                P.op('act', lambda e, ch=ch: e.activation(out=sl[:, ch, :], in_=cx[:, ch, :], func=AF.Silu,
                                                          bias=cvec[:, ch, 2:3], scale=cvec[:, ch, 1:2]), ['cx', 'cvec'], ['sl'])
            proj(C_GATE, 12, 3)
            P.op('act', lambda e: e.activation(out=gst[:, so, :], in_=PS[3][0:12, :], func=AF.Sigmoid), [psn[3]], [f'gstA{so}'])
            P.dma('pool', lambda e: e.dma_start(out=gates_d[:, tsl], in_=gst[:, so, :]), f'gsto{s}', reads=[f'gstA{so}'], writes=['gates_d'])
            for sub in range(4):
                for c in range(8):
                    mm(PS[4][:, sub * 128:(sub + 1) * 128], xn[:, s, c, sub * 128:(sub + 1) * 128], win[:, c, C_V:C_V + 128],
                       c == 0, c == 7, ['win3', f'xn{s}'], [psn[4]])
            P.op('dve', lambda e: e.tensor_copy(out=vst[:, so, :, :, 0:64],
                                                in_=PS[4][:].rearrange("p (a b c) -> p a b c", a=4, b=2)), [psn[4]], [f'vst{so}'])
            P.dma('pool', lambda e: e.dma_start(out=vtok_d[t * 4:(t + 1) * 4].rearrange("k p c -> p k c"),
                                                in_=vst[:, so].rearrange("p a b c -> p a (b c)")), f'vsto{s}',
                  reads=[f'vst{so}'], writes=['vtok_d'])
            for j, col in enumerate((C_CQ, C_CQ + 128, C_CKV)):
                pi = (5, 6, 7)[j]
                proj(col, 128, pi)
                if j % 2 == 0:
                    P.op('act', lambda e, j=j, pi=pi: e.activation(out=cq[:, j, :], in_=PS[pi][:], func=AF.Copy), [psn[pi]], ['cq'])
                else:
                    P.op('dve', lambda e, j=j, pi=pi: e.tensor_copy(out=cq[:, j, :], in_=PS[pi][:]), [psn[pi]], ['cq'])
            P.dma('pool', lambda e: e.dma_start(out=cq_d[:, :, tsl].rearrange("c p s -> p c s"), in_=cq[:]), 'cqo', reads=['cq'], writes=['cq_d'])
            proj(C_KR, 64, 0)
            P.op('act', lambda e: e.activation(out=krst[:], in_=PS[0][0:64, :], func=AF.Copy), [psn[0]], ['krst'])
            P.dma('pool', lambda e: e.dma_start(out=krr_d[:, tsl], in_=krst[:]), 'kro', reads=['krst'], writes=['krr_d'])
            for m in range(2):
                pi = 3 + m
                for ch in range(2):
                    mm(PS[pi][:], pw[:, ch, m * 128:(m + 1) * 128], sl[:, ch, :], ch == 0, ch == 1, ['pw', 'sl'], [psn[pi]])
                P.op('dve', lambda e, m=m, pi=pi: e.tensor_copy(out=yst[:, so, m, :], in_=PS[pi][:]), [psn[pi]], [f'yst{so}'])
            for ch in range(2):
                pi = 3 + ch
                proj(C_POOL + ch * 128, 128, pi)
                P.op('act', lambda e, ch=ch, pi=pi: e.activation(out=ub[:, ch, 16:528], in_=PS[pi][:], func=AF.Copy), [psn[pi]], ['ub'])
            P.op('pool', lambda e: e.tensor_tensor(out=s2[:, :, 2:528], in0=ub[:, :, 2:528], in1=ub[:, :, 1:527], op=ALU.add), ['ub'], ['s2'])
            P.op('pool', lambda e: e.tensor_tensor(out=s4[:, :, 4:528], in0=s2[:, :, 4:528], in1=s2[:, :, 2:526], op=ALU.add), ['s2'], ['s4'])
            P.op('pool', lambda e: e.tensor_tensor(out=s8[:, 8:528], in0=s4[:, 1, 8:528], in1=s4[:, 1, 4:524], op=ALU.add), ['s4'], ['s8'])
            P.op('pool', lambda e: e.tensor_tensor(out=s16[:, 16:528], in0=s8[:, 16:528], in1=s8[:, 8:520], op=ALU.add), ['s8'], ['s16'])
            srcs = ((s2[:, 0, 16:528], s4[:, 0, 16:528]), (s8[:, 16:528], s16[:, 16:528]))
            for ch in range(2):
                P.op('dve', lambda e, ch=ch: e.scalar_tensor_tensor(out=dacc[:, ch, :], in0=srcs[ch][0], scalar=poolc[:, ch, 0:1],
                                                                     in1=ub[:, ch, 16:528], op0=ALU.mult, op1=ALU.subtract),
                     ['s2', 's8', 'ub', 'poolc'], ['dacc'])
                P.op('dve', lambda e, ch=ch: e.scalar_tensor_tensor(out=dacc[:, ch, :], in0=srcs[ch][1], scalar=poolc[:, ch, 1:2],
                                                                     in1=dacc[:, ch, :], op0=ALU.mult, op1=ALU.add),
                     ['s4', 's16', 'dacc', 'poolc'], ['dacc'])
            if t == 0:
                P.op('dve', lambda e: e.tensor_tensor(out=dacc[:, :, 0:16], in0=dacc[:, :, 0:16], in1=ub[:, :, 16:32], op=ALU.add), ['dacc', 'ub'], ['dacc'])
                P.op('dve', lambda e: e.tensor_tensor(out=dacc[:, :, 0:16], in0=dacc[:, :, 0:16], in1=poolcorr[:], op=ALU.mult), ['dacc', 'poolcorr'], ['dacc'])
                P.op('dve', lambda e: e.tensor_tensor(out=dacc[:, :, 0:16], in0=dacc[:, :, 0:16], in1=ub[:, :, 16:32], op=ALU.subtract), ['dacc', 'ub'], ['dacc'])
            P.op('pool', lambda e: e.tensor_copy(out=db[:], in_=dacc[:]), ['dacc'], ['db'])
            P.op('pool', lambda e: e.tensor_copy(out=ub[:, :, 0:16], in_=ub[:, :, 512:528]), ['ub'], ['ub'])
            for ch in range(2):
                pi = 5 + ch
                mm(PS[pi][:], poolw[:, ch, :], db[:, ch, :], True, True, ['poolw', 'db'], [psn[pi]])
                P.op('act', lambda e, ch=ch, pi=pi: e.activation(out=yst[:, so, 2 + ch, :], in_=PS[pi][:], func=AF.Copy,
                                                                scale=pscale[:, ch:ch + 1]), [psn[pi], 'pscale'], [f'yst{so}'])
            P.dma('pool', lambda e: e.dma_start(out=yT_d[0:2, :, tsl].rearrange("c p s -> p c s"), in_=yst[:, so, 0:2, :]), f'ysto{s}',
                  reads=[f'yst{so}'], writes=['yT_d'])
            P.dma('pool', lambda e: e.dma_start(out=yT_d[6:8, :, tsl].rearrange("c p s -> p c s"), in_=yst[:, so, 2:4, :]), f'ysto{s}',
                  reads=[f'yst{so}'], writes=['yT_d'])
        load_tile(0)
        normA1(0)
        normA2(0)
        for t in range(NT):
            tileA(t)
        P.barrier()
        es.close()

    def attn_run(items, spairs, pbuf, pnames):
        groups = []
        k = 0
        while k < len(items):
            if k + 1 < len(items) and items[k]['cls'] == items[k + 1]['cls']:
                groups.append([items[k], items[k + 1]])
                k += 2
            else:
                groups.append([items[k]])
                k += 1
        ng = len(groups)

        def emit_s(g):
            bp = spairs[g % len(spairs)]
            for u, it in enumerate(groups[g]):
                bnk = bp[u]
                madds = it.get('madd') or []
                mm(PS[bnk][:, :], it['kT'], it['q'], True, len(madds) == 0, it['kn'] + it['qn'], [psn[bnk]])
                for mi, madd in enumerate(madds):
                    mm(PS[bnk][:, :], madd[0], madd[1], False, mi == len(madds) - 1, madd[2], [psn[bnk]])

        if ng == 0:
            return
        LA = len(spairs) - 1
        for g in range(min(LA, ng)):
            emit_s(g)
        for g in range(ng):
            if g + LA < ng:
                emit_s(g + LA)
            grp = groups[g]
            bp = spairs[g % len(spairs)]
            pi = g % len(pnames)
            pn = pnames[pi]
            w = 512 * len(grp)
            src = psall[:, bp[0] * 512:bp[0] * 512 + w]
            dst = pbuf[:, pi, 0:w]
            rd = [psn[bp[u]] for u in range(len(grp))]
            it0 = grp[0]
            if it0['bias'] is not None:
                P.op('act', lambda e, src=src, dst=dst, it0=it0: e.activation(out=dst, in_=src, func=AF.Exp, bias=it0['bias'], scale=1.0),
                     rd + ['b31'], [pn])
            else:
                P.op('act', lambda e, src=src, dst=dst: e.activation(out=dst, in_=src, func=AF.Exp), rd, [pn])
            for u, it in enumerate(grp):
                pt = pbuf[:, pi, u * 512:(u + 1) * 512]
                for (map_, mnames, meng) in it['mults']:
                    P.op(meng, lambda e, pt=pt, map_=map_: e.tensor_tensor(out=pt, in0=pt, in1=map_, op=ALU.mult), [pn] + mnames, [pn])
            for u, it in enumerate(grp):
                pt = pbuf[:, pi, u * 512:(u + 1) * 512]
                mm(it['o'], it['v'], pt, it['start'], it['stop'], it['vnames'] + [pn], [it['on']])
                if it.get('post') is not None:
                    it['post']()

    def phaseNSA(l):
        es = contextlib.ExitStack()
        ksX = mk(es, "ksX", [128, S], BF16)
        rq = mk(es, "rq", [128, 4, 2, 512], BF16)
        kwT = mk(es, "kwT", [128, 2, S], BF16)
        vtok = mk(es, "vtokS", [128, NKT, 130], BF16)
        kcT = mk(es, "kcTs", [128, 2, 512], BF16)
        vcA = mk(es, "vcAs", [128, 4, 65], BF16)
        gw = mk(es, "gwS", [128, 4, 1408], BF16)
        gs = mk(es, "gsS", [128, 4, 1024], BF16)
        gc = mk(es, "gcS", [128, 4, 5, 512], BF16)
        ovl = mk(es, "ovlS", [128, 4, 128], BF16)
        fbase = mk(es, "fbaseS", [128, 256], F32)
        identb = mk(es, "identbS", [128, 128], BF16)
        pbuf2 = mk(es, "pbuf2S", [128, 4, 1024], BF16)
        nsT2 = mk(es, "nsT2", [128, 512], BF16)
        qT = mk(es, "qTS", [128, 2, 2, 512], BF16)
        gat = mk(es, "gatS", [128, 2, 512], F32)
        pbuf = mk(es, "pbufS", [128, 4, 512], BF16)
        impsum = mk(es, "impsum", [128, 4, 128], F32)
        rimp = mk(es, "rimp", [128, 4], F32)
        adj = mk(es, "adj", [128, 2, 128], F32)
        m8 = mk(es, "m8", [128, 16], F32)
        thr = mk(es, "thr", [128, 1], F32)
        sel = mk(es, "sel", [128, 4, 128], F32)
        rinv = mk(es, "rinv", [128, 2, 512], F32)
        osb = mk(es, "osb", [64, 4, 512], F32)
        rs4 = mk(es, "rs4", [128, 512], F32)
        gat4 = mk(es, "gat4", [128, 512], F32)
        whl4 = mk(es, "whl4", [128, 2, 512], BF16)
        whl3 = mk(es, "whl3", [128, 2, 512], BF16)
        ynsa = mk(es, "ynsa", [64, 4, 512], F32)
        ytmp = mk(es, "ytmp", [64, 2, 512], F32)
        ynb = mk(es, "ynb", [64, 1, 4, 512], BF16)

        P.dma('sp', lambda e: e.dma_start(out=ksX[0:64, :], in_=ks_d[0:64, :]), 'n0', writes=['ksX'])
        P.op('pool', lambda e: e.memset(kwT[64:128, 0, :], 0.0), writes=['kwT'])
        P.op('pool', lambda e: e.memset(kwT[0:64, 1, :], 0.0), writes=['kwT'])
        P.dma('sp', lambda e: e.dma_start(out=kwT[0:64, 0, :], in_=kw_d[0:64, :]), 'n1', writes=['kwT'])
        P.dma('sp', lambda e: e.dma_start(out=kwT[64:128, 1, :], in_=kw_d[64:128, :]), 'n0', writes=['kwT'])
        for k0 in range(0, NKT, 8):
            P.dma('sp', lambda e, k0=k0: e.dma_start(out=vtok[:, k0:k0 + 8, :], in_=vtok_d[k0:k0 + 8].rearrange("k p c -> p k c")),
                  f'n{(k0 // 8) % 2}', writes=['vtok'])
        P.op('pool', lambda e: e.memset(kcT[64:128, 0, :], 0.0), writes=['kcT'])
        P.op('pool', lambda e: e.memset(kcT[0:64, 1, :], 0.0), writes=['kcT'])
        P.dma('sp', lambda e: e.dma_start(out=kcT[0:64, 0, :], in_=kc_d[0:64, :]), 'n1', writes=['kcT'])
        P.dma('sp', lambda e: e.dma_start(out=kcT[64:128, 1, :], in_=kc_d[64:128, :]), 'n0', writes=['kcT'])
        P.dma('sp', lambda e: e.dma_start(out=vcA[:], in_=vc_d), 'n0', writes=['vcA'])
        P.dma('sp', lambda e: e.dma_start(out=gw[:], in_=gw_d.rearrange("h p x -> p h x")), 'n1', writes=['gw'])
        P.dma('sp', lambda e: e.dma_start(out=gs[:], in_=gs_d[0:4].rearrange("h p x -> p h x")), 'n0', writes=['gs'])
        for h in range(4):
            P.dma('sp', lambda e, h=h: e.dma_start(out=gc[:, h], in_=gc_d[h].rearrange("m p x -> p m x")), f'n{h % 2}', writes=['gc'])
        P.dma('pool', lambda e: e.dma_start(out=ovl[:], in_=cd['ovl']), 'wl0', writes=['ovl'])
        P.op('dve', lambda e: e.memset(rs4[:], 1.0), writes=['rs4'])
        P.op('dve', lambda e: e.memset(gat4[:], 1.0), writes=['gat4'])
        P.op('dve', lambda e: e.tensor_copy(out=identb[:], in_=ident[:]), ['ident'], ['identb'])
        P.dma('sp', lambda e: e.dma_start(out=fbase[:], in_=cd['fbase']), 'n0', writes=['fbase'])
        for c0 in range(0, S, 2048):
            P.dma('pool', lambda e, c0=c0: e.dma_start(out=ksX[64:128, c0:c0 + 2048], in_=cd['kpat'][:, c0:c0 + 2048]), f'wl{(c0 // 2048) % 2}',
                  writes=['ksX'])

        def load_q(i):
            s = i % 2
            P.dma('sp', lambda e: e.dma_start(out=qT[:, s], in_=qn_d[:, :, i * 512:(i + 1) * 512].rearrange("c p s -> p c s")),
                  f'nq{s}', writes=[f'qT{s}'])

        def evac(h, ob):
            P.op('dve', lambda e: e.tensor_copy(out=osb[0:64, h, :], in_=PS[ob][0:64, :]), [psn[ob]], [f'osb{h}'])
            P.op('dve', lambda e: e.tensor_copy(out=rs4[32 * h:32 * h + 1, :], in_=PS[ob][64:65, :]), [psn[ob]], ['rs4'])

        def finish_stage(br, first, tsl):
            for h in range(4):
                P.dma('sp', lambda e, h=h: e.dma_start(out=gat4[32 * h:32 * h + 1, :], in_=gates_d[3 * h + br:3 * h + br + 1, tsl]), f'ngat{h % 2}',
                      reads=['gates_d'], writes=['gat4'])
            P.op('dve', lambda e: e.tensor_scalar(out=rs4[:], in0=rs4[:], scalar1=1e-30, scalar2=None, op0=ALU.max), ['rs4'], ['rs4'])
            P.op('dve', lambda e: e.reciprocal(out=rs4[:], in_=rs4[:]), ['rs4'], ['rs4'])
            P.op('dve', lambda e: e.tensor_tensor(out=rs4[:], in0=rs4[:], in1=gat4[:], op=ALU.mult), ['rs4', 'gat4'], ['rs4'])
            P.op('dve', lambda e: e.tensor_copy(out=whl4[:, 0, :], in_=rs4[:]), ['rs4'], ['whl4'])
            P.op('dve', lambda e: e.tensor_tensor(out=whl4[:, 1, :], in0=rs4[:], in1=whl4[:, 0, :], op=ALU.subtract), ['rs4', 'whl4'], ['whl4'])
            P.op('dve', lambda e: e.tensor_copy(out=whl3[64:65, :, :], in_=whl4[96:97, :, :]), ['whl4'], ['whl3'])
            P.op('dve', lambda e: e.memset(rs4[:], 1.0), ['rs4', 'whl4'], ['rs4'])
            for h in range(4):
                r = h % 2
                if h < 3:
                    src_hi, src_lo, lh, nm = whl4[32 * h:32 * h + 1, 0, :], whl4[32 * h:32 * h + 1, 1, :], ones_bf[32 * h:32 * h + 1, 0:64], 'whl4'
                else:
                    src_hi, src_lo, lh, nm = whl3[64:65, 0, :], whl3[64:65, 1, :], ones_bf[64:65, 0:64], 'whl3'
                mm(PS[r][0:64, :], lh, src_hi, True, False, ['ones_bf', nm], [psn[r]])
                mm(PS[r][0:64, :], lh, src_lo, False, True, ['ones_bf', nm], [psn[r]])
                if first:
                    P.op('dve', lambda e, h=h, r=r: e.tensor_tensor(out=ynsa[:, h, :], in0=PS[r][0:64, :], in1=osb[0:64, h, :], op=ALU.mult),
                         [psn[r], f'osb{h}'], [f'ynsa{h}'])
                else:
                    P.op('dve', lambda e, h=h, r=r: e.tensor_tensor(out=ytmp[:, r, :], in0=PS[r][0:64, :], in1=osb[0:64, h, :], op=ALU.mult),
                         [psn[r], f'osb{h}'], [f'ytmp{r}'])
                    P.op('pool', lambda e, h=h, r=r: e.tensor_tensor(out=ynsa[:, h, :], in0=ynsa[:, h, :], in1=ytmp[:, r, :], op=ALU.add),
                         [f'ynsa{h}', f'ytmp{r}'], [f'ynsa{h}'])

        def qtile(i):
            s = i % 2
            tsl = slice(i * 512, (i + 1) * 512)
            if i + 1 < NT:
                load_q(i + 1)
            qn_ = [f'qT{s}']

            def qap(h):
                return qT[64 * (h % 2):64 * (h % 2) + 64, s, h // 2, :]

            _MARKS.append(('q%d_start' % i, len([1 for it in P.ops['pe'] if it[0] == 'op'])))
            ncc = min(NCC, (32 * i + 30) // 128 + 1)
            for h in range(4):
                pr = slice(64 * (h % 2), 64 * (h % 2) + 64)
                ib = 2 + (h % 2)
                items = []
                for cc in range(ncc):
                    m = i - 4 * cc
                    near = m <= 4
                    items.append(dict(kT=kcT[:, h % 2, cc * 128:(cc + 1) * 128], q=qT[:, s, h // 2, :], kn=['kcT'], qn=qn_,
                                      bias=None if near else b31[:, h:h + 1],
                                      mults=[], madd=[(identb[:], gc[:, h, m, :], ['identb', 'gc'])] if near else [],
                                      v=vcA[:, cc, :], vnames=['vcA'], o=PS[6 + h % 2][0:65, :], on=psn[6 + h % 2],
                                      start=(cc == 0), stop=(cc == ncc - 1)))
                n = len(items)
                sb = [0, 1]

                def emit_s(k):
                    it = items[k]
                    b = sb[k % 2]
                    madds = it.get('madd') or []
                    mm(PS[b][:, :], it['kT'], it['q'], True, len(madds) == 0, it['kn'] + it['qn'], [psn[b]])
                    for mi, madd in enumerate(madds):
                        mm(PS[b][:, :], madd[0], madd[1], False, mi == len(madds) - 1, madd[2], [psn[b]])
                emit_s(0)
                for k in range(n):
                    it = items[k]
                    if k + 1 < n:
                        emit_s(k + 1)
                    b = sb[k % 2]
                    pi = k % 4
                    pt = pbuf[:, pi, :]
                    pn = f'pb{pi}'
                    if it['bias'] is not None:
                        P.op('act', lambda e, b=b, pt=pt, it=it: e.activation(out=pt, in_=PS[b][:, :], func=AF.Exp, bias=it['bias'], scale=1.0),
                             [psn[b], 'b31'], [pn])
                    else:
                        P.op('act', lambda e, b=b, pt=pt: e.activation(out=pt, in_=PS[b][:, :], func=AF.Exp), [psn[b]], [pn])
                    for (map_, mnames, meng) in it['mults']:
                        P.op(meng, lambda e, pt=pt, map_=map_: e.tensor_tensor(out=pt, in0=pt, in1=map_, op=ALU.mult), [pn] + mnames, [pn])
                    mm(it['o'], it['v'], pt, it['start'], it['stop'], it['vnames'] + [pn], [it['on']])
                    for qb in range(4):
                        mm(PS[ib][:, qb * 128:(qb + 1) * 128], pbuf[:, pi, qb * 128:(qb + 1) * 128], ovl[:, k, :],
                           (k == 0 and qb == 0), (k == n - 1 and qb == 3), [pn, 'ovl'], [psn[ib]])
                ib3 = PS[ib][:].rearrange("p (a b) -> p a b", a=4)
                P.op('dve', lambda e, ib3=ib3: e.tensor_scalar(out=rimp[:], in0=ib3[:, :, 0], scalar1=1e-30, scalar2=None, op0=ALU.max),
                     [psn[ib]], ['rimp'])
                P.op('dve', lambda e: e.reciprocal(out=rimp[:], in_=rimp[:]), ['rimp'], ['rimp'])
                for qb in range(4):
                    if h == 0:
                        P.op('dve', lambda e, qb=qb, ib3=ib3: e.tensor_scalar(out=impsum[:, qb, :], in0=ib3[:, qb, :], scalar1=rimp[:, qb:qb + 1],
                                                                               scalar2=None, op0=ALU.mult), [psn[ib], 'rimp'], ['impsum'])
                    else:
                        P.op('dve', lambda e, qb=qb, ib3=ib3: e.scalar_tensor_tensor(out=impsum[:, qb, :], in0=ib3[:, qb, :], scalar=rimp[:, qb:qb + 1],
                                                                                      in1=impsum[:, qb, :], op0=ALU.mult, op1=ALU.add),
                             [psn[ib], 'rimp', 'impsum'], ['impsum'])
                evac(h, 6 + h % 2)
            finish_stage(0, True, tsl)
            _MARKS.append(('q%d_cmpdone' % i, len([1 for it in P.ops['pe'] if it[0] == 'op'])))
            for qb in range(4):
                g = 4 * i + qb
                a = qb % 2
                P.op('dve', lambda e, qb=qb, g=g, a=a: e.tensor_tensor(out=adj[:, a, :], in0=impsum[:, qb, :], in1=fbase[:, 127 - 2 * g:255 - 2 * g],
                                                                      op=ALU.add), ['impsum', 'fbase'], [f'adj{a}'])
                P.op('dve', lambda e, a=a: e.memset(adj[:, a, 0:1], 1e9), [], [f'adj{a}'])
                P.op('dve', lambda e, a=a: e.max(out=m8[:, 0:8], in_=adj[:, a, :]), [f'adj{a}'], ['m8'])
                P.op('dve', lambda e, a=a, qb=qb: e.match_replace(out=sel[:, qb, :], in_to_replace=m8[:, 0:8], in_values=adj[:, a, :], imm_value=-3e38),
                     [f'adj{a}', 'm8'], [f'sel{qb}'])
                P.op('dve', lambda e, qb=qb: e.max(out=m8[:, 8:16], in_=sel[:, qb, :]), [f'sel{qb}'], ['m8'])
                P.op('dve', lambda e: e.tensor_scalar(out=thr[:], in0=m8[:, 15:16], scalar1=-1e29, scalar2=None, op0=ALU.max), ['m8'], ['thr'])
                P.op('dve', lambda e, a=a, qb=qb: e.tensor_scalar(out=sel[:, qb, :], in0=adj[:, a, :], scalar1=thr[:, 0:1], scalar2=None, op0=ALU.is_ge),
                     [f'adj{a}', 'thr'], [f'sel{qb}'])
                P.op('pe', lambda e, qb=qb: e.transpose(out=PS[2][:, qb * 128:(qb + 1) * 128], in_=sel[:, qb, :], identity=ident[:]),
                     [f'sel{qb}', 'ident'], [psn[2]])
            for g_ in range(2):
                P.op('dve', lambda e, g_=g_: e.tensor_scalar(
                    out=rq[64:128, :, g_, :], in0=PS[2][64 * g_:64 * g_ + 64, :].unsqueeze(1).to_broadcast([64, 4, 512]),
                    scalar1=30000.0, scalar2=-30000.0, op0=ALU.mult, op1=ALU.add), [psn[2]], ['rq'])
            for h in range(4):
                srcq = qT[64 * (h % 2):64 * (h % 2) + 64, s, h // 2, :].unsqueeze(1).to_broadcast([64, 2, 512])
                if h % 2 == 0:
                    P.op('act', lambda e, h=h, srcq=srcq: e.activation(out=rq[0:64, h, :, :], in_=srcq, func=AF.Copy), qn_, ['rq'])
                else:
                    P.op('pool', lambda e, h=h, srcq=srcq: e.tensor_copy(out=rq[0:64, h, :, :], in_=srcq), qn_, ['rq'])
            _MARKS.append(('q%d_seldone' % i, len([1 for it in P.ops['pe'] if it[0] == 'op'])))
            nkt = 4 * i + 4
            items = []
            for h in range(4):
                pr = slice(64 * (h % 2), 64 * (h % 2) + 64)
                for kt in range(nkt):
                    near = kt >= 4 * i - 1
                    j = kt - 4 * i + 4
                    mults = []
                    madds = []
                    if near:
                        madds.append((identb[:], gs[:, h, 128 * (7 - j):128 * (7 - j) + 512], ['identb', 'gs']))
                    items.append(dict(kT=ksX[:, kt * 128:(kt + 1) * 128], q=rq[:, h, (2 * kt) // 64, :], kn=['ksX'], qn=['rq'],
                                      bias=None if near else b31[:, h:h + 1], mults=mults,
                                      madd=madds,
                                      v=vtok[:, kt, 0:65], vnames=['vtok'], o=PS[6 + h % 2][0:65, :], on=psn[6 + h % 2],
                                      start=(kt == 0), stop=(kt == nkt - 1), cls=(h, near),
                                      post=(lambda h=h: evac(h, 6 + h % 2)) if kt == nkt - 1 else None))
            attn_run(items, [(0, 1), (2, 3), (4, 5)], pbuf2, [f'pq{k}' for k in range(4)])
            finish_stage(1, False, tsl)
            _MARKS.append(('q%d_sbdone' % i, len([1 for it in P.ops['pe'] if it[0] == 'op'])))
            items = []
            kts = [kt for kt in range(4 * i - 4, 4 * i + 4) if kt >= 0]
            for h in range(4):
                pr = slice(64 * (h % 2), 64 * (h % 2) + 64)
                for kt in kts:
                    j = kt - 4 * i + 4
                    items.append(dict(kT=kwT[:, h % 2, kt * 128:(kt + 1) * 128], q=qT[:, s, h // 2, :], kn=['kwT'], qn=qn_, bias=None,
                                      mults=[], madd=[(identb[:], gw[:, h, 128 * (7 - j):128 * (7 - j) + 512], ['identb', 'gw'])],
                                      v=vtok[:, kt, 65:130], vnames=['vtok'], o=PS[6 + h % 2][0:65, :], on=psn[6 + h % 2],
                                      start=(kt == kts[0]), stop=(kt == kts[-1]), cls=(h,),
                                      post=(lambda h=h: evac(h, 6 + h % 2)) if kt == kts[-1] else None))
            attn_run(items, [(0, 1), (2, 3), (4, 5)], pbuf2, [f'pq{k}' for k in range(4)])
            finish_stage(2, False, tsl)
            _MARKS.append(('q%d_windone' % i, len([1 for it in P.ops['pe'] if it[0] == 'op'])))
            for h in range(4):
                P.op('pool' if h % 2 else 'act',
                     (lambda e, h=h: e.tensor_copy(out=ynb[:, 0, h, :], in_=ynsa[:, h, :])) if h % 2 else
                     (lambda e, h=h: e.activation(out=ynb[:, 0, h, :], in_=ynsa[:, h, :], func=AF.Copy)),
                     [f'ynsa{h}'], ['ynb0'])
            P.dma('pool', lambda e: e.dma_start(out=yT_d[2:4, :, tsl].rearrange("c (hh p) s -> p (c hh) s", hh=2), in_=ynb[:, 0]), f'nyo{s}',
                  reads=['ynb0'], writes=['yT_d'])

        load_q(0)
        for i in range(NT):
            qtile(i)
        P.barrier()
        es.close()

    def phaseMLA(l):
        es = contextlib.ExitStack()
        kmS = mk(es, "kmS", [96, 4, S], BF16)
        vmS = mk(es, "vmS", [128, NKT, 260], BF16)
        g0 = mk(es, "g0S", [128, 1024], BF16)
        identb = mk(es, "identbM", [128, 128], BF16)
        qmS = mk(es, "qmS", [96, 2, 4, 512], BF16)
        pbuf = mk(es, "pbufM", [128, 4, 1024], BF16)
        rinv = mk(es, "rinvM", [128, 2, 512], F32)
        osb = mk(es, "osbM", [64, 4, 512], F32)
        rs4 = mk(es, "rs4M", [128, 512], F32)
        whl4 = mk(es, "whl4M", [128, 2, 512], BF16)
        whl3 = mk(es, "whl3M", [128, 2, 512], BF16)
        ymb = mk(es, "ymb", [64, 2, 4, 512], BF16)
        for h in range(4):
            P.dma('sp', lambda e, h=h: e.dma_start(out=kmS[:, h, :], in_=km_d[h]), f'n{h % 2}', writes=['kmS'])
        for k0 in range(0, NKT, 8):
            P.dma('sp', lambda e, k0=k0: e.dma_start(out=vmS[:, k0:k0 + 8, :], in_=vm_d[k0:k0 + 8].rearrange("k p c -> p k c")),
                  f'n{(k0 // 8) % 2}', writes=['vmS'])
        P.dma('sp', lambda e: e.dma_start(out=g0[:], in_=gs_d[4]), 'n1', writes=['g0'])
        P.op('dve', lambda e: e.memset(rs4[:], 1.0), writes=['rs4M'])
        P.op('dve', lambda e: e.tensor_copy(out=identb[:], in_=ident[:]), ['ident'], ['identbM'])

        def load_q(i):
            s = i % 2
            P.dma('sp', lambda e: e.dma_start(out=qmS[:, s], in_=qm_d[:, :, i * 512:(i + 1) * 512].rearrange("h p s -> p h s")),
                  f'nq{s}', writes=[f'qm{s}'])

        def qtile(i):
            s = i % 2
            tsl = slice(i * 512, (i + 1) * 512)
            if i + 1 < NT:
                load_q(i + 1)
            def mla_evac(h):
                ob = 6 + h % 2
                P.op('dve', lambda e: e.tensor_copy(out=osb[0:64, h, :], in_=PS[ob][0:64, :]), [psn[ob]], [f'osbM{h}'])
                P.op('dve', lambda e: e.tensor_copy(out=rs4[32 * h:32 * h + 1, :], in_=PS[ob][64:65, :]), [psn[ob]], ['rs4M'])

            nkt = 4 * i + 4
            items = []
            for h in range(4):
                for kt in range(nkt):
                    j = kt - 4 * i + 4
                    diag_ = kt >= 4 * i
                    items.append(dict(kT=kmS[:, h, kt * 128:(kt + 1) * 128], q=qmS[:, s, h, :], kn=['kmS'], qn=[f'qm{s}'], bias=None,
                                      mults=[], madd=[(identb[:], g0[:, 128 * (7 - j):128 * (7 - j) + 512], ['identbM', 'g0'])] if diag_ else [],
                                      v=vmS[:, kt, h * 65:(h + 1) * 65], vnames=['vmS'], o=PS[6 + h % 2][0:65, :], on=psn[6 + h % 2],
                                      start=(kt == 0), stop=(kt == nkt - 1), cls=(h,),
                                      post=(lambda h=h: mla_evac(h)) if kt == nkt - 1 else None))
            attn_run(items, [(0, 1), (2, 3), (4, 5)], pbuf, [f'pm{k}' for k in range(4)])
            P.op('dve', lambda e: e.reciprocal(out=rs4[:], in_=rs4[:]), ['rs4M'], ['rs4M'])
            P.op('dve', lambda e: e.tensor_copy(out=whl4[:, 0, :], in_=rs4[:]), ['rs4M'], ['whl4M'])
            P.op('dve', lambda e: e.tensor_tensor(out=whl4[:, 1, :], in0=rs4[:], in1=whl4[:, 0, :], op=ALU.subtract), ['rs4M', 'whl4M'], ['whl4M'])
            P.op('dve', lambda e: e.tensor_copy(out=whl3[64:65, :, :], in_=whl4[96:97, :, :]), ['whl4M'], ['whl3M'])
            P.op('dve', lambda e: e.memset(rs4[:], 1.0), ['rs4M', 'whl4M'], ['rs4M'])
            for h in range(4):
                r = h % 2
                if h < 3:
                    src_hi, src_lo, lh, nm = whl4[32 * h:32 * h + 1, 0, :], whl4[32 * h:32 * h + 1, 1, :], ones_bf[32 * h:32 * h + 1, 0:64], 'whl4M'
                else:
                    src_hi, src_lo, lh, nm = whl3[64:65, 0, :], whl3[64:65, 1, :], ones_bf[64:65, 0:64], 'whl3M'
                mm(PS[r][0:64, :], lh, src_hi, True, False, ['ones_bf', nm], [psn[r]])
                mm(PS[r][0:64, :], lh, src_lo, False, True, ['ones_bf', nm], [psn[r]])
                P.op('dve', lambda e, r=r, h=h: e.tensor_tensor(out=ymb[:, s, h, :], in0=PS[r][0:64, :], in1=osb[0:64, h, :], op=ALU.mult),
                     [psn[r], f'osbM{h}'], [f'ymb{s}'])
            P.dma('pool', lambda e: e.dma_start(out=yT_d[4:6, :, tsl].rearrange("c (hh p) s -> p (c hh) s", hh=2), in_=ymb[:, s]), f'nyo{s}',
                  reads=[f'ymb{s}'], writes=['yT_d'])

        load_q(0)
        for i in range(NT):
            qtile(i)
        P.barrier()
        es.close()

    def rmsnorm_tile(src3, srcn, gvec, gn, sqb, rstd, ones1024, dst3, dstn, psb, dst_f32=False):
        P.op('act', lambda e: e.activation(out=sqb[:].rearrange("p c s -> p (c s)"), in_=src3.rearrange("p c s -> p (c s)"), func=AF.Square),
             [srcn], ['sqbC'])
        for c in range(8):
            mm(PS[psb][:], ones1024[:], sqb[:, c, :], c == 0, c == 7, ['ones1024C', 'sqbC'], [psn[psb]])
        P.op('act', lambda e: e.activation(out=rstd[:], in_=PS[psb][:], func=AF.Sqrt, bias=EPS, scale=1.0), [psn[psb]], ['rstdC'])
        P.op('dve', lambda e: e.reciprocal(out=rstd[:], in_=rstd[:]), ['rstdC'], ['rstdC'])
        for c in range(8):
            P.op('dve', lambda e, c=c: e.scalar_tensor_tensor(out=dst3[:, c, :], in0=src3[:, c, :], scalar=gvec[:, c:c + 1],
                                                               in1=rstd[:], op0=ALU.mult, op1=ALU.mult), [srcn, gn, 'rstdC'], [dstn])

    def phaseC1(l):
        es = contextlib.ExitStack()
        wo = mk(es, "wo", [128, 8, 1024], BF16)
        gm = mk(es, "gmlp", [128, 8], F32)
        ones1024 = mk(es, "ones1024C", [128, 128], BF16)
        hin = mk(es, "hinC", [128, 3, 8, 512], F32)
        yin = mk(es, "yinC", [128, 2, 8, 512], BF16)
        sqb = mk(es, "sqbC", [128, 8, 512], BF16)
        rstd = mk(es, "rstdC", [128, 512], F32)
        xo = mk(es, "xoC", [128, 2, 8, 512], BF16)
        for jb in range(4):
            P.dma('pool', lambda e, jb=jb: e.dma_start(out=wo[:, :, jb * 256:(jb + 1) * 256],
                                                       in_=w_out_d[l, :, jb * 256:(jb + 1) * 256].rearrange("(c p) n -> p c n", p=128)),
                  f'wl{jb % 2}', writes=[f'wo{jb}'])
        P.dma('sp', lambda e: e.dma_start(out=gm[:], in_=g_mlp_d[l]), 'c0', writes=['gmlp'])
        P.op('dve', lambda e: e.memset(ones1024[:], 1.0 / 1024), writes=['ones1024C'])

        def load(t):
            s3 = t % 3
            s = t % 2
            tsl = slice(t * 512, (t + 1) * 512)
            hsrc = x_d if l == 0 else hT_d
            P.dma('sp', lambda e: e.dma_start(out=hin[:, s3], in_=hsrc[:, :, tsl].rearrange("c p s -> p c s")), f'hin{s3}',
                  reads=['hT_d'], writes=[f'hinC{s3}'])
            P.dma('sp', lambda e: e.dma_start(out=yin[:, s], in_=yT_d[:, :, tsl].rearrange("c p s -> p c s")), f'yin{s}',
                  reads=['yT_d'], writes=[f'yinC{s}'])

        def mmC(t):
            s3 = t % 3
            s = t % 2
            for m in range(8):
                pi = m % 4
                for k in range(8):
                    mm(PS[pi][:], wo[:, k, m * 128:(m + 1) * 128], yin[:, s, k, :], k == 0, k == 7, [f'wo{m // 2}', f'yinC{s}'], [psn[pi]])
                P.op('dve', lambda e, m=m, pi=pi: e.tensor_tensor(out=hin[:, s3, m, :], in0=PS[pi][:], in1=hin[:, s3, m, :], op=ALU.add),
                     [psn[pi], f'hinC{s3}'], [f'hinC{s3}'])

        def normC(t):
            s3 = t % 3
            s = t % 2
            tsl = slice(t * 512, (t + 1) * 512)
            P.dma('pool', lambda e: e.dma_start(out=hT_d[:, :, tsl].rearrange("c p s -> p c s"), in_=hin[:, s3]), f'hsto{s}',
                  reads=[f'hinC{s3}'], writes=['hT_d2'])
            rmsnorm_tile(hin[:, s3], f'hinC{s3}', gm, 'gmlp', sqb, rstd, ones1024, xo[:, s], f'xoC{s}', 4)
            P.dma('pool', lambda e: e.dma_start(out=xn2_d[:, :, tsl].rearrange("c p s -> p c s"), in_=xo[:, s]), f'xsto{s}',
                  reads=[f'xoC{s}'], writes=['xn2_d'])

        load(0)
        if NT > 1:
            load(1)
        mmC(0)
        for t in range(NT):
            if t + 2 < NT:
                load(t + 2)
            if t + 1 < NT:
                mmC(t + 1)
            normC(t)
        P.barrier()
        es.close()

    def phaseC2(l, half):
        es = contextlib.ExitStack()
        w1s = mk(es, "w1s", [128, 8, 2048], BF16)
        w2s = mk(es, "w2s", [128, 16, 1024], BF16)
        hin = mk(es, "hinD", [128, 2, 8, 512], F32)
        xin = mk(es, "xinD", [128, 2, 8, 512], BF16)
        rl = mk(es, "rlD", [128, 2, 512], F32)
        act = mk(es, "actD", [128, 16, 512], BF16)
        f0 = half * 2048
        for jb in range(8):
            P.dma('pool', lambda e, jb=jb: e.dma_start(
                out=w1s[:, :, jb * 256:(jb + 1) * 256],
                in_=mlp_w1_d[l, :, f0 + jb * 256:f0 + (jb + 1) * 256].rearrange("(c p) n -> p c n", p=128)), f'wl{jb % 2}',
                writes=[f'w1s{jb}'])
        for c in range(16):
            P.dma('pool', lambda e, c=c: e.dma_start(out=w2s[:, c, :], in_=mlp_w2_d[l, f0 + c * 128:f0 + (c + 1) * 128, :]), f'wl{c % 2}',
                  writes=[f'w2s{c}'])

        def load(t):
            s = t % 2
            tsl = slice(t * 512, (t + 1) * 512)
            P.dma('sp', lambda e: e.dma_start(out=hin[:, s], in_=hT_d[:, :, tsl].rearrange("c p s -> p c s")), f'hin{s}',
                  reads=['hT_d'], writes=[f'hinD{s}'])
            P.dma('sp', lambda e: e.dma_start(out=xin[:, s], in_=xn2_d[:, :, tsl].rearrange("c p s -> p c s")), f'yin{s}',
                  reads=['xn2_d'], writes=[f'xinD{s}'])

        def tile(t):
            s = t % 2
            tsl = slice(t * 512, (t + 1) * 512)
            if t + 1 < NT:
                load(t + 1)
            for f in range(16):
                pi = f % 4
                r = f % 2
                for k in range(8):
                    mm(PS[pi][:], w1s[:, k, f * 128:(f + 1) * 128], xin[:, s, k, :], k == 0, k == 7, [f'w1s{f // 2}', f'xinD{s}'], [psn[pi]])
                P.op('act', lambda e, pi=pi, r=r: e.activation(out=rl[:, r, :], in_=PS[pi][:], func=AF.Relu), [psn[pi]], [f'rlD{r}'])
                P.op('dve', lambda e, pi=pi, r=r, f=f: e.scalar_tensor_tensor(out=act[:, f, :], in0=PS[pi][:], scalar=0.0, in1=rl[:, r, :],
                                                                              op0=ALU.max, op1=ALU.mult), [psn[pi], f'rlD{r}'], ['actD'])
            for m in range(8):
                pi = 4 + (m % 4)
                for f in range(16):
                    mm(PS[pi][:], w2s[:, f, m * 128:(m + 1) * 128], act[:, f, :], f == 0, f == 15, [f'w2s{f}', 'actD'], [psn[pi]])
                P.op('dve', lambda e, m=m, pi=pi: e.tensor_tensor(out=hin[:, s, m, :], in0=PS[pi][:], in1=hin[:, s, m, :], op=ALU.add),
                     [psn[pi], f'hinD{s}'], [f'hinD{s}'])
            P.dma('pool', lambda e: e.dma_start(out=hT_d[:, :, tsl].rearrange("c p s -> p c s"), in_=hin[:, s]), f'hsto{s}',
                  reads=[f'hinD{s}'], writes=['hT_d2'])

        load(0)
        for t in range(NT):
            tile(t)
        P.barrier()
        es.close()

    def phaseF():
        es = contextlib.ExitStack()
        gf = mk(es, "gfin", [128, 8], F32)
        ones1024 = mk(es, "ones1024F", [128, 128], BF16)
        hin = mk(es, "hinF", [128, 2, 8, 512], F32)
        sqb = mk(es, "sqbF", [128, 8, 512], BF16)
        rstd = mk(es, "rstdF", [128, 512], F32)
        xo = mk(es, "xoF", [128, 2, 8, 512], F32)
        P.dma('sp', lambda e: e.dma_start(out=gf[:], in_=g_fin_d), 'c0', writes=['gfin'])
        P.op('dve', lambda e: e.memset(ones1024[:], 1.0 / 1024), writes=['ones1024C'])

        def load(t):
            s = t % 2
            P.dma('sp', lambda e: e.dma_start(out=hin[:, s], in_=hT_d[:, :, t * 512:(t + 1) * 512].rearrange("c p s -> p c s")), f'hin{s}',
                  reads=['hT_d'], writes=[f'hinF{s}'])

        def tile(t):
            s = t % 2
            if t + 1 < NT:
                load(t + 1)
            rmsnorm_tile(hin[:, s], f'hinF{s}', gf, 'gfin', sqb, rstd, ones1024, xo[:, s], f'xoF{s}', 0)
            P.dma('pool', lambda e: e.dma_start(out=out_d[:, :, t * 512:(t + 1) * 512].rearrange("c p s -> p c s"), in_=xo[:, s]), f'osto{s}',
                  reads=[f'xoF{s}'], writes=['out_d'])

        load(0)
        for t in range(NT):
            tile(t)
        P.barrier()
        es.close()


    def phaseN0(l):
        es = contextlib.ExitStack()
        kvcT = mk(es, "kvcT", [128, S], BF16)
        posT = mk(es, "posT", [128, 32], BF16)
        w1 = mk(es, "w1", [128, 32, 128], BF16)
        w2 = mk(es, "w2", [128, 192], BF16)
        hid = mk(es, "hid", [128, 2, 512], BF16)
        cvs = mk(es, "cvs", [128, 2], F32)
        kcs = mk(es, "kcs", [128, 512], BF16)
        vcs = mk(es, "vcs", [128, 4, 65], BF16)
        P.dma('sp', lambda e: e.dma_start(out=kvcT[:], in_=kvc_d), 'n0', writes=['kvcT'])
        P.dma('pool', lambda e: e.dma_start(out=posT[:], in_=cmp_posT_d[l]), 'wl1', writes=['posT'])
        P.dma('pool', lambda e: e.dma_start(out=w1[:, 0:16, :], in_=cmp_w1_d[l, :, 0:16, :]), 'wl0', writes=['w1'])
        P.dma('pool', lambda e: e.dma_start(out=w1[:, 16:32, :], in_=cmp_w1_d[l, :, 16:32, :]), 'wl1', writes=['w1'])
        P.dma('pool', lambda e: e.dma_start(out=w2[:], in_=cmp_w2_d[l]), 'wl0', writes=['w2'])
        P.op('dve', lambda e: e.memset(kcs[:], 0.0), writes=['kcs'])
        P.op('dve', lambda e: e.memset(vcs[:], 0.0), writes=['vcs'])
        kv3 = kvcT[:].rearrange("p (c s) -> p c s", s=16)
        for i in range(2):
            pr = slice(64 * i, 64 * i + 64)
            for li in range(32):
                mm(PS[1 + i][:, 0:NCMP], w1[pr, li, :], kv3[pr, li // 16:li // 16 + NCMP, li % 16], li == 0, li == 31,
                   ['w1', 'kvcT'], [psn[1 + i]])
            for li in range(32):
                mm(PS[3 + i][:, 0:1], w1[pr, li, :], posT[pr, li:li + 1], li == 0, li == 31, ['w1', 'posT'], [psn[3 + i]])
            P.op('dve', lambda e, i=i: e.tensor_copy(out=cvs[:, i:i + 1], in_=PS[3 + i][:, 0:1]), [psn[3 + i]], ['cvs'])
            P.op('act', lambda e, i=i: e.activation(out=hid[:, i, 0:NCMP], in_=PS[1 + i][:, 0:NCMP], func=AF.Silu,
                                                    bias=cvs[:, i:i + 1], scale=1.0), [psn[1 + i], 'cvs'], ['hid'])
        mm(PS[5][:, 0:NCMP], w2[:, 0:128], hid[:, 0, 0:NCMP], True, True, ['w2', 'hid'], [psn[5]])
        P.op('dve', lambda e: e.tensor_copy(out=kcs[:, 0:NCMP], in_=PS[5][:, 0:NCMP]), [psn[5]], ['kcs'])
        for cc in range(NCC):
            n = min(128, NCMP - cc * 128)
            mm(PS[6][0:n, cc * 64:(cc + 1) * 64], hid[:, 1, cc * 128:cc * 128 + n], w2[:, 128:192], True, True, ['w2', 'hid'], [psn[6]])
            P.op('dve', lambda e, cc=cc, n=n: e.tensor_copy(out=vcs[0:n, cc, 0:64], in_=PS[6][0:n, cc * 64:(cc + 1) * 64]), [psn[6]], ['vcs'])
            P.op('dve', lambda e, cc=cc, n=n: e.memset(vcs[0:n, cc, 64:65], 1.0), [], ['vcs'])
        P.dma('pool', lambda e: e.dma_start(out=kc_d, in_=kcs[:]), 'kco', reads=['kcs'], writes=['kc_d'])
        P.dma('pool', lambda e: e.dma_start(out=vc_d, in_=vcs[:]), 'vco', reads=['vcs'], writes=['vc_d'])
        P.barrier()
        es.close()

    def phaseM0(l):
        es = contextlib.ExitStack()
        mlag = mk(es, "mlag", [128, 3], F32)
        wuq = mk(es, "wuq", [128, 2, 512], BF16)
        wukv = mk(es, "wukv", [128, 512], BF16)
        ones256 = mk(es, "ones256M", [128, 128], BF16)
        ones128 = mk(es, "ones128M", [128, 128], BF16)
        cqin = mk(es, "cqin", [128, 2, 3, 512], F32)
        krin = mk(es, "krin", [32, 2, 2, 512], F32)
        ropet = mk(es, "ropet", [32, 2, 4, 512], F32)
        cqsq = mk(es, "cqsq", [128, 3, 512], BF16)
        rs2 = mk(es, "rs2", [128, 2, 2, 512], F32)
        cqn = mk(es, "cqn", [128, 2, 3, 512], BF16)
        mq = mk(es, "mq", [96, 4, 512], BF16)
        mk_ = mk(es, "mkk", [96, 4, 512], BF16)
        t1 = mk(es, "t1", [32, 3, 2, 512], F32)
        krb = mk(es, "krb", [32, 512], BF16)
        vmst = mk(es, "vmst", [128, 4, 4, 65], BF16)
        P.dma('sp', lambda e: e.dma_start(out=mlag[:], in_=mla_g_d[l]), 'c1', writes=['mlag'])
        P.dma('pool', lambda e: e.dma_start(out=wuq[:], in_=w_uq_d[l].rearrange("(c p) n -> p c n", p=128)), 'wl0', writes=['wuq'])
        P.dma('pool', lambda e: e.dma_start(out=wukv[:], in_=w_ukv_d[l]), 'wl1', writes=['wukv'])
        P.op('dve', lambda e: e.memset(ones256[:], 1.0 / 256), writes=['ones256'])
        P.op('dve', lambda e: e.memset(ones128[:], 1.0 / 128), writes=['ones128'])
        P.op('pool', lambda e: e.memset(vmst[:], 1.0), writes=['vmst'])

        def load(t):
            s = t % 2
            tsl = slice(t * 512, (t + 1) * 512)
            P.dma('sp', lambda e: e.dma_start(out=cqin[:, s], in_=cq_d[:, :, tsl].rearrange("c p s -> p c s")), f'hin{s}',
                  reads=['cq_d'], writes=[f'cqin{s}'])
            P.dma('sp', lambda e: e.dma_start(out=krin[:, s], in_=krr_d[:, tsl].rearrange("(a p) s -> p a s", a=2)), f'yin{s}',
                  reads=['krr_d'], writes=[f'krin{s}'])
            P.dma('sp', lambda e: e.dma_start(out=ropet[:, s], in_=cd['rope'][:, :, tsl].rearrange("a p s -> p a s")),
                  f'rope{s}', writes=[f'rope{s}'])

        def normM(t):
            s = t % 2
            cn = f'cqin{s}'
            P.op('act', lambda e: e.activation(out=cqsq[:].rearrange("p c s -> p (c s)"), in_=cqin[:, s].rearrange("p c s -> p (c s)"),
                                               func=AF.Square), [cn], ['cqsq'])
            for j in range(2):
                mm(PS[7][:], ones256[:], cqsq[:, j, :], j == 0, j == 1, ['ones256', 'cqsq'], [psn[7]])
            mm(PS[0][:], ones128[:], cqsq[:, 2, :], True, True, ['ones128', 'cqsq'], [psn[0]])
            for j, pi in ((0, 7), (1, 0)):
                P.op('act', lambda e, j=j, pi=pi: e.activation(out=rs2[:, s, j, :], in_=PS[pi][:], func=AF.Sqrt, bias=EPS, scale=1.0),
                     [psn[pi]], [f'rs2{s}'])
            P.op('dve', lambda e: e.reciprocal(out=rs2[:, s], in_=rs2[:, s]), [f'rs2{s}'], [f'rs2{s}'])
            for j in range(3):
                P.op('dve', lambda e, j=j: e.scalar_tensor_tensor(out=cqn[:, s, j, :], in0=cqin[:, s, j, :], scalar=mlag[:, j:j + 1],
                                                                   in1=rs2[:, s, 0 if j < 2 else 1, :], op0=ALU.mult, op1=ALU.mult),
                     [cn, 'mlag', f'rs2{s}'], [f'cqn{s}'])

        def tile(t):
            s = t % 2
            tsl = slice(t * 512, (t + 1) * 512)
            if t + 1 < NT:
                load(t + 1)
            if t + 1 < NT:
                normM(t + 1)
            for h in range(4):
                pa = 2 + (h % 2) * 2
                pb = pa + 1
                for c in range(2):
                    mm(PS[pa][0:96, :], wuq[:, c, h * 128:h * 128 + 96], cqn[:, s, c, :], c == 0, c == 1, ['wuq', f'cqn{s}'], [psn[pa]])
                for c in range(2):
                    mm(PS[pb][0:32, :], wuq[:, c, h * 128 + 96:h * 128 + 128], cqn[:, s, c, :], c == 0, c == 1, ['wuq', f'cqn{s}'], [psn[pb]])
                P.op('act', lambda e, h=h, pa=pa: e.activation(out=mq[0:64, h, :], in_=PS[pa][0:64, :], func=AF.Copy, scale=MLA_SCALE),
                     [psn[pa]], ['mq'])
                hh = h % 2
                P.op('dve', lambda e, pa=pa, hh=hh: e.tensor_tensor(out=t1[:, hh, 0, :], in0=PS[pa][64:96, :], in1=ropet[:, s, 0, :], op=ALU.mult),
                     [psn[pa], f'rope{s}'], [f't1a{hh}'])
                P.op('dve', lambda e, pb=pb, hh=hh: e.tensor_tensor(out=t1[:, hh, 1, :], in0=PS[pb][0:32, :], in1=ropet[:, s, 1, :], op=ALU.mult),
                     [psn[pb], f'rope{s}'], [f't1b{hh}'])
                P.op('pool', lambda e, h=h, hh=hh: e.tensor_tensor(out=mq[64:96, h, :], in0=t1[:, hh, 0, :], in1=t1[:, hh, 1, :], op=ALU.add),
                     [f't1a{hh}', f't1b{hh}'], ['mq'])
            P.dma('pool', lambda e: e.dma_start(out=qm_d[:, :, tsl].rearrange("h p s -> p h s"), in_=mq[:]), 'mqo',
                  reads=['mq'], writes=['qm_d'])
            for h in range(4):
                pi = 5 + (h % 2)
                mm(PS[pi][0:64, :], wukv[:, h * 64:(h + 1) * 64], cqn[:, s, 2, :], True, True, ['wukv', f'cqn{s}'], [psn[pi]])
                if h % 2 == 0:
                    P.op('act', lambda e, h=h, pi=pi: e.activation(out=mk_[0:64, h, :], in_=PS[pi][0:64, :], func=AF.Copy),
                         [psn[pi]], ['mk'])
                else:
                    P.op('dve', lambda e, h=h, pi=pi: e.tensor_copy(out=mk_[0:64, h, :], in_=PS[pi][0:64, :]), [psn[pi]], ['mk'])
            P.op('dve', lambda e: e.tensor_tensor(out=t1[:, 2, 0, :], in0=krin[:, s, 0, :], in1=ropet[:, s, 2, :], op=ALU.mult),
                 [f'krin{s}', f'rope{s}'], ['t1a2'])
            P.op('dve', lambda e: e.tensor_tensor(out=t1[:, 2, 1, :], in0=krin[:, s, 1, :], in1=ropet[:, s, 3, :], op=ALU.mult),
                 [f'krin{s}', f'rope{s}'], ['t1b2'])
            P.op('pool', lambda e: e.tensor_tensor(out=krb[:], in0=t1[:, 2, 0, :], in1=t1[:, 2, 1, :], op=ALU.add), ['t1a2', 't1b2'], ['krb'])
            P.dma('pool', lambda e: e.dma_start(out=km_d[:, 0:64, tsl].rearrange("h p s -> p h s"), in_=mk_[0:64]), 'mko',
                  reads=['mk'], writes=['km_d'])
            for h in range(4):
                P.dma('pool', lambda e, h=h: e.dma_start(out=km_d[h, 64:96, tsl], in_=krb[:]), f'krbo{h % 2}', reads=['krb'], writes=['km_d'])
            for sub in range(4):
                pi = 1 + (sub // 2)
                mm(PS[pi][:, (sub % 2) * 256:(sub % 2) * 256 + 256], cqn[:, s, 2, sub * 128:(sub + 1) * 128], wukv[:, 256:512],
                   True, True, ['wukv', f'cqn{s}'], [psn[pi]])
            P.op('dve', lambda e: e.tensor_copy(out=vmst[:, 0:2, :, 0:64], in_=PS[1][:].rearrange("p (a b c) -> p a b c", a=2, b=4)),
                 [psn[1]], ['vmst'])
            P.op('act', lambda e: e.activation(out=vmst[:, 2:4, :, 0:64], in_=PS[2][:].rearrange("p (a b c) -> p a b c", a=2, b=4), func=AF.Copy),
                 [psn[2]], ['vmst'])
            P.dma('pool', lambda e: e.dma_start(out=vm_d[t * 4:(t + 1) * 4].rearrange("k p c -> p k c"),
                                                in_=vmst[:].rearrange("p a b c -> p a (b c)")), 'vmo',
                  reads=['vmst'], writes=['vm_d'])

        load(0)
        normM(0)
        for t in range(NT):
            tile(t)
        P.barrier()
        es.close()

    setup_tables()
    for l in range(depth):
        phaseA(l)
        if stop_after == 'A':
            break
        phaseN0(l)
        phaseM0(l)
        if stop_after == 'M0':
            break
        phaseNSA(l)
        if stop_after == 'NSA':
            break
        phaseMLA(l)
        if stop_after == 'MLA':
            break
        phaseC1(l)
        phaseC2(l, 0)
        phaseC2(l, 1)
        if stop_after == 'L0':
            break
    if stop_after is None:
        phaseF()

    P.barrier()
    P.finish()
    top.close()
    return nc, consts


_CACHE = {}


def _get_program(S):
    if S not in _CACHE:
        _CACHE[S] = build_program(S)
    return _CACHE[S]


def make_in_maps(inputs, S, consts, B):
    w = layout_weights(inputs)
    maps = []
    for b in range(B):
        m = {"x": np.ascontiguousarray(inputs['x'][b].T).reshape(8, 128, S)}
        m.update(w)
        for k, v in consts.items():
            m["c_" + k] = np.ascontiguousarray(v)
        maps.append(m)
    return maps


def kernel(**inputs):
    inputs = {k: np.asarray(v, dtype=np.float32) for k, v in inputs.items()}
    B, S, _ = inputs['x'].shape
    nc, consts = _get_program(S)
    maps = make_in_maps(inputs, S, consts, B)
    res = run_bass_kernel_spmd(nc, maps, core_ids=list(range(B)))
    out = np.stack([np.ascontiguousarray(np.asarray(r["out"]).reshape(1024, S).T) for r in res.results], 0)
    return out.astype(np.float32)
```
